# Optimizing a Trainium2 kernel written in Bass

```python
import math
import jax, jax.numpy as jnp
from jax import lax
import numpy as np

D_MODEL = 1024
BATCH = 8
SEQ = 4096
DEPTH = 1

CHUNK = 64
RET_HEADS = 4
RET_HEAD_DIM = D_MODEL // 8
RET_WIDTH = RET_HEADS * RET_HEAD_DIM
POOL_WINDOWS = (2, 4, 8, 16)
POOL_GROUPS = len(POOL_WINDOWS)
POOL_GROUP_DIM = D_MODEL // 8
POOL_WIDTH = POOL_GROUPS * POOL_GROUP_DIM
MIX_WIDTH = RET_WIDTH + POOL_WIDTH
IN_WIDTH = 4 * RET_WIDTH + POOL_WIDTH
D_FF = ((8 * D_MODEL // 3 + 127) // 128) * 128
CONV_WIDTH = 3
ROPE_BASE = 10000.0
LN_EPS = 1e-5
RMS_EPS = 1e-6
DEEPNORM_ALPHA = (2.0 * DEPTH) ** 0.25
DEEPNORM_BETA = (8.0 * DEPTH) ** -0.25

kernel_name = "hybrid_retention_pool_convffn_deepnorm"


def _layernorm(x, g, b):
    xf = x.astype(jnp.float32)
    mu = jnp.mean(xf, axis=-1, keepdims=True)
    var = jnp.mean(jnp.square(xf - mu), axis=-1, keepdims=True)
    y = (xf - mu) * lax.rsqrt(var + LN_EPS) * g.astype(jnp.float32) + b.astype(jnp.float32)
    return y.astype(x.dtype)


def _rope(t):
    s, dh = t.shape[1], t.shape[-1]
    inv_freq = ROPE_BASE ** (-jnp.arange(0, dh, 2, dtype=jnp.float32) / dh)
    ang = jnp.arange(s, dtype=jnp.float32)[:, None] * inv_freq[None, :]
    cos = jnp.cos(ang)[None, :, None, :]
    sin = jnp.sin(ang)[None, :, None, :]
    tf = t.astype(jnp.float32)
    t1, t2 = tf[..., : dh // 2], tf[..., dh // 2:]
    return jnp.concatenate([t1 * cos - t2 * sin, t1 * sin + t2 * cos], axis=-1)


def _retention(q, k, v):
    b, s, h, dh = q.shape
    nc = s // CHUNK
    log_gamma = jnp.log(1.0 - 2.0 ** (-5.0 - jnp.arange(h, dtype=jnp.float32)))
    idx = jnp.arange(CHUNK, dtype=jnp.float32)
    inner_decay = jnp.exp(log_gamma[:, None, None] * jnp.abs(idx[:, None] - idx[None, :]))
    q_decay = jnp.exp(log_gamma[None, :] * (idx[:, None] + 1.0))
    k_decay = jnp.exp(log_gamma[None, :] * (CHUNK - 1.0 - idx[:, None]))
    chunk_decay = jnp.exp(log_gamma * CHUNK)

    qc = q.reshape(b, nc, CHUNK, h, dh)
    kc = k.reshape(b, nc, CHUNK, h, dh)
    vc = v.reshape(b, nc, CHUNK, h, dh)

    scores = jnp.einsum('bnihd,bnjhd->bnhij', qc, kc) * inner_decay[None, None]
    inner = jnp.einsum('bnhij,bnjhe->bnihe', scores, vc)

    def step(state, inp):
        q_n, k_n, v_n = inp
        cross = jnp.einsum('bihd,bhde->bihe', q_n * q_decay[None, :, :, None], state)
        new_state = state * chunk_decay[None, :, None, None] + jnp.einsum(
            'bjhd,bjhe->bhde', k_n * k_decay[None, :, :, None], v_n)
        return new_state, cross

    state0 = jnp.zeros((b, h, dh, dh), jnp.float32)
    xs = (jnp.moveaxis(qc, 1, 0), jnp.moveaxis(kc, 1, 0), jnp.moveaxis(vc, 1, 0))
    _, cross = lax.scan(step, state0, xs)
    out = inner + jnp.moveaxis(cross, 0, 1)
    return out.reshape(b, s, h, dh)


def _pool_mixer(p, w_pool, pool_scale):
    b, s, _ = p.shape
    pf = p.astype(jnp.float32).reshape(b, s, POOL_GROUPS, POOL_GROUP_DIM)
    cs = jnp.cumsum(pf, axis=1)
    pos = jnp.arange(1, s + 1, dtype=jnp.float32)
    outs = []
    for gi, w in enumerate(POOL_WINDOWS):
        c = cs[:, :, gi]
        prev = jnp.pad(c, ((0, 0), (w, 0), (0, 0)))[:, :s]
        mean = (c - prev) / jnp.minimum(pos, float(w))[None, :, None]
        outs.append(mean - pf[:, :, gi])
    pooled = jnp.stack(outs, axis=2).astype(p.dtype)
    y = jnp.einsum('bsgc,gcd->bsgd', pooled, w_pool).reshape(b, s, POOL_WIDTH)
    return y * pool_scale


def _conv_ffn(x, w_up, conv_w, conv_b, w_down):
    s = x.shape[1]
    u = x @ w_up
    val, gate = u[..., :D_FF], u[..., D_FF:]
    gp = jnp.pad(gate, ((0, 0), (CONV_WIDTH - 1, 0), (0, 0)))
    h = conv_b + sum(gp[:, j:j + s] * conv_w[j] for j in range(CONV_WIDTH))
    return (jax.nn.silu(h) * val) @ w_down


def setup_inputs(seed: int = 0) -> dict:
    key = jax.random.key(seed)
    ks = jax.random.split(key, 20)
    f32 = jnp.float32
    L = DEPTH
    x = jax.random.normal(ks[0], (BATCH, SEQ, D_MODEL), f32)
    sd = D_MODEL ** -0.5
    w_qk = jax.random.normal(ks[1], (L, D_MODEL, 2 * RET_WIDTH), f32) * sd
    w_v = jax.random.normal(ks[2], (L, D_MODEL, RET_WIDTH), f32) * sd * DEEPNORM_BETA
    w_g = jax.random.normal(ks[3], (L, D_MODEL, RET_WIDTH), f32) * sd
    w_p = jax.random.normal(ks[4], (L, D_MODEL, POOL_WIDTH), f32) * sd * DEEPNORM_BETA
    w_in = jnp.concatenate([w_qk, w_v, w_g, w_p], axis=-1)
    w_pool = jax.random.normal(ks[5], (L, POOL_GROUPS, POOL_GROUP_DIM, POOL_GROUP_DIM), f32) * POOL_GROUP_DIM ** -0.5
    pool_scale = 1.0 + 0.1 * jax.random.normal(ks[6], (L, POOL_WIDTH), f32)
    w_out = jax.random.normal(ks[7], (L, MIX_WIDTH, D_MODEL), f32) * MIX_WIDTH ** -0.5 * DEEPNORM_BETA
    ln1_g = 1.0 + 0.05 * jax.random.normal(ks[8], (L, D_MODEL), f32)
    ln1_b = 0.02 * jax.random.normal(ks[9], (L, D_MODEL), f32)
    w_up = jax.random.normal(ks[10], (L, D_MODEL, 2 * D_FF), f32) * sd * DEEPNORM_BETA
    conv_w = jax.random.normal(ks[11], (L, CONV_WIDTH, D_FF), f32) * CONV_WIDTH ** -0.5
    conv_b = 0.02 * jax.random.normal(ks[12], (L, D_FF), f32)
    w_down = jax.random.normal(ks[13], (L, D_FF, D_MODEL), f32) * D_FF ** -0.5 * DEEPNORM_BETA
    ln2_g = 1.0 + 0.05 * jax.random.normal(ks[14], (L, D_MODEL), f32)
    ln2_b = 0.02 * jax.random.normal(ks[15], (L, D_MODEL), f32)
    return {"x": x, "w_in": w_in, "w_pool": w_pool, "pool_scale": pool_scale, "w_out": w_out,
            "ln1_g": ln1_g, "ln1_b": ln1_b, "w_up": w_up, "conv_w": conv_w, "conv_b": conv_b,
            "w_down": w_down, "ln2_g": ln2_g, "ln2_b": ln2_b}


def reference(x, w_in, w_pool, pool_scale, w_out, ln1_g, ln1_b, w_up, conv_w, conv_b,
              w_down, ln2_g, ln2_b):
    b, s, _ = x.shape
    for l in range(DEPTH):
        proj = x @ w_in[l]
        q, k, v, g, p = jnp.split(proj, [RET_WIDTH, 2 * RET_WIDTH, 3 * RET_WIDTH, 4 * RET_WIDTH], axis=-1)
        q = _rope(q.reshape(b, s, RET_HEADS, RET_HEAD_DIM))
        k = _rope(k.reshape(b, s, RET_HEADS, RET_HEAD_DIM)) * (RET_HEAD_DIM ** -0.5)
        v = v.reshape(b, s, RET_HEADS, RET_HEAD_DIM).astype(jnp.float32)
        ret = _retention(q, k, v)
        ret = ret * lax.rsqrt(jnp.mean(jnp.square(ret), axis=-1, keepdims=True) + RMS_EPS)
        ret = ret.reshape(b, s, RET_WIDTH).astype(x.dtype) * jax.nn.silu(g)
        pool = _pool_mixer(p, w_pool[l], pool_scale[l])
        mix = jnp.concatenate([ret, pool], axis=-1) @ w_out[l]
        x = _layernorm(DEEPNORM_ALPHA * x + mix, ln1_g[l], ln1_b[l])
        ffn = _conv_ffn(x, w_up[l], conv_w[l], conv_b[l], w_down[l])
        x = _layernorm(DEEPNORM_ALPHA * x + ffn, ln2_g[l], ln2_b[l])
    return x
```

```python
import contextlib
import numpy as np
import ml_dtypes
import concourse.bass as bass
import concourse.mybir as mybir
from concourse.bass_utils import run_bass_kernel_spmd

F32 = mybir.dt.float32
BF16 = mybir.dt.bfloat16
ALU = mybir.AluOpType
AF = mybir.ActivationFunctionType

D = 1024
KD = 8
TT = 512
S = 4
DFF = 2816
NF = 22
NSLOT = 24
RING = 3
PROJ_ORDER = (2, 3, 4, 0, 1)
ALPHA = 2.0 ** 0.25
LN_EPS = 1e-5
RMS_EPS = 1e-6
SELF_SYNC = True


class Buf:
    def __init__(self, name, excl=False):
        self.name = name
        self.w = []
        self.r = []
        self.excl = excl


class Prog:
    ENG = ("pe", "act", "dve", "pool", "sp")

    def __init__(self, nc, es):
        self.nc = nc
        self.es = es
        self.streams = {e: [] for e in self.ENG}
        self.cnt = {e: 0 for e in self.ENG}
        self.seen = {e: {} for e in self.ENG}
        self.sems = {}
        self.dcnt = {}
        for e in self.ENG:
            self.sems[e] = es.enter_context(nc.semaphore("s_" + e))

    def dsem(self, key):
        if key not in self.sems:
            self.sems[key] = self.es.enter_context(self.nc.semaphore("d_" + key))
            self.dcnt[key] = 0
        return self.sems[key]

    def _deps(self, eng, reads, writes):
        toks = []
        for b in reads:
            toks += b.w
            if b.excl:
                toks += b.r
        for b in writes:
            toks += b.w
            toks += b.r
        need = {}
        for (k, v) in toks:
            if k == eng and (not SELF_SYNC or eng == "pe"):
                continue
            if self.seen[eng].get(k, 0) >= v:
                continue
            if need.get(k, 0) < v:
                need[k] = v
        for k, v in need.items():
            self.seen[eng][k] = v
            self.streams[eng].append(("w", k, v))

    def _commit(self, tok, reads, writes):
        for b in reads:
            if b not in writes:
                b.r.append(tok)
                if len(b.r) > 64:
                    best = {}
                    for (k, v) in b.r:
                        if best.get(k, 0) < v:
                            best[k] = v
                    b.r = list(best.items())
        for b in writes:
            b.w = [tok]
            b.r = []

    def op(self, eng, fn, reads=(), writes=()):
        reads = list(reads)
        writes = list(writes)
        self._deps(eng, reads, writes)
        self.streams[eng].append(("op", fn))
        self.cnt[eng] += 1
        tok = (eng, self.cnt[eng])
        self._commit(tok, reads, writes)
        return tok

    def dma(self, eng, out, in_, key, reads=(), writes=()):
        reads = list(reads)
        writes = list(writes)
        self.dsem(key)
        self._deps(eng, reads, writes)
        self.streams[eng].append(("dma", out, in_, key))
        self.dcnt[key] += 16
        tok = (key, self.dcnt[key])
        self._commit(tok, reads, writes)
        return tok

    def wait_all(self, eng, toks):
        need = {}
        for (k, v) in toks:
            if need.get(k, 0) < v:
                need[k] = v
        for k, v in need.items():
            self.streams[eng].append(("w", k, v))

    def emit(self):
        nc = self.nc

        def mk(engname):
            def body(e):
                for ent in self.streams[engname]:
                    if ent[0] == "w":
                        e.wait_ge(self.sems[ent[1]], ent[2])
                    elif ent[0] == "op":
                        ins = ent[1](e)
                        ins.then_inc(self.sems[engname], 1)
                    else:
                        e.dma_start(out=ent[1], in_=ent[2]).then_inc(self.sems[ent[3]], 16)
            return body

        with nc.Block() as block:
            block.tensor(mk("pe"))
            block.scalar(mk("act"))
            block.vector(mk("dve"))
            block.gpsimd(mk("pool"))
            block.sync(mk("sp"))


def host_consts():
    dh = 128
    B = 128
    inv_freq = (10000.0 ** (-np.arange(0, dh, 2, dtype=np.float32) / np.float32(dh))).astype(np.float32)
    ang = (np.arange(4096, dtype=np.float32)[:, None] * inv_freq[None, :]).astype(np.float32)
    rope = np.concatenate([np.cos(ang), np.sin(ang)], axis=1).astype(np.float32)
    gam = 1.0 - 2.0 ** (-5.0 - np.arange(4, dtype=np.float64))
    i = np.arange(B, dtype=np.float64)
    qdec = np.zeros((128, 512), np.float32)
    vdec = np.zeros((128, 4), np.float32)
    mask = np.zeros((128, 512), np.float32)
    cds = []
    for h in range(4):
        g = gam[h]
        qd = g ** (i + 1.0) * dh ** -0.5
        qdec[:, h * 128:(h + 1) * 128] = qd[None, :]
        vd = g ** (B - 1.0 - i)
        vdec[:, h] = vd
        jj = i[:, None]
        ii = i[None, :]
        W = np.where((jj // 64) <= (ii // 64), g ** np.abs(ii - jj), 0.0)
        M = W / (g ** (ii + 1.0) * g ** (B - 1.0 - jj))
        mask[:, h * 128:(h + 1) * 128] = M
        cds.append(float(g ** B))
    bands = np.zeros((128, 12, 128), np.float32)
    m = np.arange(128)[:, None]
    t = np.arange(128)[None, :]
    for gi, w in enumerate((2, 4, 8, 16)):
        cur = np.where((m <= t) & (m > t - w), 1.0 / w, 0.0) - (m == t)
        prev = np.where((m - 128 <= t) & (m - 128 > t - w), 1.0 / w, 0.0)
        div = np.minimum(t + 1.0, float(w))
        cur0 = np.where((m <= t) & (m > t - w), 1.0 / div, 0.0) - (m == t)
        bands[:, gi, :] = prev
        bands[:, 4 + gi, :] = cur
        bands[:, 8 + gi, :] = cur0
    ident = np.eye(128, dtype=np.float32)
    return dict(rope=rope, qdec=qdec, vdec=vdec, mask=mask,
                bands=bands.astype(ml_dtypes.bfloat16), ident=ident.astype(ml_dtypes.bfloat16)), cds


def build(NT, debug=(), stop=None):
    consts, cds = host_consts()
    T = NT * TT
    nc = bass.Bass("TRN2", target_bir_lowering=False)
    dt_in = lambda name, shape, dt=F32: nc.dram_tensor(name, list(shape), dt, kind="ExternalInput").ap()
    x_d = dt_in("x", [T, D])
    w_in_d = dt_in("w_in", [D, 2560])
    w_pool_d = dt_in("w_pool", [4, 128, 128])
    w_out_d = dt_in("w_out", [D, D])
    w_up_d = dt_in("w_up", [D, 2 * DFF])
    w_down_d = dt_in("w_down", [DFF, D])
    pscale_d = dt_in("pscale", [128, 4])
    g1_d = dt_in("g1t", [128, D])
    b1_d = dt_in("b1t", [128, D])
    g2_d = dt_in("g2t", [128, D])
    b2_d = dt_in("b2t", [128, D])
    cw_d = dt_in("cw", [128, NF * 3])
    cb_d = dt_in("cb", [128, NF])
    rope_d = dt_in("rope", [4096, 128])
    qdec_d = dt_in("qdec", [128, 512])
    vdec_d = dt_in("vdec", [128, 4])
    mask_d = dt_in("mask", [128, 512])
    bands_d = dt_in("bands", [128, 12 * 128], BF16)
    g1p_d = dt_in("g1p", [128, KD])
    b1p_d = dt_in("b1p", [128, KD])
    ident_d = dt_in("ident", [128, 128], BF16)
    out_d = nc.dram_tensor("out", [T, D], F32, kind="ExternalOutput").ap()
    scr_d = nc.dram_tensor("wscr", [NSLOT, 128, 4096], BF16, kind="Internal").ap()
    dbg_d = {}

    es = contextlib.ExitStack()
    with es:
        P = Prog(nc, es)

        def sb(name, shape, dt=F32):
            return es.enter_context(nc.sbuf_tensor("sb_" + name, list(shape), dt))

        def ps(name, shape, dt=F32):
            return es.enter_context(nc.psum_tensor("ps_" + name, list(shape), dt))

        ring = [sb("ring%d" % i, [128, 4096], BF16) for i in range(RING)]
        ringB = [Buf("ring%d" % i) for i in range(RING)]
        scrB = [Buf("scr%d" % i) for i in range(NSLOT)]
        ident = sb("ident", [128, 128], BF16)
        qdec = sb("qdec", [128, 512])
        vdec = sb("vdec", [128, 4])
        mask = sb("mask", [128, 512])
        bands = sb("bands", [128, 12 * 128], BF16)
        wpool = sb("wpool", [128, 512], BF16)
        pscale = sb("pscale", [128, 4])
        g1t = sb("g1t", [128, D])
        b1t = sb("b1t", [128, D])
        g2t = sb("g2t", [128, D])
        b2t = sb("b2t", [128, D])
        cw = sb("cw", [128, NF * 3])
        cb = sb("cb", [128, NF])
        g1p = sb("g1p", [128, KD])
        b1p = sb("b1p", [128, KD])
        identB = Buf("identc")
        constB = Buf("const")
        cs = [sb("cs%d" % i, [128, S * 128]) for i in range(2)]
        csB = [Buf("cs%d" % i) for i in range(2)]
        X = sb("X", [128, S * D])
        XB = [Buf("X%d" % s) for s in range(S)]
        xbm = sb("xbm", [128, 4096], BF16)
        xbmB = Buf("xbm")
        XT = sb("XT", [128, KD * TT], BF16)
        XTB = [Buf("XT%d" % s) for s in range(S)]
        XTB2 = [Buf("XTh%d" % s) for s in range(S)]
        XA = sb("XA", [128, KD * TT], BF16)
        XAB = [Buf("XA%d" % s) for s in range(S)]
        qkb = sb("qkb", [128, 2 * S * 512], BF16)
        qkbB = [[Buf("qkb%d_%d" % (j, s)) for s in range(S)] for j in range(2)]
        qT = sb("qT", [128, 4 * TT], BF16)
        kT = sb("kT", [128, 4 * TT], BF16)
        qTB = [Buf("qT%d" % s) for s in range(S)]
        kTB = [Buf("kT%d" % s) for s in range(S)]
        vd = sb("vd", [128, S * 512], BF16)
        vdB = [Buf("vd%d" % s) for s in range(S)]
        sg = sb("sg", [128, S * 512])
        sgB = [Buf("sg%d" % s) for s in range(S)]
        pb = sb("pb", [128, 5 * 512], BF16)
        pbB = [Buf("pb%d" % i) for i in range(5)]
        ropeA = [sb("ropeA%d" % i, [128, 512]) for i in range(2)]
        ropeBt = [sb("ropeB%d" % i, [128, 512]) for i in range(2)]
        ropeAB = [Buf("ropeA%d" % i) for i in range(2)]
        ropeBB = [Buf("ropeB%d" % i) for i in range(2)]
        ST = [sb("ST%d" % i, [128, 512], BF16) for i in range(4)]
        STB = [Buf("ST%d" % i) for i in range(4)]
        state2 = [sb("state%d" % i, [128, 512]) for i in range(2)]
        state2B = [Buf("state%d" % i) for i in range(2)]
        stateb = [sb("stateb%d" % i, [128, 512], BF16) for i in range(5)]
        statebB = [Buf("stateb%d" % i) for i in range(5)]
        ss = [sb("ss%d" % i, [128, 4]) for i in range(4)]
        rstd = [sb("rstd%d" % i, [128, 4]) for i in range(4)]
        ssB = [Buf("ss%d" % i) for i in range(4)]
        rstdB = [Buf("rstd%d" % i) for i in range(4)]
        retn = [sb("retn%d" % i, [128, 512], BF16) for i in range(4)]
        retnB = [Buf("retn%d" % i) for i in range(4)]
        pooledT = sb("pooledT", [128, 4 * TT], BF16)
        pooledB = [Buf("pooled%d" % s) for s in range(S)]
        Y = [sb("Y%d" % i, [128, D]) for i in range(4)]
        YB = [Buf("Y%d" % i) for i in range(4)]
        lnst = [sb("lnst%d" % i, [128, 12]) for i in range(4)]
        lnmv = [sb("lnmv%d" % i, [128, 4]) for i in range(4)]
        lnB = [Buf("ln%d" % i) for i in range(4)]
        G = [sb("G%d" % i, [128, 514]) for i in range(3)]
        Hh = [sb("H%d" % i, [128, 512]) for i in range(3)]
        GB = [Buf("G%d" % i) for i in range(3)]
        HB = [Buf("H%d" % i) for i in range(3)]
        carry = sb("carry", [128, NF * 2])
        carryB = [Buf("carry%d" % c) for c in range(NF)]
        aT = sb("aT", [128, NF * TT], BF16)
        aTB = [Buf("aT%d" % c) for c in range(NF)]
        Q = [ps("Q%d" % i, [128, 1024]) for i in range(4)]
        bankB = [Buf("bank%d" % i, excl=True) for i in range(8)]

        def bank(i):
            return Q[i // 2][:, (i % 2) * 512:(i % 2) * 512 + 512]

        def bank_bf(i):
            return bank(i).bitcast(BF16)

        P.dma("sp", ident[:, :], ident_d, "identc", writes=[identB])
        ctoks = []
        for (dst, src) in ((qdec, qdec_d), (vdec, vdec_d), (mask, mask_d), (bands, bands_d),
                           (pscale, pscale_d), (g1t, g1_d), (b1t, b1_d), (g2t, g2_d), (b2t, b2_d),
                           (cw, cw_d), (cb, cb_d), (g1p, g1p_d), (b1p, b1p_d)):
            ctoks.append(P.dma("sp", dst[:, :], src, "const", writes=[Buf("c_" + dst.name)]))
        mhalf = sb("mhalf", [128, 4])
        P.op("pool", lambda e: e.memset(mhalf[:, :], -0.5), writes=[Buf("mhalf")])
        P.op("pool", lambda e: e.memset(carry[:, :], 0.0), writes=carryB)
        P.op("pool", lambda e: e.memset(state2[1][:, :], 0.0), writes=[state2B[1]])

        def slot_src(sid):
            if sid < 5:
                j = PROJ_ORDER[sid]
                return [(lambda r: r[:, :].rearrange("p (k n) -> p k n", k=KD),
                         w_in_d[:, j * 512:(j + 1) * 512].rearrange("(k p) n -> p k n", p=128))]
            if sid < 7:
                h = sid - 5
                return [(lambda r: r[:, :].rearrange("p (k n) -> p k n", k=KD),
                         w_out_d[:, h * 512:(h + 1) * 512].rearrange("(k p) n -> p k n", p=128))]
            if sid < 18:
                u = sid - 7
                return [
                    (lambda r: r[:, :].rearrange("p (k n) -> p k n", k=KD)[:, :, 0:256],
                     w_up_d[:, u * 256:(u + 1) * 256].rearrange("(k p) n -> p k n", p=128)),
                    (lambda r: r[:, :].rearrange("p (k n) -> p k n", k=KD)[:, :, 256:512],
                     w_up_d[:, DFF + u * 256:DFF + (u + 1) * 256].rearrange("(k p) n -> p k n", p=128)),
                ]
            v = sid - 18
            nch = 4 if v < 5 else 2
            return [(lambda r: r[:, 0:nch * 1024].rearrange("p (c n) -> p c n", c=nch),
                     w_down_d[v * 512:v * 512 + nch * 128, :].rearrange("(c p) n -> p c n", p=128))]

        nseq = NT * NSLOT
        issued = [0]

        def issue_slot_load():
            g = issued[0]
            if g >= nseq:
                return
            issued[0] += 1
            t, sid = divmod(g, NSLOT)
            ri = g % RING
            if t == 0:
                for (dv, src) in slot_src(sid):
                    P.dma("pool", dv(ring[ri]), src, "ringp%d" % ri, writes=[ringB[ri]])
                ncol = 2048 if sid == NSLOT - 1 else 4096
                if NT > 1:
                    P.dma("sp", scr_d[sid][:, 0:ncol], ring[ri][:, 0:ncol], "ringst%d" % ri, reads=[ringB[ri]],
                          writes=[scrB[sid]])
            else:
                ncol = 2048 if sid == NSLOT - 1 else 4096
                P.dma("sp", ring[ri][:, 0:ncol], scr_d[sid][:, 0:ncol], "ring%d" % ri, reads=[scrB[sid]],
                      writes=[ringB[ri]])

        def slot(t, sid):
            g = t * NSLOT + sid
            return ring[g % RING], ringB[g % RING]

        def xload_bf(t):
            P.dma("pool", xbm[:, :].rearrange("p (s d) -> p s d", s=S),
                  x_d[t * TT:(t + 1) * TT, :].rearrange("(s p) d -> p s d", p=128), "xbm", writes=[xbmB])

        def xload_f32(t):
            for s in range(S):
                P.dma("sp", X[:, s * D:(s + 1) * D], x_d[t * TT + s * 128:t * TT + (s + 1) * 128, :],
                      "X%d" % s, writes=[XB[s]])

        def csload(t):
            P.dma("sp", cs[t % 2][:, :].rearrange("p (s c) -> p s c", s=S),
                  rope_d[t * TT:(t + 1) * TT, :].rearrange("(s p) c -> p s c", p=128), "cs%d" % (t % 2),
                  writes=[csB[t % 2]])

        def dbg(name, ap_fn, shape, reads, dt=F32):
            if name not in debug:
                return
            if name not in dbg_d:
                dbg_d[name] = nc.dram_tensor("dbg_" + name, list(shape), dt, kind="ExternalOutput").ap()
            tok = P.dma("sp", dbg_d[name], ap_fn(), "dbg_" + name, reads=reads)
            out_toks.append(tok)

        def ln_stats(yi, li):
            def lnstats(e):
                e.bn_stats(out=lnst[li][:, 0:6], in_=Y[yi][:, 0:512])
                return e.bn_stats(out=lnst[li][:, 6:12], in_=Y[yi][:, 512:1024])
            P.op("dve", lnstats, reads=[YB[yi]], writes=[lnB[li]])
            P.op("dve", lambda e: e.bn_aggr(out=lnmv[li][:, 0:2], in_=lnst[li][:, :]), reads=[], writes=[lnB[li]])
            P.op("dve", lambda e: e.tensor_scalar(out=lnmv[li][:, 2:3], in0=lnmv[li][:, 1:2], scalar1=LN_EPS,
                                                  scalar2=1.0, op0=ALU.add, op1=ALU.mult),
                 reads=[], writes=[lnB[li]])

        def ln_rstd(li):
            P.op("pool", lambda e: e.tensor_tensor(out=lnmv[li][:, 2:3], in0=lnmv[li][:, 2:3], in1=mhalf[:, 0:1],
                                                   op=ALU.pow), reads=[], writes=[lnB[li]])
            P.op("dve", lambda e: e.scalar_tensor_tensor(out=lnmv[li][:, 3:4], in0=lnmv[li][:, 0:1], scalar=-1.0,
                                                         in1=lnmv[li][:, 2:3], op0=ALU.mult, op1=ALU.mult),
                 reads=[], writes=[lnB[li]])

        def ln_fin(yi, li):
            ln_rstd(li)
            P.op("act", lambda e: e.activation(out=Y[yi][:, :], in_=Y[yi][:, :], func=AF.Identity,
                                               bias=lnmv[li][:, 3:4], scale=lnmv[li][:, 2:3]),
                 reads=[lnB[li]], writes=[YB[yi]])

        out_toks = []
        xload_bf(0)
        ctoks.append(P.dma("pool", wpool[:, :].rearrange("c (g d) -> c g d", g=4),
                           w_pool_d.rearrange("g c d -> c g d"), "constp", writes=[Buf("c_wpool")]))
        constB.w = [ctoks[-2], ctoks[-1]]
        for _ in range(RING):
            issue_slot_load()
        xload_f32(0)
        csload(0)
        rr = dict(pe=0, ffn=0, y=0, ln=0, u=0)
        XT3 = XT[:, :].rearrange("p (k n) -> p k n", k=KD)
        mixT3 = xbm[:, :].rearrange("p (k n) -> p k n", k=KD)
        xb3 = xbm[:, :].rearrange("p (s d) -> p s d", s=S)
        qT3 = qT[:, :].rearrange("p (h n) -> p h n", h=4)
        kT3 = kT[:, :].rearrange("p (h n) -> p h n", h=4)
        pT3 = pooledT[:, :].rearrange("p (g n) -> p g n", g=4)
        cw3 = cw[:, :].rearrange("p (c j) -> p c j", j=3)
        wtiles = [0, 2, 3]

        XA3 = XA[:, :].rearrange("p (k n) -> p k n", k=KD)

        def phaseX_group(t, s):
            bi = 2 + (s % 2)

            def f(e):
                ins = None
                for k in range(KD):
                    ins = e.transpose(bank_bf(bi)[:, k * 128:(k + 1) * 128], xb3[:, s, k * 128:(k + 1) * 128],
                                      ident[:, :])
                return ins
            P.op("pe", f, reads=[xbmB, identB], writes=[bankB[bi]])
            if s % 2 == 0:
                P.op("act", lambda e: e.activation(out=XA3[:, :, s * 128:(s + 1) * 128],
                                                   in_=bank_bf(bi).rearrange("p (k n) -> p k n", k=KD),
                                                   func=AF.Copy),
                     reads=[bankB[bi]], writes=[XAB[s]])
            else:
                P.op("dve", lambda e: e.tensor_copy(out=XA3[:, :, s * 128:(s + 1) * 128],
                                                    in_=bank_bf(bi).rearrange("p (k n) -> p k n", k=KD)),
                     reads=[bankB[bi]], writes=[XAB[s]])

        def phaseX(t):
            for s in range(S):
                phaseX_group(t, s)

        def proj_group(t, sid, s):
            j = PROJ_ORDER[sid]
            rt, rB = slot(t, sid)
            r3 = rt[:, :].rearrange("p (k n) -> p k n", k=KD)
            bi = rr["pe"] % 2
            rr["pe"] += 1
            cst = cs[t % 2]
            cstB = csB[t % 2]

            def f(e):
                ins = None
                for k in range(KD):
                    ins = e.matmul(bank(bi), XA3[:, k, s * 128:(s + 1) * 128], r3[:, k, :],
                                   start=(k == 0), stop=(k == KD - 1))
                return ins
            P.op("pe", f, reads=[XAB[s], rB], writes=[bankB[bi]])
            bk = bank(bi)
            if j < 2:
                ri = (j * S + s) % 2
                A, Bt = ropeA[ri], ropeBt[ri]
                bk4 = bk.rearrange("p (h two x) -> p h two x", h=4, two=2)
                A4 = A[:, :].rearrange("p (h two x) -> p h two x", h=4, two=2)
                B4 = Bt[:, :].rearrange("p (h two x) -> p h two x", h=4, two=2)
                cosv = cst[:, s * 128:s * 128 + 64].unsqueeze(1).unsqueeze(1).to_broadcast([128, 4, 2, 64])
                sinv = cst[:, s * 128 + 64:s * 128 + 128].unsqueeze(1).to_broadcast([128, 4, 64])
                P.op("dve", lambda e: e.tensor_tensor(out=A4, in0=bk4, in1=cosv, op=ALU.mult),
                     reads=[bankB[bi], cstB], writes=[ropeAB[ri]])

                def fb(e):
                    e.scalar_tensor_tensor(out=B4[:, :, 0, :], in0=bk4[:, :, 1, :], scalar=-1.0, in1=sinv,
                                           op0=ALU.mult, op1=ALU.mult)
                    return e.tensor_tensor(out=B4[:, :, 1, :], in0=bk4[:, :, 0, :], in1=sinv, op=ALU.mult)
                P.op("dve", fb, reads=[bankB[bi], cstB], writes=[ropeBB[ri]])
                dst = qkb[:, (j * S + s) * 512:(j * S + s + 1) * 512]
                P.op("pool", lambda e: e.tensor_tensor(out=dst, in0=A[:, :], in1=Bt[:, :], op=ALU.add),
                     reads=[ropeAB[ri], ropeBB[ri]], writes=[qkbB[j][s]])
            elif j == 2:
                def fv(e):
                    ins = None
                    for h in range(4):
                        ins = e.activation(out=vd[:, s * 512 + h * 128:s * 512 + (h + 1) * 128],
                                           in_=bk[:, h * 128:(h + 1) * 128], func=AF.Identity,
                                           scale=vdec[:, h:h + 1])
                    return ins
                P.op("act", fv, reads=[bankB[bi], constB], writes=[vdB[s]])
            elif j == 3:
                P.op("act", lambda e: e.activation(out=sg[:, s * 512:(s + 1) * 512], in_=bk, func=AF.Silu),
                     reads=[bankB[bi]], writes=[sgB[s]])
            else:
                pi = (t * S + s) % 5
                P.op("act", lambda e: e.activation(out=pb[:, pi * 512:(pi + 1) * 512], in_=bk, func=AF.Copy),
                     reads=[bankB[bi]], writes=[pbB[pi]])

        def phaseO(t):
            for s in range(S):
                gs = t * S + s
                pi, pp = gs % 5, (gs - 1) % 5
                bi = 4 + (s % 2)

                def f(e, gs=gs, pi=pi, pp=pp, bi=bi):
                    ins = None
                    for g in range(4):
                        o = bank(bi)[:, g * 128:(g + 1) * 128]
                        if gs == 0:
                            ins = e.matmul(o, pb[:, pi * 512 + g * 128:pi * 512 + (g + 1) * 128],
                                           bands[:, (8 + g) * 128:(9 + g) * 128], start=True, stop=True)
                        else:
                            e.matmul(o, pb[:, pi * 512 + g * 128:pi * 512 + (g + 1) * 128],
                                     bands[:, (4 + g) * 128:(5 + g) * 128], start=True, stop=False)
                            ins = e.matmul(o, pb[:, pp * 512 + g * 128:pp * 512 + (g + 1) * 128],
                                           bands[:, g * 128:(g + 1) * 128], start=False, stop=True)
                    return ins
                P.op("pe", f, reads=[pbB[pi], pbB[pp], constB], writes=[bankB[bi]])
                P.op("act", lambda e, s=s, bi=bi: e.activation(out=pT3[:, :, s * 128:(s + 1) * 128],
                                                               in_=bank(bi).rearrange("p (g n) -> p g n", g=4),
                                                               func=AF.Copy),
                     reads=[bankB[bi]], writes=[pooledB[s]])
            for g in range(4):
                bi = 6 + (g % 2)
                P.op("pe", lambda e, g=g, bi=bi: e.matmul(bank(bi), wpool[:, g * 128:(g + 1) * 128], pT3[:, g, :],
                                                          start=True, stop=True),
                     reads=pooledB + [constB], writes=[bankB[bi]])
                P.op("act", lambda e, g=g, bi=bi: e.activation(out=mixT3[:, 4 + g, :], in_=bank(bi),
                                                               func=AF.Identity, scale=pscale[:, g:g + 1]),
                     reads=[bankB[bi], constB], writes=[xbmB])

        def phaseT(t):
            qbk = [2, 3, 4, 5]
            kbk = [6, 7, 0, 1]

            def tr(j, s, bi):
                def f(e):
                    ins = None
                    for h in range(4):
                        ins = e.transpose(bank_bf(bi)[:, h * 128:(h + 1) * 128],
                                          qkb[:, (j * S + s) * 512 + h * 128:(j * S + s) * 512 + (h + 1) * 128],
                                          ident[:, :])
                    return ins
                P.op("pe", f, reads=[qkbB[j][s], identB], writes=[bankB[bi]])
            for s in range(S):
                bi = qbk[s]
                tr(0, s, bi)
                P.op("dve", lambda e, s=s, bi=bi: e.tensor_tensor(
                    out=qT3[:, :, s * 128:(s + 1) * 128],
                    in0=bank_bf(bi)[:, 0:512].rearrange("p (h n) -> p h n", h=4),
                    in1=qdec[:, :].rearrange("p (h n) -> p h n", h=4), op=ALU.mult),
                    reads=[bankB[bi], constB], writes=[qTB[s]])
            for s in range(S):
                bi = kbk[s]
                tr(1, s, bi)
                P.op("act", lambda e, s=s, bi=bi: e.activation(
                    out=kT3[:, :, s * 128:(s + 1) * 128],
                    in_=bank_bf(bi)[:, 0:512].rearrange("p (h n) -> p h n", h=4), func=AF.Copy),
                    reads=[bankB[bi]], writes=[kTB[s]])

        def phaseR(t):
            ub = [4, 5, 0, 1]
            sbk = [2, 3, 6, 7]
            rbk = [4, 5, 0, 1]
            for s in range(S):
                gs = t * S + s
                bi = sbk[s]

                def f(e, s=s, bi=bi):
                    ins = None
                    for h in range(4):
                        ins = e.matmul(bank(bi)[:, h * 128:(h + 1) * 128], kT3[:, h, s * 128:(s + 1) * 128],
                                       qT3[:, h, s * 128:(s + 1) * 128], start=True, stop=True)
                    return ins
                P.op("pe", f, reads=[kTB[s], qTB[s]], writes=[bankB[bi]])
                bu = ub[s]

                def fu(e, s=s, bu=bu):
                    ins = None
                    for h in range(4):
                        ins = e.matmul(bank(bu)[:, h * 128:(h + 1) * 128],
                                       qkb[:, (S + s) * 512 + h * 128:(S + s) * 512 + (h + 1) * 128],
                                       vd[:, s * 512 + h * 128:s * 512 + (h + 1) * 128], start=True, stop=True)
                    return ins
                P.op("pe", fu, reads=[qkbB[1][s], vdB[s]], writes=[bankB[bu]])
                P.op("dve", lambda e, s=s, bi=bi: e.tensor_tensor(out=ST[s][:, :], in0=bank(bi), in1=mask[:, :],
                                                                  op=ALU.mult),
                     reads=[bankB[bi], constB], writes=[STB[s]])
                snew, sold = state2[gs % 2], state2[(gs - 1) % 2]

                def fs(e, bu=bu, snew=snew, sold=sold):
                    ins = None
                    for h in range(4):
                        ins = e.scalar_tensor_tensor(out=snew[:, h * 128:(h + 1) * 128],
                                                     in0=sold[:, h * 128:(h + 1) * 128], scalar=cds[h],
                                                     in1=bank(bu)[:, h * 128:(h + 1) * 128],
                                                     op0=ALU.mult, op1=ALU.add)
                    return ins
                P.op("dve", fs, reads=[bankB[bu], state2B[(gs - 1) % 2]], writes=[state2B[gs % 2]])
                sbi = gs % 5
                P.op("act", lambda e, sbi=sbi, snew=snew: e.activation(out=stateb[sbi][:, :], in_=snew[:, :],
                                                                       func=AF.Copy),
                     reads=[state2B[gs % 2]], writes=[statebB[sbi]])
            for s in range(S):
                gs = t * S + s
                bi = rbk[s]
                prevb = stateb[(gs - 1) % 5]
                prevB = statebB[(gs - 1) % 5]

                def f(e, s=s, gs=gs, bi=bi, prevb=prevb):
                    ins = None
                    for h in range(4):
                        o = bank(bi)[:, h * 128:(h + 1) * 128]
                        ins = e.matmul(o, ST[s][:, h * 128:(h + 1) * 128],
                                       vd[:, s * 512 + h * 128:s * 512 + (h + 1) * 128],
                                       start=True, stop=(gs == 0))
                        if gs > 0:
                            ins = e.matmul(o, qT3[:, h, s * 128:(s + 1) * 128], prevb[:, h * 128:(h + 1) * 128],
                                           start=False, stop=True)
                    return ins
                P.op("pe", f, reads=[STB[s], vdB[s], qTB[s], prevB], writes=[bankB[bi]])

            def rms_head(s):
                bi = rbk[s]

                def fq(e):
                    ins = None
                    for h in range(4):
                        ins = e.activation(out=Hh[0][:, h * 128:(h + 1) * 128], in_=bank(bi)[:, h * 128:(h + 1) * 128],
                                           func=AF.Square, accum_out=ss[s][:, h:h + 1])
                    return ins
                P.op("act", fq, reads=[bankB[bi]], writes=[ssB[s], HB[0]])
                P.op("dve", lambda e: e.tensor_scalar(
                    out=rstd[s][:, :], in0=ss[s][:, :], scalar1=1.0 / 128.0, scalar2=RMS_EPS,
                    op0=ALU.mult, op1=ALU.add), reads=[ssB[s]], writes=[rstdB[s]])
                P.op("pool", lambda e: e.tensor_tensor(
                    out=rstd[s][:, :], in0=rstd[s][:, :], in1=mhalf[:, :], op=ALU.pow),
                    reads=[], writes=[rstdB[s]])

            def rms_tail(s):
                bi = rbk[s]

                def fr(e):
                    ins = None
                    for h in range(4):
                        ins = e.scalar_tensor_tensor(out=retn[s][:, h * 128:(h + 1) * 128],
                                                     in0=bank(bi)[:, h * 128:(h + 1) * 128],
                                                     scalar=rstd[s][:, h:h + 1],
                                                     in1=sg[:, s * 512 + h * 128:s * 512 + (h + 1) * 128],
                                                     op0=ALU.mult, op1=ALU.mult)
                    return ins
                P.op("dve", fr, reads=[bankB[bi], rstdB[s], sgB[s]], writes=[retnB[s]])
                bt = 2 + (s % 2)

                def ft(e):
                    ins = None
                    for h in range(4):
                        ins = e.transpose(bank_bf(bt)[:, h * 128:(h + 1) * 128], retn[s][:, h * 128:(h + 1) * 128],
                                          ident[:, :])
                    return ins
                P.op("pe", ft, reads=[retnB[s], identB], writes=[bankB[bt]])
                P.op("act", lambda e: e.activation(
                    out=mixT3[:, 0:4, s * 128:(s + 1) * 128],
                    in_=bank_bf(bt)[:, 0:512].rearrange("p (h n) -> p h n", h=4), func=AF.Copy),
                    reads=[bankB[bt]], writes=[xbmB])
            rms_head(0)
            rms_head(1)
            rms_tail(0)
            rms_head(2)
            rms_tail(1)
            rms_head(3)
            rms_tail(2)
            rms_tail(3)

        def phaseW(t):
            r0, r0B = slot(t, 5)
            r1, r1B = slot(t, 6)
            wo = [r0[:, :].rearrange("p (k n) -> p k n", k=KD), r1[:, :].rearrange("p (k n) -> p k n", k=KD)]
            st = {}

            def stage1(s):
                qi = wtiles[s % 3]

                def f(e):
                    ins = None
                    for hh in range(2):
                        for k in range(KD):
                            ins = e.matmul(Q[qi][:, hh * 512:(hh + 1) * 512], mixT3[:, k, s * 128:(s + 1) * 128],
                                           wo[hh][:, k, :], start=(k == 0), stop=(k == KD - 1))
                    return ins
                P.op("pe", f, reads=[xbmB, r0B, r1B], writes=[bankB[2 * qi], bankB[2 * qi + 1]])
                yi = rr["y"] % 4
                rr["y"] += 1
                li = rr["ln"] % 4
                rr["ln"] += 1
                st[s] = (yi, li)
                Xs = X[:, s * D:(s + 1) * D]
                P.op("dve", lambda e: e.scalar_tensor_tensor(
                    out=Y[yi][:, :], in0=Xs, scalar=ALPHA, in1=Q[qi][:, :], op0=ALU.mult, op1=ALU.add),
                    reads=[XB[s], bankB[2 * qi], bankB[2 * qi + 1]], writes=[YB[yi]])
                ln_stats(yi, li)

            def ynb(s):
                return aT[:, s * 1024:(s + 1) * 1024]

            def stage1b(s):
                yi, li = st[s]
                ln_rstd(li)
                P.op("act", lambda e: e.activation(out=ynb(s), in_=Y[yi][:, :], func=AF.Identity,
                                                   bias=lnmv[li][:, 3:4], scale=lnmv[li][:, 2:3]),
                     reads=[lnB[li], YB[yi]], writes=[aTB[2 * s], aTB[2 * s + 1]])

            def stage2(s):
                yi, li = st[s]
                Xs = X[:, s * D:(s + 1) * D]
                P.op("dve", lambda e: e.tensor_scalar(out=Xs, in0=Y[yi][:, :], scalar1=lnmv[li][:, 2:3],
                                                      scalar2=lnmv[li][:, 3:4], op0=ALU.mult, op1=ALU.add),
                     reads=[YB[yi], lnB[li]], writes=[XB[s]])
                P.op("dve", lambda e: e.tensor_tensor(out=Xs, in0=Xs, in1=g1t[:, :], op=ALU.mult),
                     reads=[constB], writes=[XB[s]])
                P.op("dve", lambda e: e.tensor_tensor(out=Xs, in0=Xs, in1=b1t[:, :], op=ALU.add),
                     reads=[constB], writes=[XB[s]])

            def stage3(s):
                yi, li = st[s]
                qi = 1 if s % 2 == 0 else 2
                ba, bb = 2 * qi, 2 * qi + 1

                def f(e):
                    ins = None
                    for k in range(KD):
                        ob = bank_bf(ba) if k < 4 else bank_bf(bb)
                        ins = e.transpose(ob[:, (k % 4) * 128:(k % 4 + 1) * 128], ynb(s)[:, k * 128:(k + 1) * 128],
                                          ident[:, :])
                    return ins
                P.op("pe", f, reads=[aTB[2 * s], aTB[2 * s + 1], identB], writes=[bankB[ba], bankB[bb]])

                def fa(e):
                    ins = None
                    for k in range(0, 4):
                        ins = e.activation(out=XT3[:, k, s * 128:(s + 1) * 128],
                                           in_=bank_bf(ba)[:, k * 128:(k + 1) * 128],
                                           func=AF.Identity, bias=b1p[:, k:k + 1], scale=g1p[:, k:k + 1])
                    return ins
                P.op("act", fa, reads=[bankB[ba], constB], writes=[XTB[s]])

                def fd(e):
                    ins = None
                    for k in range(4, 8):
                        ins = e.tensor_scalar(out=XT3[:, k, s * 128:(s + 1) * 128],
                                              in0=bank_bf(bb)[:, (k - 4) * 128:(k - 3) * 128],
                                              scalar1=g1p[:, k:k + 1], scalar2=b1p[:, k:k + 1],
                                              op0=ALU.mult, op1=ALU.add)
                    return ins
                P.op("dve", fd, reads=[bankB[bb], constB], writes=[XTB2[s]])

            stage1(0)
            stage1(1)
            stage1b(0)
            stage1(2)
            stage1b(1)
            stage1(3)
            stage1b(2)
            stage1b(3)
            for s in range(S):
                stage3(s)
            issue_slot_load()
            issue_slot_load()
            return [(lambda s=s: stage2(s)) for s in range(S)]

        def phaseU(t, deferred):
            pend = []
            for u in range(11):
                rt, rB = slot(t, 7 + u)
                r3 = rt[:, :].rearrange("p (k n) -> p k n", k=KD)
                for cl in range(2):
                    c = 2 * u + cl
                    vb = (0, 1, 4, 5)[rr["u"] % 4]
                    gb = (6, 7)[rr["u"] % 2]
                    rr["u"] += 1

                    def f(e, vb=vb, gb=gb, cl=cl, r3=r3):
                        ins = None
                        for part in range(2):
                            col = part * 256 + cl * 128
                            ob = bank(vb) if part == 0 else bank(gb)
                            for k in range(KD):
                                ins = e.matmul(ob, r3[:, k, col:col + 128],
                                               XT3[:, k, :], start=(k == 0), stop=(k == KD - 1))
                        return ins
                    P.op("pe", f, reads=XTB + XTB2 + [rB], writes=[bankB[vb], bankB[gb]])
                    fi = rr["ffn"] % 3
                    rr["ffn"] += 1
                    Vp = bank(vb)
                    Gp = bank(gb)
                    def fg(e, fi=fi, c=c, Gp=Gp):
                        e.activation(out=G[fi][:, 0:2], in_=carry[:, 2 * c:2 * c + 2], func=AF.Copy)
                        return e.activation(out=G[fi][:, 2:514], in_=Gp, func=AF.Copy)
                    P.op("act", fg, reads=[carryB[c], bankB[gb]], writes=[GB[fi]])
                    P.op("act", lambda e, c=c, Gp=Gp: e.activation(out=carry[:, 2 * c:2 * c + 2],
                                                                  in_=Gp[:, 510:512], func=AF.Copy),
                         reads=[bankB[gb]], writes=[carryB[c]])
                    P.op("act", lambda e, fi=fi, Gp=Gp, c=c: e.activation(
                        out=Hh[fi][:, :], in_=Gp, func=AF.Identity, bias=cb[:, c:c + 1], scale=cw3[:, c, 2:3]),
                        reads=[bankB[gb], constB], writes=[HB[fi]])
                    P.op("dve", lambda e, fi=fi, c=c: e.scalar_tensor_tensor(
                        out=Hh[fi][:, :], in0=G[fi][:, 1:513], scalar=cw3[:, c, 1:2], in1=Hh[fi][:, :],
                        op0=ALU.mult, op1=ALU.add),
                        reads=[GB[fi], constB], writes=[HB[fi]])
                    P.op("dve", lambda e, fi=fi, c=c: e.scalar_tensor_tensor(
                        out=Hh[fi][:, :], in0=G[fi][:, 0:512], scalar=cw3[:, c, 0:1], in1=Hh[fi][:, :],
                        op0=ALU.mult, op1=ALU.add),
                        reads=[GB[fi], constB], writes=[HB[fi]])
                    if pend:
                        pend.pop(0)()

                    def tail(fi=fi, c=c, Vp=Vp, vb=vb):
                        P.op("act", lambda e: e.activation(out=Hh[fi][:, :], in_=Hh[fi][:, :], func=AF.Silu),
                             reads=[], writes=[HB[fi]])
                        P.op("dve", lambda e: e.tensor_tensor(
                            out=aT[:, c * TT:(c + 1) * TT], in0=Hh[fi][:, :], in1=Vp, op=ALU.mult),
                            reads=[HB[fi], bankB[vb]], writes=[aTB[c]])
                    pend.append(tail)
                    if deferred and c in (3, 7, 11, 15):
                        deferred.pop(0)()
                issue_slot_load()
                if t + 1 < NT and 4 <= u < 8:
                    phaseX_group(t + 1, u - 4)
            while pend:
                pend.pop(0)()

        def phaseD(t):
            for v in range(6):
                rt, rB = slot(t, 18 + v)
                nch = 4 if v < 5 else 2
                r3 = rt[:, 0:nch * 1024].rearrange("p (c n) -> p c n", c=nch)
                for s in (1, 3, 2, 0):
                    def f(e, v=v, nch=nch, r3=r3, s=s):
                        ins = None
                        for hh in range(2):
                            for cl in range(nch):
                                c = 4 * v + cl
                                ins = e.matmul(Q[s][:, hh * 512:(hh + 1) * 512],
                                               aT[:, c * TT + s * 128:c * TT + (s + 1) * 128],
                                               r3[:, cl, hh * 512:(hh + 1) * 512],
                                               start=(c == 0), stop=(c == NF - 1))
                        return ins
                    P.op("pe", f, reads=aTB[4 * v:4 * v + nch] + [rB], writes=[bankB[2 * s], bankB[2 * s + 1]])
                issue_slot_load()

        def phaseL2(t, filler):
            st = {}

            def stage1(s):
                yi = rr["y"] % 4
                rr["y"] += 1
                li = rr["ln"] % 4
                rr["ln"] += 1
                st[s] = (yi, li)
                Xs = X[:, s * D:(s + 1) * D]
                P.op("dve", lambda e: e.scalar_tensor_tensor(
                    out=Y[yi][:, :], in0=Xs, scalar=ALPHA, in1=Q[s][:, :], op0=ALU.mult, op1=ALU.add),
                    reads=[XB[s], bankB[2 * s], bankB[2 * s + 1]], writes=[YB[yi]])
                ln_stats(yi, li)

            def stage1b(s):
                yi, li = st[s]
                ln_fin(yi, li)

            def stage2(s):
                yi, li = st[s]
                P.op("dve", lambda e: e.tensor_tensor(out=Y[yi][:, :], in0=Y[yi][:, :], in1=g2t[:, :], op=ALU.mult),
                     reads=[constB], writes=[YB[yi]])
                P.op("dve" if t == NT - 1 else "pool",
                     lambda e: e.tensor_tensor(out=Y[yi][:, :], in0=Y[yi][:, :], in1=b2t[:, :], op=ALU.add),
                     reads=[constB], writes=[YB[yi]])
                r0w = t * TT + s * 128
                out_toks.append(P.dma("sp", out_d[r0w:r0w + 128, :], Y[yi][:, :], "Y%d" % yi, reads=[YB[yi]]))
            stage1(0)
            filler(2)
            stage1b(0)
            stage1(1)
            filler(2)
            stage1b(1)
            stage2(0)
            stage1(2)
            filler(2)
            stage1b(2)
            stage2(1)
            stage1(3)
            filler(2)
            stage1b(3)
            stage2(2)
            filler(2)
            stage2(3)
            filler(100)

        def proj_items(t):
            items = []
            for sid in (0, 1, 2):
                for s in range(S):
                    items.append(lambda sid=sid, s=s: proj_group(t, sid, s))
                items.append(None)
            return items

        def make_filler(items):
            def filler(n):
                k = 0
                while items and k < n:
                    it = items.pop(0)
                    if it is None:
                        issue_slot_load()
                    else:
                        it()
                        k += 1
                while items and items[0] is None:
                    items.pop(0)
                    issue_slot_load()
            return filler

        phaseX(0)
        make_filler(proj_items(0))(100)
        for t in range(NT):
            if t + 1 < NT:
                csload(t + 1)
            for sid in (3, 4):
                for s in range(S):
                    proj_group(t, sid, s)
                issue_slot_load()
            phaseT(t)
            phaseO(t)
            phaseR(t)
            dbg("mixT", lambda: xbm[:, :], [128, 4096], [xbmB], BF16)
            deferred = phaseW(t)
            if t + 1 < NT:
                xload_bf(t + 1)
            phaseU(t, deferred)
            dbg("x1", lambda: X[:, :], [128, S * D], XB)
            dbg("aT", lambda: aT[:, :], [128, NF * TT], aTB, BF16)
            phaseD(t)
            phaseL2(t, make_filler(proj_items(t + 1) if t + 1 < NT else []))
            if t + 1 < NT:
                xload_f32(t + 1)
        P.wait_all("sp", out_toks)
        P.emit()
    return nc, list(dbg_d.keys())


def make_inputs(x_b, w):
    consts, _ = host_consts()
    m = {
        "x": np.ascontiguousarray(x_b, dtype=np.float32),
        "w_in": np.ascontiguousarray(w["w_in"][0]),
        "w_pool": np.ascontiguousarray(w["w_pool"][0]),
        "w_out": np.ascontiguousarray(w["w_out"][0]),
        "w_up": np.ascontiguousarray(w["w_up"][0]),
        "w_down": np.ascontiguousarray(w["w_down"][0]),
        "pscale": np.ascontiguousarray(w["pool_scale"][0].reshape(4, 128).T),
        "g1t": np.ascontiguousarray(np.broadcast_to(w["ln1_g"][0][None, :], (128, D))),
        "b1t": np.ascontiguousarray(np.broadcast_to(w["ln1_b"][0][None, :], (128, D))),
        "g2t": np.ascontiguousarray(np.broadcast_to(w["ln2_g"][0][None, :], (128, D))),
        "b2t": np.ascontiguousarray(np.broadcast_to(w["ln2_b"][0][None, :], (128, D))),
        "cw": np.ascontiguousarray(w["conv_w"][0].reshape(3, NF, 128).transpose(2, 1, 0).reshape(128, NF * 3)),
        "cb": np.ascontiguousarray(w["conv_b"][0].reshape(NF, 128).T),
        "rope": consts["rope"],
        "qdec": consts["qdec"],
        "vdec": consts["vdec"],
        "mask": consts["mask"],
        "bands": np.ascontiguousarray(consts["bands"].reshape(128, 12 * 128)),
        "ident": consts["ident"],
        "g1p": np.ascontiguousarray(w["ln1_g"][0].reshape(KD, 128).T),
        "b1p": np.ascontiguousarray(w["ln1_b"][0].reshape(KD, 128).T),
    }
    return m


def kernel(x, w_in, w_pool, pool_scale, w_out, ln1_g, ln1_b, w_up, conv_w, conv_b, w_down, ln2_g, ln2_b):
    w = dict(w_in=np.asarray(w_in, np.float32), w_pool=np.asarray(w_pool, np.float32),
             pool_scale=np.asarray(pool_scale, np.float32), w_out=np.asarray(w_out, np.float32),
             ln1_g=np.asarray(ln1_g, np.float32), ln1_b=np.asarray(ln1_b, np.float32),
             w_up=np.asarray(w_up, np.float32), conv_w=np.asarray(conv_w, np.float32),
             conv_b=np.asarray(conv_b, np.float32), w_down=np.asarray(w_down, np.float32),
             ln2_g=np.asarray(ln2_g, np.float32), ln2_b=np.asarray(ln2_b, np.float32))
    x = np.asarray(x, np.float32)
    nb, T, _ = x.shape
    nc, _ = build(T // TT)
    in_maps = [make_inputs(x[b], w) for b in range(nb)]
    res = run_bass_kernel_spmd(nc, in_maps, core_ids=list(range(nb)))
    return np.stack([np.asarray(r["out"], dtype=np.float32) for r in res.results], axis=0)
```

```python
import contextlib
import numpy as np
import ml_dtypes
import concourse.bass as bass
import concourse.mybir as mybir
from concourse.bass_utils import run_bass_kernel_spmd

F32 = mybir.dt.float32
BF16 = mybir.dt.bfloat16
ALU = mybir.AluOpType
AF = mybir.ActivationFunctionType

D = 1024
KD = 8
TT = 512
S = 4
DFF = 2816
NF = 22
NSLOT = 24
RING = 3
PROJ_ORDER = (2, 3, 4, 0, 1)
ALPHA = 2.0 ** 0.25
LN_EPS = 1e-5
RMS_EPS = 1e-6
SELF_SYNC = True


class Buf:
    def __init__(self, name, excl=False):
        self.name = name
        self.w = []
        self.r = []
        self.excl = excl


class Prog:
    ENG = ("pe", "act", "dve", "pool", "sp")

    def __init__(self, nc, es):
        self.nc = nc
        self.es = es
        self.streams = {e: [] for e in self.ENG}
        self.cnt = {e: 0 for e in self.ENG}
        self.seen = {e: {} for e in self.ENG}
        self.sems = {}
        self.dcnt = {}
        for e in self.ENG:
            self.sems[e] = es.enter_context(nc.semaphore("s_" + e))

    def dsem(self, key):
        if key not in self.sems:
            self.sems[key] = self.es.enter_context(self.nc.semaphore("d_" + key))
            self.dcnt[key] = 0
        return self.sems[key]

    def _deps(self, eng, reads, writes):
        toks = []
        for b in reads:
            toks += b.w
            if b.excl:
                toks += b.r
        for b in writes:
            toks += b.w
            toks += b.r
        need = {}
        for (k, v) in toks:
            if k == eng and (not SELF_SYNC or eng == "pe"):
                continue
            if self.seen[eng].get(k, 0) >= v:
                continue
            if need.get(k, 0) < v:
                need[k] = v
        for k, v in need.items():
            self.seen[eng][k] = v
            self.streams[eng].append(("w", k, v))

    def _commit(self, tok, reads, writes):
        for b in reads:
            if b not in writes:
                b.r.append(tok)
                if len(b.r) > 64:
                    best = {}
                    for (k, v) in b.r:
                        if best.get(k, 0) < v:
                            best[k] = v
                    b.r = list(best.items())
        for b in writes:
            b.w = [tok]
            b.r = []

    def op(self, eng, fn, reads=(), writes=()):
        reads = list(reads)
        writes = list(writes)
        self._deps(eng, reads, writes)
        self.streams[eng].append(("op", fn))
        self.cnt[eng] += 1
        tok = (eng, self.cnt[eng])
        self._commit(tok, reads, writes)
        return tok

    def dma(self, eng, out, in_, key, reads=(), writes=()):
        reads = list(reads)
        writes = list(writes)
        self.dsem(key)
        self._deps(eng, reads, writes)
        self.streams[eng].append(("dma", out, in_, key))
        self.dcnt[key] += 16
        tok = (key, self.dcnt[key])
        self._commit(tok, reads, writes)
        return tok

    def wait_all(self, eng, toks):
        need = {}
        for (k, v) in toks:
            if need.get(k, 0) < v:
                need[k] = v
        for k, v in need.items():
            self.streams[eng].append(("w", k, v))

    def emit(self):
        nc = self.nc

        def mk(engname):
            def body(e):
                for ent in self.streams[engname]:
                    if ent[0] == "w":
                        e.wait_ge(self.sems[ent[1]], ent[2])
                    elif ent[0] == "op":
                        ins = ent[1](e)
                        ins.then_inc(self.sems[engname], 1)
                    else:
                        e.dma_start(out=ent[1], in_=ent[2]).then_inc(self.sems[ent[3]], 16)
            return body

        with nc.Block() as block:
            block.tensor(mk("pe"))
            block.scalar(mk("act"))
            block.vector(mk("dve"))
            block.gpsimd(mk("pool"))
            block.sync(mk("sp"))


def host_consts():
    dh = 128
    B = 128
    inv_freq = (10000.0 ** (-np.arange(0, dh, 2, dtype=np.float32) / np.float32(dh))).astype(np.float32)
    ang = (np.arange(4096, dtype=np.float32)[:, None] * inv_freq[None, :]).astype(np.float32)
    rope = np.concatenate([np.cos(ang), np.sin(ang)], axis=1).astype(np.float32)
    gam = 1.0 - 2.0 ** (-5.0 - np.arange(4, dtype=np.float64))
    i = np.arange(B, dtype=np.float64)
    qdec = np.zeros((128, 512), np.float32)
    vdec = np.zeros((128, 4), np.float32)
    mask = np.zeros((128, 512), np.float32)
    cds = []
    for h in range(4):
        g = gam[h]
        qd = g ** (i + 1.0) * dh ** -0.5
        qdec[:, h * 128:(h + 1) * 128] = qd[None, :]
        vd = g ** (B - 1.0 - i)
        vdec[:, h] = vd
        jj = i[:, None]
        ii = i[None, :]
        W = np.where((jj // 64) <= (ii // 64), g ** np.abs(ii - jj), 0.0)
        M = W / (g ** (ii + 1.0) * g ** (B - 1.0 - jj))
        mask[:, h * 128:(h + 1) * 128] = M
        cds.append(float(g ** B))
    bands = np.zeros((128, 12, 128), np.float32)
    m = np.arange(128)[:, None]
    t = np.arange(128)[None, :]
    for gi, w in enumerate((2, 4, 8, 16)):
        cur = np.where((m <= t) & (m > t - w), 1.0 / w, 0.0) - (m == t)
        prev = np.where((m - 128 <= t) & (m - 128 > t - w), 1.0 / w, 0.0)
        div = np.minimum(t + 1.0, float(w))
        cur0 = np.where((m <= t) & (m > t - w), 1.0 / div, 0.0) - (m == t)
        bands[:, gi, :] = prev
        bands[:, 4 + gi, :] = cur
        bands[:, 8 + gi, :] = cur0
    ident = np.eye(128, dtype=np.float32)
    return dict(rope=rope, qdec=qdec, vdec=vdec, mask=mask,
                bands=bands.astype(ml_dtypes.bfloat16), ident=ident.astype(ml_dtypes.bfloat16)), cds


def build(NT, debug=(), stop=None):
    consts, cds = host_consts()
    T = NT * TT
    nc = bass.Bass("TRN2", target_bir_lowering=False)
    dt_in = lambda name, shape, dt=F32: nc.dram_tensor(name, list(shape), dt, kind="ExternalInput").ap()
    x_d = dt_in("x", [T, D])
    w_in_d = dt_in("w_in", [D, 2560])
    w_pool_d = dt_in("w_pool", [4, 128, 128])
    w_out_d = dt_in("w_out", [D, D])
    w_up_d = dt_in("w_up", [D, 2 * DFF])
    w_down_d = dt_in("w_down", [DFF, D])
    pscale_d = dt_in("pscale", [128, 4])
    g1_d = dt_in("g1t", [128, D])
    b1_d = dt_in("b1t", [128, D])
    g2_d = dt_in("g2t", [128, D])
    b2_d = dt_in("b2t", [128, D])
    cw_d = dt_in("cw", [128, NF * 3])
    cb_d = dt_in("cb", [128, NF])
    rope_d = dt_in("rope", [4096, 128])
    qdec_d = dt_in("qdec", [128, 512])
    vdec_d = dt_in("vdec", [128, 4])
    mask_d = dt_in("mask", [128, 512])
    bands_d = dt_in("bands", [128, 12 * 128], BF16)
    g1p_d = dt_in("g1p", [128, KD])
    b1p_d = dt_in("b1p", [128, KD])
    ident_d = dt_in("ident", [128, 128], BF16)
    out_d = nc.dram_tensor("out", [T, D], F32, kind="ExternalOutput").ap()
    scr_d = nc.dram_tensor("wscr", [NSLOT, 128, 4096], BF16, kind="Internal").ap()
    dbg_d = {}

    es = contextlib.ExitStack()
    with es:
        P = Prog(nc, es)

        def sb(name, shape, dt=F32):
            return es.enter_context(nc.sbuf_tensor("sb_" + name, list(shape), dt))

        def ps(name, shape, dt=F32):
            return es.enter_context(nc.psum_tensor("ps_" + name, list(shape), dt))

        ring = [sb("ring%d" % i, [128, 4096], BF16) for i in range(RING)]
        ringB = [Buf("ring%d" % i) for i in range(RING)]
        scrB = [Buf("scr%d" % i) for i in range(NSLOT)]
        ident = sb("ident", [128, 128], BF16)
        qdec = sb("qdec", [128, 512])
        vdec = sb("vdec", [128, 4])
        mask = sb("mask", [128, 512])
        bands = sb("bands", [128, 12 * 128], BF16)
        wpool = sb("wpool", [128, 512], BF16)
        pscale = sb("pscale", [128, 4])
        g1t = sb("g1t", [128, D])
        b1t = sb("b1t", [128, D])
        g2t = sb("g2t", [128, D])
        b2t = sb("b2t", [128, D])
        cw = sb("cw", [128, NF * 3])
        cb = sb("cb", [128, NF])
        g1p = sb("g1p", [128, KD])
        b1p = sb("b1p", [128, KD])
        identB = Buf("identc")
        constB = Buf("const")
        cs = [sb("cs%d" % i, [128, S * 128]) for i in range(2)]
        csB = [Buf("cs%d" % i) for i in range(2)]
        X = sb("X", [128, S * D])
        XB = [Buf("X%d" % s) for s in range(S)]
        xbm = sb("xbm", [128, 4096], BF16)
        xbmB = Buf("xbm")
        XT = sb("XT", [128, KD * TT], BF16)
        XTB = [Buf("XT%d" % s) for s in range(S)]
        XTB2 = [Buf("XTh%d" % s) for s in range(S)]
        XA = sb("XA", [128, KD * TT], BF16)
        XAB = [Buf("XA%d" % s) for s in range(S)]
        qkb = sb("qkb", [128, 2 * S * 512], BF16)
        qkbB = [[Buf("qkb%d_%d" % (j, s)) for s in range(S)] for j in range(2)]
        qT = sb("qT", [128, 4 * TT], BF16)
        kT = sb("kT", [128, 4 * TT], BF16)
        qTB = [Buf("qT%d" % s) for s in range(S)]
        kTB = [Buf("kT%d" % s) for s in range(S)]
        vd = sb("vd", [128, S * 512], BF16)
        vdB = [Buf("vd%d" % s) for s in range(S)]
        sg = sb("sg", [128, S * 512])
        sgB = [Buf("sg%d" % s) for s in range(S)]
        pb = sb("pb", [128, 5 * 512], BF16)
        pbB = [Buf("pb%d" % i) for i in range(5)]
        ropeA = [sb("ropeA%d" % i, [128, 512]) for i in range(2)]
        ropeBt = [sb("ropeB%d" % i, [128, 512]) for i in range(2)]
        ropeAB = [Buf("ropeA%d" % i) for i in range(2)]
        ropeBB = [Buf("ropeB%d" % i) for i in range(2)]
        ST = [sb("ST%d" % i, [128, 512], BF16) for i in range(4)]
        STB = [Buf("ST%d" % i) for i in range(4)]
        state2 = [sb("state%d" % i, [128, 512]) for i in range(2)]
        state2B = [Buf("state%d" % i) for i in range(2)]
        stateb = [sb("stateb%d" % i, [128, 512], BF16) for i in range(5)]
        statebB = [Buf("stateb%d" % i) for i in range(5)]
        ss = [sb("ss%d" % i, [128, 4]) for i in range(4)]
        rstd = [sb("rstd%d" % i, [128, 4]) for i in range(4)]
        ssB = [Buf("ss%d" % i) for i in range(4)]
        rstdB = [Buf("rstd%d" % i) for i in range(4)]
        retn = [sb("retn%d" % i, [128, 512], BF16) for i in range(4)]
        retnB = [Buf("retn%d" % i) for i in range(4)]
        pooledT = sb("pooledT", [128, 4 * TT], BF16)
        pooledB = [Buf("pooled%d" % s) for s in range(S)]
        Y = [sb("Y%d" % i, [128, D]) for i in range(4)]
        YB = [Buf("Y%d" % i) for i in range(4)]
        lnst = [sb("lnst%d" % i, [128, 12]) for i in range(4)]
        lnmv = [sb("lnmv%d" % i, [128, 4]) for i in range(4)]
        lnB = [Buf("ln%d" % i) for i in range(4)]
        G = [sb("G%d" % i, [128, 514]) for i in range(3)]
        Hh = [sb("H%d" % i, [128, 512]) for i in range(3)]
        GB = [Buf("G%d" % i) for i in range(3)]
        HB = [Buf("H%d" % i) for i in range(3)]
        carry = sb("carry", [128, NF * 2])
        carryB = [Buf("carry%d" % c) for c in range(NF)]
        aT = sb("aT", [128, NF * TT], BF16)
        aTB = [Buf("aT%d" % c) for c in range(NF)]
        Q = [ps("Q%d" % i, [128, 1024]) for i in range(4)]
        bankB = [Buf("bank%d" % i, excl=True) for i in range(8)]

        def bank(i):
            return Q[i // 2][:, (i % 2) * 512:(i % 2) * 512 + 512]

        def bank_bf(i):
            return bank(i).bitcast(BF16)

        P.dma("sp", ident[:, :], ident_d, "identc", writes=[identB])
        ctoks = []
        for (dst, src) in ((qdec, qdec_d), (vdec, vdec_d), (mask, mask_d), (bands, bands_d),
                           (pscale, pscale_d), (g1t, g1_d), (b1t, b1_d), (g2t, g2_d), (b2t, b2_d),
                           (cw, cw_d), (cb, cb_d), (g1p, g1p_d), (b1p, b1p_d)):
            ctoks.append(P.dma("sp", dst[:, :], src, "const", writes=[Buf("c_" + dst.name)]))
        mhalf = sb("mhalf", [128, 4])
        P.op("pool", lambda e: e.memset(mhalf[:, :], -0.5), writes=[Buf("mhalf")])
        P.op("pool", lambda e: e.memset(carry[:, :], 0.0), writes=carryB)
        P.op("pool", lambda e: e.memset(state2[1][:, :], 0.0), writes=[state2B[1]])

        def slot_src(sid):
            if sid < 5:
                j = PROJ_ORDER[sid]
                return [(lambda r: r[:, :].rearrange("p (k n) -> p k n", k=KD),
                         w_in_d[:, j * 512:(j + 1) * 512].rearrange("(k p) n -> p k n", p=128))]
            if sid < 7:
                h = sid - 5
                return [(lambda r: r[:, :].rearrange("p (k n) -> p k n", k=KD),
                         w_out_d[:, h * 512:(h + 1) * 512].rearrange("(k p) n -> p k n", p=128))]
            if sid < 18:
                u = sid - 7
                return [
                    (lambda r: r[:, :].rearrange("p (k n) -> p k n", k=KD)[:, :, 0:256],
                     w_up_d[:, u * 256:(u + 1) * 256].rearrange("(k p) n -> p k n", p=128)),
                    (lambda r: r[:, :].rearrange("p (k n) -> p k n", k=KD)[:, :, 256:512],
                     w_up_d[:, DFF + u * 256:DFF + (u + 1) * 256].rearrange("(k p) n -> p k n", p=128)),
                ]
            v = sid - 18
            nch = 4 if v < 5 else 2
            return [(lambda r: r[:, 0:nch * 1024].rearrange("p (c n) -> p c n", c=nch),
                     w_down_d[v * 512:v * 512 + nch * 128, :].rearrange("(c p) n -> p c n", p=128))]

        nseq = NT * NSLOT
        issued = [0]

        def issue_slot_load():
            g = issued[0]
            if g >= nseq:
                return
            issued[0] += 1
            t, sid = divmod(g, NSLOT)
            ri = g % RING
            if t == 0:
                for (dv, src) in slot_src(sid):
                    P.dma("pool", dv(ring[ri]), src, "ringp%d" % ri, writes=[ringB[ri]])
                ncol = 2048 if sid == NSLOT - 1 else 4096
                if NT > 1:
                    P.dma("sp", scr_d[sid][:, 0:ncol], ring[ri][:, 0:ncol], "ringst%d" % ri, reads=[ringB[ri]],
                          writes=[scrB[sid]])
            else:
                ncol = 2048 if sid == NSLOT - 1 else 4096
                P.dma("sp", ring[ri][:, 0:ncol], scr_d[sid][:, 0:ncol], "ring%d" % ri, reads=[scrB[sid]],
                      writes=[ringB[ri]])

        def slot(t, sid):
            g = t * NSLOT + sid
            return ring[g % RING], ringB[g % RING]

        def xload_bf(t):
            P.dma("pool", xbm[:, :].rearrange("p (s d) -> p s d", s=S),
                  x_d[t * TT:(t + 1) * TT, :].rearrange("(s p) d -> p s d", p=128), "xbm", writes=[xbmB])

        def xload_f32(t):
            for s in range(S):
                P.dma("sp", X[:, s * D:(s + 1) * D], x_d[t * TT + s * 128:t * TT + (s + 1) * 128, :],
                      "X%d" % s, writes=[XB[s]])

        def csload(t):
            P.dma("sp", cs[t % 2][:, :].rearrange("p (s c) -> p s c", s=S),
                  rope_d[t * TT:(t + 1) * TT, :].rearrange("(s p) c -> p s c", p=128), "cs%d" % (t % 2),
                  writes=[csB[t % 2]])

        def dbg(name, ap_fn, shape, reads, dt=F32):
            if name not in debug:
                return
            if name not in dbg_d:
                dbg_d[name] = nc.dram_tensor("dbg_" + name, list(shape), dt, kind="ExternalOutput").ap()
            tok = P.dma("sp", dbg_d[name], ap_fn(), "dbg_" + name, reads=reads)
            out_toks.append(tok)

        def ln_stats(yi, li):
            def lnstats(e):
                e.bn_stats(out=lnst[li][:, 0:6], in_=Y[yi][:, 0:512])
                return e.bn_stats(out=lnst[li][:, 6:12], in_=Y[yi][:, 512:1024])
            P.op("dve", lnstats, reads=[YB[yi]], writes=[lnB[li]])
            P.op("dve", lambda e: e.bn_aggr(out=lnmv[li][:, 0:2], in_=lnst[li][:, :]), reads=[], writes=[lnB[li]])
            P.op("dve", lambda e: e.tensor_scalar(out=lnmv[li][:, 2:3], in0=lnmv[li][:, 1:2], scalar1=LN_EPS,
                                                  scalar2=1.0, op0=ALU.add, op1=ALU.mult),
                 reads=[], writes=[lnB[li]])

        def ln_rstd(li):
            P.op("pool", lambda e: e.tensor_tensor(out=lnmv[li][:, 2:3], in0=lnmv[li][:, 2:3], in1=mhalf[:, 0:1],
                                                   op=ALU.pow), reads=[], writes=[lnB[li]])
            P.op("dve", lambda e: e.scalar_tensor_tensor(out=lnmv[li][:, 3:4], in0=lnmv[li][:, 0:1], scalar=-1.0,
                                                         in1=lnmv[li][:, 2:3], op0=ALU.mult, op1=ALU.mult),
                 reads=[], writes=[lnB[li]])

        def ln_fin(yi, li):
            ln_rstd(li)
            P.op("act", lambda e: e.activation(out=Y[yi][:, :], in_=Y[yi][:, :], func=AF.Identity,
                                               bias=lnmv[li][:, 3:4], scale=lnmv[li][:, 2:3]),
                 reads=[lnB[li]], writes=[YB[yi]])

        out_toks = []
        xload_bf(0)
        ctoks.append(P.dma("pool", wpool[:, :].rearrange("c (g d) -> c g d", g=4),
                           w_pool_d.rearrange("g c d -> c g d"), "constp", writes=[Buf("c_wpool")]))
        constB.w = [ctoks[-2], ctoks[-1]]
        for _ in range(RING):
            issue_slot_load()
        xload_f32(0)
        csload(0)
        rr = dict(pe=0, ffn=0, y=0, ln=0, u=0)
        XT3 = XT[:, :].rearrange("p (k n) -> p k n", k=KD)
        mixT3 = xbm[:, :].rearrange("p (k n) -> p k n", k=KD)
        xb3 = xbm[:, :].rearrange("p (s d) -> p s d", s=S)
        qT3 = qT[:, :].rearrange("p (h n) -> p h n", h=4)
        kT3 = kT[:, :].rearrange("p (h n) -> p h n", h=4)
        pT3 = pooledT[:, :].rearrange("p (g n) -> p g n", g=4)
        cw3 = cw[:, :].rearrange("p (c j) -> p c j", j=3)
        wtiles = [0, 2, 3]

        XA3 = XA[:, :].rearrange("p (k n) -> p k n", k=KD)

        def phaseX_group(t, s):
            bi = 2 + (s % 2)

            def f(e):
                ins = None
                for k in range(KD):
                    ins = e.transpose(bank_bf(bi)[:, k * 128:(k + 1) * 128], xb3[:, s, k * 128:(k + 1) * 128],
                                      ident[:, :])
                return ins
            P.op("pe", f, reads=[xbmB, identB], writes=[bankB[bi]])
            if s % 2 == 0:
                P.op("act", lambda e: e.activation(out=XA3[:, :, s * 128:(s + 1) * 128],
                                                   in_=bank_bf(bi).rearrange("p (k n) -> p k n", k=KD),
                                                   func=AF.Copy),
                     reads=[bankB[bi]], writes=[XAB[s]])
            else:
                P.op("dve", lambda e: e.tensor_copy(out=XA3[:, :, s * 128:(s + 1) * 128],
                                                    in_=bank_bf(bi).rearrange("p (k n) -> p k n", k=KD)),
                     reads=[bankB[bi]], writes=[XAB[s]])

        def phaseX(t):
            for s in range(S):
                phaseX_group(t, s)

        def proj_group(t, sid, s):
            j = PROJ_ORDER[sid]
            rt, rB = slot(t, sid)
            r3 = rt[:, :].rearrange("p (k n) -> p k n", k=KD)
            bi = rr["pe"] % 2
            rr["pe"] += 1
            cst = cs[t % 2]
            cstB = csB[t % 2]

            def f(e):
                ins = None
                for k in range(KD):
                    ins = e.matmul(bank(bi), XA3[:, k, s * 128:(s + 1) * 128], r3[:, k, :],
                                   start=(k == 0), stop=(k == KD - 1))
                return ins
            P.op("pe", f, reads=[XAB[s], rB], writes=[bankB[bi]])
            bk = bank(bi)
            if j < 2:
                ri = (j * S + s) % 2
                A, Bt = ropeA[ri], ropeBt[ri]
                bk4 = bk.rearrange("p (h two x) -> p h two x", h=4, two=2)
                A4 = A[:, :].rearrange("p (h two x) -> p h two x", h=4, two=2)
                B4 = Bt[:, :].rearrange("p (h two x) -> p h two x", h=4, two=2)
                cosv = cst[:, s * 128:s * 128 + 64].unsqueeze(1).unsqueeze(1).to_broadcast([128, 4, 2, 64])
                sinv = cst[:, s * 128 + 64:s * 128 + 128].unsqueeze(1).to_broadcast([128, 4, 64])
                P.op("dve", lambda e: e.tensor_tensor(out=A4, in0=bk4, in1=cosv, op=ALU.mult),
                     reads=[bankB[bi], cstB], writes=[ropeAB[ri]])

                def fb(e):
                    e.scalar_tensor_tensor(out=B4[:, :, 0, :], in0=bk4[:, :, 1, :], scalar=-1.0, in1=sinv,
                                           op0=ALU.mult, op1=ALU.mult)
                    return e.tensor_tensor(out=B4[:, :, 1, :], in0=bk4[:, :, 0, :], in1=sinv, op=ALU.mult)
                P.op("dve", fb, reads=[bankB[bi], cstB], writes=[ropeBB[ri]])
                dst = qkb[:, (j * S + s) * 512:(j * S + s + 1) * 512]
                P.op("pool", lambda e: e.tensor_tensor(out=dst, in0=A[:, :], in1=Bt[:, :], op=ALU.add),
                     reads=[ropeAB[ri], ropeBB[ri]], writes=[qkbB[j][s]])
            elif j == 2:
                def fv(e):
                    ins = None
                    for h in range(4):
                        ins = e.activation(out=vd[:, s * 512 + h * 128:s * 512 + (h + 1) * 128],
                                           in_=bk[:, h * 128:(h + 1) * 128], func=AF.Identity,
                                           scale=vdec[:, h:h + 1])
                    return ins
                P.op("act", fv, reads=[bankB[bi], constB], writes=[vdB[s]])
            elif j == 3:
                P.op("act", lambda e: e.activation(out=sg[:, s * 512:(s + 1) * 512], in_=bk, func=AF.Silu),
                     reads=[bankB[bi]], writes=[sgB[s]])
            else:
                pi = (t * S + s) % 5
                P.op("act", lambda e: e.activation(out=pb[:, pi * 512:(pi + 1) * 512], in_=bk, func=AF.Copy),
                     reads=[bankB[bi]], writes=[pbB[pi]])

        def phaseO(t):
            for s in range(S):
                gs = t * S + s
                pi, pp = gs % 5, (gs - 1) % 5
                bi = 4 + (s % 2)

                def f(e, gs=gs, pi=pi, pp=pp, bi=bi):
                    ins = None
                    for g in range(4):
                        o = bank(bi)[:, g * 128:(g + 1) * 128]
                        if gs == 0:
                            ins = e.matmul(o, pb[:, pi * 512 + g * 128:pi * 512 + (g + 1) * 128],
                                           bands[:, (8 + g) * 128:(9 + g) * 128], start=True, stop=True)
                        else:
                            e.matmul(o, pb[:, pi * 512 + g * 128:pi * 512 + (g + 1) * 128],
                                     bands[:, (4 + g) * 128:(5 + g) * 128], start=True, stop=False)
                            ins = e.matmul(o, pb[:, pp * 512 + g * 128:pp * 512 + (g + 1) * 128],
                                           bands[:, g * 128:(g + 1) * 128], start=False, stop=True)
                    return ins
                P.op("pe", f, reads=[pbB[pi], pbB[pp], constB], writes=[bankB[bi]])
                P.op("act", lambda e, s=s, bi=bi: e.activation(out=pT3[:, :, s * 128:(s + 1) * 128],
                                                               in_=bank(bi).rearrange("p (g n) -> p g n", g=4),
                                                               func=AF.Copy),
                     reads=[bankB[bi]], writes=[pooledB[s]])
            for g in range(4):
                bi = 6 + (g % 2)
                P.op("pe", lambda e, g=g, bi=bi: e.matmul(bank(bi), wpool[:, g * 128:(g + 1) * 128], pT3[:, g, :],
                                                          start=True, stop=True),
                     reads=pooledB + [constB], writes=[bankB[bi]])
                P.op("act", lambda e, g=g, bi=bi: e.activation(out=mixT3[:, 4 + g, :], in_=bank(bi),
                                                               func=AF.Identity, scale=pscale[:, g:g + 1]),
                     reads=[bankB[bi], constB], writes=[xbmB])

        def phaseT(t):
            qbk = [2, 3, 4, 5]
            kbk = [6, 7, 0, 1]

            def tr(j, s, bi):
                def f(e):
                    ins = None
                    for h in range(4):
                        ins = e.transpose(bank_bf(bi)[:, h * 128:(h + 1) * 128],
                                          qkb[:, (j * S + s) * 512 + h * 128:(j * S + s) * 512 + (h + 1) * 128],
                                          ident[:, :])
                    return ins
                P.op("pe", f, reads=[qkbB[j][s], identB], writes=[bankB[bi]])
            for s in range(S):
                bi = qbk[s]
                tr(0, s, bi)
                P.op("dve", lambda e, s=s, bi=bi: e.tensor_tensor(
                    out=qT3[:, :, s * 128:(s + 1) * 128],
                    in0=bank_bf(bi)[:, 0:512].rearrange("p (h n) -> p h n", h=4),
                    in1=qdec[:, :].rearrange("p (h n) -> p h n", h=4), op=ALU.mult),
                    reads=[bankB[bi], constB], writes=[qTB[s]])
            for s in range(S):
                bi = kbk[s]
                tr(1, s, bi)
                P.op("act", lambda e, s=s, bi=bi: e.activation(
                    out=kT3[:, :, s * 128:(s + 1) * 128],
                    in_=bank_bf(bi)[:, 0:512].rearrange("p (h n) -> p h n", h=4), func=AF.Copy),
                    reads=[bankB[bi]], writes=[kTB[s]])

        def phaseR(t):
            ub = [4, 5, 0, 1]
            sbk = [2, 3, 6, 7]
            rbk = [4, 5, 0, 1]
            for s in range(S):
                gs = t * S + s
                bi = sbk[s]

                def f(e, s=s, bi=bi):
                    ins = None
                    for h in range(4):
                        ins = e.matmul(bank(bi)[:, h * 128:(h + 1) * 128], kT3[:, h, s * 128:(s + 1) * 128],
                                       qT3[:, h, s * 128:(s + 1) * 128], start=True, stop=True)
                    return ins
                P.op("pe", f, reads=[kTB[s], qTB[s]], writes=[bankB[bi]])
                bu = ub[s]

                def fu(e, s=s, bu=bu):
                    ins = None
                    for h in range(4):
                        ins = e.matmul(bank(bu)[:, h * 128:(h + 1) * 128],
                                       qkb[:, (S + s) * 512 + h * 128:(S + s) * 512 + (h + 1) * 128],
                                       vd[:, s * 512 + h * 128:s * 512 + (h + 1) * 128], start=True, stop=True)
                    return ins
                P.op("pe", fu, reads=[qkbB[1][s], vdB[s]], writes=[bankB[bu]])
                P.op("dve", lambda e, s=s, bi=bi: e.tensor_tensor(out=ST[s][:, :], in0=bank(bi), in1=mask[:, :],
                                                                  op=ALU.mult),
                     reads=[bankB[bi], constB], writes=[STB[s]])
                snew, sold = state2[gs % 2], state2[(gs - 1) % 2]

                def fs(e, bu=bu, snew=snew, sold=sold):
                    ins = None
                    for h in range(4):
                        ins = e.scalar_tensor_tensor(out=snew[:, h * 128:(h + 1) * 128],
                                                     in0=sold[:, h * 128:(h + 1) * 128], scalar=cds[h],
                                                     in1=bank(bu)[:, h * 128:(h + 1) * 128],
                                                     op0=ALU.mult, op1=ALU.add)
                    return ins
                P.op("dve", fs, reads=[bankB[bu], state2B[(gs - 1) % 2]], writes=[state2B[gs % 2]])
                sbi = gs % 5
                P.op("act", lambda e, sbi=sbi, snew=snew: e.activation(out=stateb[sbi][:, :], in_=snew[:, :],
                                                                       func=AF.Copy),
                     reads=[state2B[gs % 2]], writes=[statebB[sbi]])
            for s in range(S):
                gs = t * S + s
                bi = rbk[s]
                prevb = stateb[(gs - 1) % 5]
                prevB = statebB[(gs - 1) % 5]

                def f(e, s=s, gs=gs, bi=bi, prevb=prevb):
                    ins = None
                    for h in range(4):
                        o = bank(bi)[:, h * 128:(h + 1) * 128]
                        ins = e.matmul(o, ST[s][:, h * 128:(h + 1) * 128],
                                       vd[:, s * 512 + h * 128:s * 512 + (h + 1) * 128],
                                       start=True, stop=(gs == 0))
                        if gs > 0:
                            ins = e.matmul(o, qT3[:, h, s * 128:(s + 1) * 128], prevb[:, h * 128:(h + 1) * 128],
                                           start=False, stop=True)
                    return ins
                P.op("pe", f, reads=[STB[s], vdB[s], qTB[s], prevB], writes=[bankB[bi]])

            def rms_sq(s):
                bi = rbk[s]

                def fq(e):
                    ins = None
                    for h in range(4):
                        ins = e.activation(out=Hh[0][:, h * 128:(h + 1) * 128], in_=bank(bi)[:, h * 128:(h + 1) * 128],
                                           func=AF.Square, accum_out=ss[s][:, h:h + 1])
                    return ins
                P.op("act", fq, reads=[bankB[bi]], writes=[ssB[s], HB[0]])

            def rms_rstd(s):
                P.op("dve", lambda e: e.tensor_scalar(
                    out=rstd[s][:, :], in0=ss[s][:, :], scalar1=1.0 / 128.0, scalar2=RMS_EPS,
                    op0=ALU.mult, op1=ALU.add), reads=[ssB[s]], writes=[rstdB[s]])
                P.op("pool", lambda e: e.tensor_tensor(
                    out=rstd[s][:, :], in0=rstd[s][:, :], in1=mhalf[:, :], op=ALU.pow),
                    reads=[], writes=[rstdB[s]])

            def rms_gate(s):
                bi = rbk[s]

                def fr(e):
                    ins = None
                    for h in range(4):
                        ins = e.scalar_tensor_tensor(out=retn[s][:, h * 128:(h + 1) * 128],
                                                     in0=bank(bi)[:, h * 128:(h + 1) * 128],
                                                     scalar=rstd[s][:, h:h + 1],
                                                     in1=sg[:, s * 512 + h * 128:s * 512 + (h + 1) * 128],
                                                     op0=ALU.mult, op1=ALU.mult)
                    return ins
                P.op("dve", fr, reads=[bankB[bi], rstdB[s], sgB[s]], writes=[retnB[s]])
                bt = 2 + (s % 2)

                def ft(e):
                    ins = None
                    for h in range(4):
                        ins = e.transpose(bank_bf(bt)[:, h * 128:(h + 1) * 128], retn[s][:, h * 128:(h + 1) * 128],
                                          ident[:, :])
                    return ins
                P.op("pe", ft, reads=[retnB[s], identB], writes=[bankB[bt]])

            def rms_evac(s):
                bt = 2 + (s % 2)
                P.op("act", lambda e: e.activation(
                    out=mixT3[:, 0:4, s * 128:(s + 1) * 128],
                    in_=bank_bf(bt)[:, 0:512].rearrange("p (h n) -> p h n", h=4), func=AF.Copy),
                    reads=[bankB[bt]], writes=[xbmB])
            for s in range(S):
                rms_sq(s)
            rms_rstd(0)
            rms_rstd(1)
            rms_gate(0)
            rms_rstd(2)
            rms_evac(0)
            rms_gate(1)
            rms_rstd(3)
            rms_evac(1)
            rms_gate(2)
            rms_evac(2)
            rms_gate(3)
            rms_evac(3)

        def phaseW(t):
            r0, r0B = slot(t, 5)
            r1, r1B = slot(t, 6)
            wo = [r0[:, :].rearrange("p (k n) -> p k n", k=KD), r1[:, :].rearrange("p (k n) -> p k n", k=KD)]
            st = {}

            def stage1(s):
                qi = wtiles[s % 3]

                def f(e):
                    ins = None
                    for hh in range(2):
                        for k in range(KD):
                            ins = e.matmul(Q[qi][:, hh * 512:(hh + 1) * 512], mixT3[:, k, s * 128:(s + 1) * 128],
                                           wo[hh][:, k, :], start=(k == 0), stop=(k == KD - 1))
                    return ins
                P.op("pe", f, reads=[xbmB, r0B, r1B], writes=[bankB[2 * qi], bankB[2 * qi + 1]])
                yi = rr["y"] % 4
                rr["y"] += 1
                li = rr["ln"] % 4
                rr["ln"] += 1
                st[s] = (yi, li)
                Xs = X[:, s * D:(s + 1) * D]
                P.op("dve", lambda e: e.scalar_tensor_tensor(
                    out=Y[yi][:, :], in0=Xs, scalar=ALPHA, in1=Q[qi][:, :], op0=ALU.mult, op1=ALU.add),
                    reads=[XB[s], bankB[2 * qi], bankB[2 * qi + 1]], writes=[YB[yi]])
                ln_stats(yi, li)

            def ynb(s):
                return aT[:, s * 1024:(s + 1) * 1024]

            def stage1b(s):
                yi, li = st[s]
                ln_rstd(li)
                P.op("act", lambda e: e.activation(out=ynb(s), in_=Y[yi][:, :], func=AF.Identity,
                                                   bias=lnmv[li][:, 3:4], scale=lnmv[li][:, 2:3]),
                     reads=[lnB[li], YB[yi]], writes=[aTB[2 * s], aTB[2 * s + 1]])

            def stage2(s):
                yi, li = st[s]
                Xs = X[:, s * D:(s + 1) * D]
                P.op("dve", lambda e: e.tensor_scalar(out=Xs, in0=Y[yi][:, :], scalar1=lnmv[li][:, 2:3],
                                                      scalar2=lnmv[li][:, 3:4], op0=ALU.mult, op1=ALU.add),
                     reads=[YB[yi], lnB[li]], writes=[XB[s]])
                P.op("dve", lambda e: e.tensor_tensor(out=Xs, in0=Xs, in1=g1t[:, :], op=ALU.mult),
                     reads=[constB], writes=[XB[s]])
                P.op("dve", lambda e: e.tensor_tensor(out=Xs, in0=Xs, in1=b1t[:, :], op=ALU.add),
                     reads=[constB], writes=[XB[s]])

            def stage3(s):
                yi, li = st[s]
                qi = 1 if s % 2 == 0 else 2
                ba, bb = 2 * qi, 2 * qi + 1

                def f(e):
                    ins = None
                    for k in range(KD):
                        ob = bank_bf(ba) if k < 4 else bank_bf(bb)
                        ins = e.transpose(ob[:, (k % 4) * 128:(k % 4 + 1) * 128], ynb(s)[:, k * 128:(k + 1) * 128],
                                          ident[:, :])
                    return ins
                P.op("pe", f, reads=[aTB[2 * s], aTB[2 * s + 1], identB], writes=[bankB[ba], bankB[bb]])

                def fa(e):
                    ins = None
                    for k in range(0, 4):
                        ins = e.activation(out=XT3[:, k, s * 128:(s + 1) * 128],
                                           in_=bank_bf(ba)[:, k * 128:(k + 1) * 128],
                                           func=AF.Identity, bias=b1p[:, k:k + 1], scale=g1p[:, k:k + 1])
                    return ins
                P.op("act", fa, reads=[bankB[ba], constB], writes=[XTB[s]])

                def fd(e):
                    ins = None
                    for k in range(4, 8):
                        ins = e.tensor_scalar(out=XT3[:, k, s * 128:(s + 1) * 128],
                                              in0=bank_bf(bb)[:, (k - 4) * 128:(k - 3) * 128],
                                              scalar1=g1p[:, k:k + 1], scalar2=b1p[:, k:k + 1],
                                              op0=ALU.mult, op1=ALU.add)
                    return ins
                P.op("dve", fd, reads=[bankB[bb], constB], writes=[XTB2[s]])

            stage1(0)
            stage1(1)
            stage1b(0)
            stage1(2)
            stage1b(1)
            stage1(3)
            stage1b(2)
            stage1b(3)
            for s in range(S):
                stage3(s)
            issue_slot_load()
            issue_slot_load()
            return [(lambda s=s: stage2(s)) for s in range(S)]

        def phaseU(t, deferred):
            pend = []
            for u in range(11):
                rt, rB = slot(t, 7 + u)
                r3 = rt[:, :].rearrange("p (k n) -> p k n", k=KD)
                for cl in range(2):
                    c = 2 * u + cl
                    vb = (0, 1, 4, 5)[rr["u"] % 4]
                    gb = (6, 7)[rr["u"] % 2]
                    rr["u"] += 1

                    def f(e, vb=vb, gb=gb, cl=cl, r3=r3):
                        ins = None
                        for part in range(2):
                            col = part * 256 + cl * 128
                            ob = bank(vb) if part == 0 else bank(gb)
                            for k in range(KD):
                                ins = e.matmul(ob, r3[:, k, col:col + 128],
                                               XT3[:, k, :], start=(k == 0), stop=(k == KD - 1))
                        return ins
                    P.op("pe", f, reads=XTB + XTB2 + [rB], writes=[bankB[vb], bankB[gb]])
                    fi = rr["ffn"] % 3
                    rr["ffn"] += 1
                    Vp = bank(vb)
                    Gp = bank(gb)
                    def fg(e, fi=fi, c=c, Gp=Gp):
                        e.activation(out=G[fi][:, 0:2], in_=carry[:, 2 * c:2 * c + 2], func=AF.Copy)
                        return e.activation(out=G[fi][:, 2:514], in_=Gp, func=AF.Copy)
                    P.op("act", fg, reads=[carryB[c], bankB[gb]], writes=[GB[fi]])
                    P.op("act", lambda e, c=c, Gp=Gp: e.activation(out=carry[:, 2 * c:2 * c + 2],
                                                                  in_=Gp[:, 510:512], func=AF.Copy),
                         reads=[bankB[gb]], writes=[carryB[c]])
                    P.op("act", lambda e, fi=fi, Gp=Gp, c=c: e.activation(
                        out=Hh[fi][:, :], in_=Gp, func=AF.Identity, bias=cb[:, c:c + 1], scale=cw3[:, c, 2:3]),
                        reads=[bankB[gb], constB], writes=[HB[fi]])
                    P.op("dve", lambda e, fi=fi, c=c: e.scalar_tensor_tensor(
                        out=Hh[fi][:, :], in0=G[fi][:, 1:513], scalar=cw3[:, c, 1:2], in1=Hh[fi][:, :],
                        op0=ALU.mult, op1=ALU.add),
                        reads=[GB[fi], constB], writes=[HB[fi]])
                    P.op("dve", lambda e, fi=fi, c=c: e.scalar_tensor_tensor(
                        out=Hh[fi][:, :], in0=G[fi][:, 0:512], scalar=cw3[:, c, 0:1], in1=Hh[fi][:, :],
                        op0=ALU.mult, op1=ALU.add),
                        reads=[GB[fi], constB], writes=[HB[fi]])
                    if pend:
                        pend.pop(0)()

                    def tail(fi=fi, c=c, Vp=Vp, vb=vb):
                        P.op("act", lambda e: e.activation(out=Hh[fi][:, :], in_=Hh[fi][:, :], func=AF.Silu),
                             reads=[], writes=[HB[fi]])
                        P.op("dve", lambda e: e.tensor_tensor(
                            out=aT[:, c * TT:(c + 1) * TT], in0=Hh[fi][:, :], in1=Vp, op=ALU.mult),
                            reads=[HB[fi], bankB[vb]], writes=[aTB[c]])
                    pend.append(tail)
                    if deferred and c in (3, 7, 11, 15):
                        deferred.pop(0)()
                issue_slot_load()
                if t + 1 < NT and 4 <= u < 8:
                    phaseX_group(t + 1, u - 4)
            while pend:
                pend.pop(0)()

        def phaseD(t):
            for v in range(6):
                rt, rB = slot(t, 18 + v)
                nch = 4 if v < 5 else 2
                r3 = rt[:, 0:nch * 1024].rearrange("p (c n) -> p c n", c=nch)
                for s in (1, 3, 2, 0):
                    def f(e, v=v, nch=nch, r3=r3, s=s):
                        ins = None
                        for hh in range(2):
                            for cl in range(nch):
                                c = 4 * v + cl
                                ins = e.matmul(Q[s][:, hh * 512:(hh + 1) * 512],
                                               aT[:, c * TT + s * 128:c * TT + (s + 1) * 128],
                                               r3[:, cl, hh * 512:(hh + 1) * 512],
                                               start=(c == 0), stop=(c == NF - 1))
                        return ins
                    P.op("pe", f, reads=aTB[4 * v:4 * v + nch] + [rB], writes=[bankB[2 * s], bankB[2 * s + 1]])
                issue_slot_load()

        def phaseL2(t, filler):
            st = {}

            def stage1(s):
                yi = rr["y"] % 4
                rr["y"] += 1
                li = rr["ln"] % 4
                rr["ln"] += 1
                st[s] = (yi, li)
                Xs = X[:, s * D:(s + 1) * D]
                P.op("dve", lambda e: e.scalar_tensor_tensor(
                    out=Y[yi][:, :], in0=Xs, scalar=ALPHA, in1=Q[s][:, :], op0=ALU.mult, op1=ALU.add),
                    reads=[XB[s], bankB[2 * s], bankB[2 * s + 1]], writes=[YB[yi]])
                ln_stats(yi, li)

            def stage1b(s):
                yi, li = st[s]
                ln_fin(yi, li)

            def stage2(s):
                yi, li = st[s]
                P.op("dve", lambda e: e.tensor_tensor(out=Y[yi][:, :], in0=Y[yi][:, :], in1=g2t[:, :], op=ALU.mult),
                     reads=[constB], writes=[YB[yi]])
                P.op("dve" if t == NT - 1 else "pool",
                     lambda e: e.tensor_tensor(out=Y[yi][:, :], in0=Y[yi][:, :], in1=b2t[:, :], op=ALU.add),
                     reads=[constB], writes=[YB[yi]])
                r0w = t * TT + s * 128
                out_toks.append(P.dma("sp", out_d[r0w:r0w + 128, :], Y[yi][:, :], "Y%d" % yi, reads=[YB[yi]]))
            stage1(0)
            filler(2)
            stage1b(0)
            stage1(1)
            filler(2)
            stage1b(1)
            stage2(0)
            stage1(2)
            filler(2)
            stage1b(2)
            stage2(1)
            stage1(3)
            filler(2)
            stage1b(3)
            stage2(2)
            filler(2)
            stage2(3)
            filler(100)

        def proj_items(t):
            items = []
            for sid in (0, 1, 2):
                for s in range(S):
                    items.append(lambda sid=sid, s=s: proj_group(t, sid, s))
                items.append(None)
            return items

        def make_filler(items):
            def filler(n):
                k = 0
                while items and k < n:
                    it = items.pop(0)
                    if it is None:
                        issue_slot_load()
                    else:
                        it()
                        k += 1
                while items and items[0] is None:
                    items.pop(0)
                    issue_slot_load()
            return filler

        phaseX(0)
        make_filler(proj_items(0))(100)
        for t in range(NT):
            if t + 1 < NT:
                csload(t + 1)
            for sid in (3, 4):
                for s in range(S):
                    proj_group(t, sid, s)
                issue_slot_load()
            phaseT(t)
            phaseO(t)
            phaseR(t)
            dbg("mixT", lambda: xbm[:, :], [128, 4096], [xbmB], BF16)
            deferred = phaseW(t)
            if t + 1 < NT:
                xload_bf(t + 1)
            phaseU(t, deferred)
            dbg("x1", lambda: X[:, :], [128, S * D], XB)
            dbg("aT", lambda: aT[:, :], [128, NF * TT], aTB, BF16)
            phaseD(t)
            phaseL2(t, make_filler(proj_items(t + 1) if t + 1 < NT else []))
            if t + 1 < NT:
                xload_f32(t + 1)
        P.wait_all("sp", out_toks)
        P.emit()
    return nc, list(dbg_d.keys())


def make_inputs(x_b, w):
    consts, _ = host_consts()
    m = {
        "x": np.ascontiguousarray(x_b, dtype=np.float32),
        "w_in": np.ascontiguousarray(w["w_in"][0]),
        "w_pool": np.ascontiguousarray(w["w_pool"][0]),
        "w_out": np.ascontiguousarray(w["w_out"][0]),
        "w_up": np.ascontiguousarray(w["w_up"][0]),
        "w_down": np.ascontiguousarray(w["w_down"][0]),
        "pscale": np.ascontiguousarray(w["pool_scale"][0].reshape(4, 128).T),
        "g1t": np.ascontiguousarray(np.broadcast_to(w["ln1_g"][0][None, :], (128, D))),
        "b1t": np.ascontiguousarray(np.broadcast_to(w["ln1_b"][0][None, :], (128, D))),
        "g2t": np.ascontiguousarray(np.broadcast_to(w["ln2_g"][0][None, :], (128, D))),
        "b2t": np.ascontiguousarray(np.broadcast_to(w["ln2_b"][0][None, :], (128, D))),
        "cw": np.ascontiguousarray(w["conv_w"][0].reshape(3, NF, 128).transpose(2, 1, 0).reshape(128, NF * 3)),
        "cb": np.ascontiguousarray(w["conv_b"][0].reshape(NF, 128).T),
        "rope": consts["rope"],
        "qdec": consts["qdec"],
        "vdec": consts["vdec"],
        "mask": consts["mask"],
        "bands": np.ascontiguousarray(consts["bands"].reshape(128, 12 * 128)),
        "ident": consts["ident"],
        "g1p": np.ascontiguousarray(w["ln1_g"][0].reshape(KD, 128).T),
        "b1p": np.ascontiguousarray(w["ln1_b"][0].reshape(KD, 128).T),
    }
    return m


def kernel(x, w_in, w_pool, pool_scale, w_out, ln1_g, ln1_b, w_up, conv_w, conv_b, w_down, ln2_g, ln2_b):
    w = dict(w_in=np.asarray(w_in, np.float32), w_pool=np.asarray(w_pool, np.float32),
             pool_scale=np.asarray(pool_scale, np.float32), w_out=np.asarray(w_out, np.float32),
             ln1_g=np.asarray(ln1_g, np.float32), ln1_b=np.asarray(ln1_b, np.float32),
             w_up=np.asarray(w_up, np.float32), conv_w=np.asarray(conv_w, np.float32),
             conv_b=np.asarray(conv_b, np.float32), w_down=np.asarray(w_down, np.float32),
             ln2_g=np.asarray(ln2_g, np.float32), ln2_b=np.asarray(ln2_b, np.float32))
    x = np.asarray(x, np.float32)
    nb, T, _ = x.shape
    nc, _ = build(T // TT)
    in_maps = [make_inputs(x[b], w) for b in range(nb)]
    res = run_bass_kernel_spmd(nc, in_maps, core_ids=list(range(nb)))
    return np.stack([np.asarray(r["out"], dtype=np.float32) for r in res.results], axis=0)
```

```python
import contextlib
import numpy as np
import ml_dtypes
import concourse.bass as bass
import concourse.mybir as mybir
from concourse.bass_utils import run_bass_kernel_spmd

F32 = mybir.dt.float32
BF16 = mybir.dt.bfloat16
ALU = mybir.AluOpType
AF = mybir.ActivationFunctionType

D = 1024
KD = 8
TT = 512
S = 4
DFF = 2816
NF = 22
NSLOT = 24
RING = 3
PROJ_ORDER = (2, 3, 4, 0, 1)
ALPHA = 2.0 ** 0.25
LN_EPS = 1e-5
RMS_EPS = 1e-6
SELF_SYNC = True


class Buf:
    def __init__(self, name, excl=False):
        self.name = name
        self.w = []
        self.r = []
        self.excl = excl


class Prog:
    ENG = ("pe", "act", "dve", "pool", "sp")

    def __init__(self, nc, es):
        self.nc = nc
        self.es = es
        self.streams = {e: [] for e in self.ENG}
        self.cnt = {e: 0 for e in self.ENG}
        self.seen = {e: {} for e in self.ENG}
        self.sems = {}
        self.dcnt = {}
        for e in self.ENG:
            self.sems[e] = es.enter_context(nc.semaphore("s_" + e))

    def dsem(self, key):
        if key not in self.sems:
            self.sems[key] = self.es.enter_context(self.nc.semaphore("d_" + key))
            self.dcnt[key] = 0
        return self.sems[key]

    def _deps(self, eng, reads, writes):
        toks = []
        for b in reads:
            toks += b.w
            if b.excl:
                toks += b.r
        for b in writes:
            toks += b.w
            toks += b.r
        need = {}
        for (k, v) in toks:
            if k == eng and (not SELF_SYNC or eng == "pe"):
                continue
            if self.seen[eng].get(k, 0) >= v:
                continue
            if need.get(k, 0) < v:
                need[k] = v
        for k, v in need.items():
            self.seen[eng][k] = v
            self.streams[eng].append(("w", k, v))

    def _commit(self, tok, reads, writes):
        for b in reads:
            if b not in writes:
                b.r.append(tok)
                if len(b.r) > 64:
                    best = {}
                    for (k, v) in b.r:
                        if best.get(k, 0) < v:
                            best[k] = v
                    b.r = list(best.items())
        for b in writes:
            b.w = [tok]
            b.r = []

    def op(self, eng, fn, reads=(), writes=()):
        reads = list(reads)
        writes = list(writes)
        self._deps(eng, reads, writes)
        self.streams[eng].append(("op", fn))
        self.cnt[eng] += 1
        tok = (eng, self.cnt[eng])
        self._commit(tok, reads, writes)
        return tok

    def dma(self, eng, out, in_, key, reads=(), writes=()):
        reads = list(reads)
        writes = list(writes)
        self.dsem(key)
        self._deps(eng, reads, writes)
        self.streams[eng].append(("dma", out, in_, key))
        self.dcnt[key] += 16
        tok = (key, self.dcnt[key])
        self._commit(tok, reads, writes)
        return tok

    def wait_all(self, eng, toks):
        need = {}
        for (k, v) in toks:
            if need.get(k, 0) < v:
                need[k] = v
        for k, v in need.items():
            self.streams[eng].append(("w", k, v))

    def emit(self):
        nc = self.nc

        def mk(engname):
            def body(e):
                for ent in self.streams[engname]:
                    if ent[0] == "w":
                        e.wait_ge(self.sems[ent[1]], ent[2])
                    elif ent[0] == "op":
                        ins = ent[1](e)
                        ins.then_inc(self.sems[engname], 1)
                    else:
                        e.dma_start(out=ent[1], in_=ent[2]).then_inc(self.sems[ent[3]], 16)
            return body

        with nc.Block() as block:
            block.tensor(mk("pe"))
            block.scalar(mk("act"))
            block.vector(mk("dve"))
            block.gpsimd(mk("pool"))
            block.sync(mk("sp"))


def host_consts():
    dh = 128
    B = 128
    inv_freq = (10000.0 ** (-np.arange(0, dh, 2, dtype=np.float32) / np.float32(dh))).astype(np.float32)
    ang = (np.arange(4096, dtype=np.float32)[:, None] * inv_freq[None, :]).astype(np.float32)
    rope = np.concatenate([np.cos(ang), np.sin(ang)], axis=1).astype(np.float32)
    gam = 1.0 - 2.0 ** (-5.0 - np.arange(4, dtype=np.float64))
    i = np.arange(B, dtype=np.float64)
    qdec = np.zeros((128, 512), np.float32)
    vdec = np.zeros((128, 4), np.float32)
    mask = np.zeros((128, 512), np.float32)
    cds = []
    for h in range(4):
        g = gam[h]
        qd = g ** (i + 1.0) * dh ** -0.5
        qdec[:, h * 128:(h + 1) * 128] = qd[None, :]
        vd = g ** (B - 1.0 - i)
        vdec[:, h] = vd
        jj = i[:, None]
        ii = i[None, :]
        W = np.where((jj // 64) <= (ii // 64), g ** np.abs(ii - jj), 0.0)
        M = W / (g ** (ii + 1.0) * g ** (B - 1.0 - jj))
        mask[:, h * 128:(h + 1) * 128] = M
        cds.append(float(g ** B))
    bands = np.zeros((128, 12, 128), np.float32)
    m = np.arange(128)[:, None]
    t = np.arange(128)[None, :]
    for gi, w in enumerate((2, 4, 8, 16)):
        cur = np.where((m <= t) & (m > t - w), 1.0 / w, 0.0) - (m == t)
        prev = np.where((m - 128 <= t) & (m - 128 > t - w), 1.0 / w, 0.0)
        div = np.minimum(t + 1.0, float(w))
        cur0 = np.where((m <= t) & (m > t - w), 1.0 / div, 0.0) - (m == t)
        bands[:, gi, :] = prev
        bands[:, 4 + gi, :] = cur
        bands[:, 8 + gi, :] = cur0
    ident = np.eye(128, dtype=np.float32)
    return dict(rope=rope, qdec=qdec, vdec=vdec, mask=mask,
                bands=bands.astype(ml_dtypes.bfloat16), ident=ident.astype(ml_dtypes.bfloat16)), cds


def build(NT, debug=(), stop=None):
    consts, cds = host_consts()
    T = NT * TT
    nc = bass.Bass("TRN2", target_bir_lowering=False)
    dt_in = lambda name, shape, dt=F32: nc.dram_tensor(name, list(shape), dt, kind="ExternalInput").ap()
    x_d = dt_in("x", [T, D])
    w_in_d = dt_in("w_in", [D, 2560])
    w_pool_d = dt_in("w_pool", [4, 128, 128])
    w_out_d = dt_in("w_out", [D, D])
    w_up_d = dt_in("w_up", [D, 2 * DFF])
    w_down_d = dt_in("w_down", [DFF, D])
    pscale_d = dt_in("pscale", [128, 4])
    g1_d = dt_in("g1t", [128, D])
    b1_d = dt_in("b1t", [128, D])
    g2_d = dt_in("g2t", [128, D])
    b2_d = dt_in("b2t", [128, D])
    cw_d = dt_in("cw", [128, NF * 3])
    cb_d = dt_in("cb", [128, NF])
    rope_d = dt_in("rope", [4096, 128])
    qdec_d = dt_in("qdec", [128, 512])
    vdec_d = dt_in("vdec", [128, 4])
    mask_d = dt_in("mask", [128, 512])
    bands_d = dt_in("bands", [128, 12 * 128], BF16)
    g1p_d = dt_in("g1p", [128, KD])
    b1p_d = dt_in("b1p", [128, KD])
    ident_d = dt_in("ident", [128, 128], BF16)
    out_d = nc.dram_tensor("out", [T, D], F32, kind="ExternalOutput").ap()
    scr_d = nc.dram_tensor("wscr", [NSLOT, 128, 4096], BF16, kind="Internal").ap()
    dbg_d = {}

    es = contextlib.ExitStack()
    with es:
        P = Prog(nc, es)

        def sb(name, shape, dt=F32):
            return es.enter_context(nc.sbuf_tensor("sb_" + name, list(shape), dt))

        def ps(name, shape, dt=F32):
            return es.enter_context(nc.psum_tensor("ps_" + name, list(shape), dt))

        ring = [sb("ring%d" % i, [128, 4096], BF16) for i in range(RING)]
        ringB = [Buf("ring%d" % i) for i in range(RING)]
        scrB = [Buf("scr%d" % i) for i in range(NSLOT)]
        ident = sb("ident", [128, 128], BF16)
        qdec = sb("qdec", [128, 512])
        vdec = sb("vdec", [128, 4])
        mask = sb("mask", [128, 512])
        bands = sb("bands", [128, 12 * 128], BF16)
        wpool = sb("wpool", [128, 512], BF16)
        pscale = sb("pscale", [128, 4])
        g1t = sb("g1t", [128, D])
        b1t = sb("b1t", [128, D])
        g2t = sb("g2t", [128, D])
        b2t = sb("b2t", [128, D])
        cw = sb("cw", [128, NF * 3])
        cb = sb("cb", [128, NF])
        g1p = sb("g1p", [128, KD])
        b1p = sb("b1p", [128, KD])
        identB = Buf("identc")
        constB = Buf("const")
        cs = [sb("cs%d" % i, [128, S * 128]) for i in range(2)]
        csB = [Buf("cs%d" % i) for i in range(2)]
        X = sb("X", [128, S * D])
        XB = [Buf("X%d" % s) for s in range(S)]
        xbm = sb("xbm", [128, 4096], BF16)
        xbmB = Buf("xbm")
        XT = sb("XT", [128, KD * TT], BF16)
        XTB = [Buf("XT%d" % s) for s in range(S)]
        XTB2 = [Buf("XTh%d" % s) for s in range(S)]
        XA = sb("XA", [128, KD * TT], BF16)
        XAB = [Buf("XA%d" % s) for s in range(S)]
        qkb = sb("qkb", [128, 2 * S * 512], BF16)
        qkbB = [[Buf("qkb%d_%d" % (j, s)) for s in range(S)] for j in range(2)]
        qT = sb("qT", [128, 4 * TT], BF16)
        kT = sb("kT", [128, 4 * TT], BF16)
        qTB = [Buf("qT%d" % s) for s in range(S)]
        kTB = [Buf("kT%d" % s) for s in range(S)]
        vd = sb("vd", [128, S * 512], BF16)
        vdB = [Buf("vd%d" % s) for s in range(S)]
        sg = sb("sg", [128, S * 512])
        sgB = [Buf("sg%d" % s) for s in range(S)]
        pb = sb("pb", [128, 5 * 512], BF16)
        pbB = [Buf("pb%d" % i) for i in range(5)]
        ropeA = [sb("ropeA%d" % i, [128, 512]) for i in range(2)]
        ropeBt = [sb("ropeB%d" % i, [128, 512]) for i in range(2)]
        ropeAB = [Buf("ropeA%d" % i) for i in range(2)]
        ropeBB = [Buf("ropeB%d" % i) for i in range(2)]
        ST = [sb("ST%d" % i, [128, 512], BF16) for i in range(4)]
        STB = [Buf("ST%d" % i) for i in range(4)]
        state2 = [sb("state%d" % i, [128, 512]) for i in range(2)]
        state2B = [Buf("state%d" % i) for i in range(2)]
        stateb = [sb("stateb%d" % i, [128, 512], BF16) for i in range(5)]
        statebB = [Buf("stateb%d" % i) for i in range(5)]
        ss = [sb("ss%d" % i, [128, 4]) for i in range(4)]
        rstd = [sb("rstd%d" % i, [128, 4]) for i in range(4)]
        ssB = [Buf("ss%d" % i) for i in range(4)]
        rstdB = [Buf("rstd%d" % i) for i in range(4)]
        retn = [sb("retn%d" % i, [128, 512], BF16) for i in range(4)]
        retnB = [Buf("retn%d" % i) for i in range(4)]
        pooledT = sb("pooledT", [128, 4 * TT], BF16)
        pooledB = [Buf("pooled%d" % s) for s in range(S)]
        Y = [sb("Y%d" % i, [128, D]) for i in range(4)]
        YB = [Buf("Y%d" % i) for i in range(4)]
        lnst = [sb("lnst%d" % i, [128, 12]) for i in range(4)]
        lnmv = [sb("lnmv%d" % i, [128, 4]) for i in range(4)]
        lnB = [Buf("ln%d" % i) for i in range(4)]
        G = [sb("G%d" % i, [128, 514]) for i in range(3)]
        Hh = [sb("H%d" % i, [128, 512]) for i in range(3)]
        GB = [Buf("G%d" % i) for i in range(3)]
        HB = [Buf("H%d" % i) for i in range(3)]
        carry = sb("carry", [128, NF * 2])
        carryB = [Buf("carry%d" % c) for c in range(NF)]
        aT = sb("aT", [128, NF * TT], BF16)
        aTB = [Buf("aT%d" % c) for c in range(NF)]
        Q = [ps("Q%d" % i, [128, 1024]) for i in range(4)]
        bankB = [Buf("bank%d" % i, excl=True) for i in range(8)]

        def bank(i):
            return Q[i // 2][:, (i % 2) * 512:(i % 2) * 512 + 512]

        def bank_bf(i):
            return bank(i).bitcast(BF16)

        P.dma("sp", ident[:, :], ident_d, "identc", writes=[identB])
        ctoks = []
        for (dst, src) in ((qdec, qdec_d), (vdec, vdec_d), (mask, mask_d), (bands, bands_d),
                           (pscale, pscale_d), (g1t, g1_d), (b1t, b1_d), (g2t, g2_d), (b2t, b2_d),
                           (cw, cw_d), (cb, cb_d), (g1p, g1p_d), (b1p, b1p_d)):
            ctoks.append(P.dma("sp", dst[:, :], src, "const", writes=[Buf("c_" + dst.name)]))
        mhalf = sb("mhalf", [128, 4])
        P.op("pool", lambda e: e.memset(mhalf[:, :], -0.5), writes=[Buf("mhalf")])
        P.op("pool", lambda e: e.memset(carry[:, :], 0.0), writes=carryB)
        P.op("pool", lambda e: e.memset(state2[1][:, :], 0.0), writes=[state2B[1]])

        def slot_src(sid):
            if sid < 5:
                j = PROJ_ORDER[sid]
                return [(lambda r: r[:, :].rearrange("p (k n) -> p k n", k=KD),
                         w_in_d[:, j * 512:(j + 1) * 512].rearrange("(k p) n -> p k n", p=128))]
            if sid < 7:
                h = sid - 5
                return [(lambda r: r[:, :].rearrange("p (k n) -> p k n", k=KD),
                         w_out_d[:, h * 512:(h + 1) * 512].rearrange("(k p) n -> p k n", p=128))]
            if sid < 18:
                u = sid - 7
                return [
                    (lambda r: r[:, :].rearrange("p (k n) -> p k n", k=KD)[:, :, 0:256],
                     w_up_d[:, u * 256:(u + 1) * 256].rearrange("(k p) n -> p k n", p=128)),
                    (lambda r: r[:, :].rearrange("p (k n) -> p k n", k=KD)[:, :, 256:512],
                     w_up_d[:, DFF + u * 256:DFF + (u + 1) * 256].rearrange("(k p) n -> p k n", p=128)),
                ]
            v = sid - 18
            nch = 4 if v < 5 else 2
            return [(lambda r: r[:, 0:nch * 1024].rearrange("p (c n) -> p c n", c=nch),
                     w_down_d[v * 512:v * 512 + nch * 128, :].rearrange("(c p) n -> p c n", p=128))]

        nseq = NT * NSLOT
        issued = [0]

        def issue_slot_load():
            g = issued[0]
            if g >= nseq:
                return
            issued[0] += 1
            t, sid = divmod(g, NSLOT)
            ri = g % RING
            if t == 0:
                for (dv, src) in slot_src(sid):
                    P.dma("pool", dv(ring[ri]), src, "ringp%d" % ri, writes=[ringB[ri]])
                ncol = 2048 if sid == NSLOT - 1 else 4096
                if NT > 1:
                    P.dma("sp", scr_d[sid][:, 0:ncol], ring[ri][:, 0:ncol], "ringst%d" % ri, reads=[ringB[ri]],
                          writes=[scrB[sid]])
            else:
                ncol = 2048 if sid == NSLOT - 1 else 4096
                P.dma("sp", ring[ri][:, 0:ncol], scr_d[sid][:, 0:ncol], "ring%d" % ri, reads=[scrB[sid]],
                      writes=[ringB[ri]])

        def slot(t, sid):
            g = t * NSLOT + sid
            return ring[g % RING], ringB[g % RING]

        def xload_bf(t):
            P.dma("pool", qkb[:, 0:2048].rearrange("p (s d) -> p s d", s=2),
                  x_d[t * TT:t * TT + 256, :].rearrange("(s p) d -> p s d", p=128), "xbm", writes=qkbB[0])
            P.dma("pool", pooledT[:, :].rearrange("p (s d) -> p s d", s=2),
                  x_d[t * TT + 256:(t + 1) * TT, :].rearrange("(s p) d -> p s d", p=128), "xbm", writes=pooledB)

        def xb_rows(s):
            if s < 2:
                return qkb[:, s * 1024:(s + 1) * 1024], [qkbB[0][2 * s], qkbB[0][2 * s + 1]]
            return pooledT[:, (s - 2) * 1024:(s - 1) * 1024], list(pooledB)

        def xload_f32(t):
            for s in range(S):
                P.dma("sp", X[:, s * D:(s + 1) * D], x_d[t * TT + s * 128:t * TT + (s + 1) * 128, :],
                      "X%d" % s, writes=[XB[s]])

        def csload(t):
            P.dma("sp", cs[t % 2][:, :].rearrange("p (s c) -> p s c", s=S),
                  rope_d[t * TT:(t + 1) * TT, :].rearrange("(s p) c -> p s c", p=128), "cs%d" % (t % 2),
                  writes=[csB[t % 2]])

        def dbg(name, ap_fn, shape, reads, dt=F32):
            if name not in debug:
                return
            if name not in dbg_d:
                dbg_d[name] = nc.dram_tensor("dbg_" + name, list(shape), dt, kind="ExternalOutput").ap()
            tok = P.dma("sp", dbg_d[name], ap_fn(), "dbg_" + name, reads=reads)
            out_toks.append(tok)

        def ln_stats(yi, li):
            def lnstats(e):
                e.bn_stats(out=lnst[li][:, 0:6], in_=Y[yi][:, 0:512])
                return e.bn_stats(out=lnst[li][:, 6:12], in_=Y[yi][:, 512:1024])
            P.op("dve", lnstats, reads=[YB[yi]], writes=[lnB[li]])
            P.op("dve", lambda e: e.bn_aggr(out=lnmv[li][:, 0:2], in_=lnst[li][:, :]), reads=[], writes=[lnB[li]])
            P.op("dve", lambda e: e.tensor_scalar(out=lnmv[li][:, 2:3], in0=lnmv[li][:, 1:2], scalar1=LN_EPS,
                                                  scalar2=1.0, op0=ALU.add, op1=ALU.mult),
                 reads=[], writes=[lnB[li]])

        def ln_rstd(li):
            P.op("pool", lambda e: e.tensor_tensor(out=lnmv[li][:, 2:3], in0=lnmv[li][:, 2:3], in1=mhalf[:, 0:1],
                                                   op=ALU.pow), reads=[], writes=[lnB[li]])
            P.op("dve", lambda e: e.scalar_tensor_tensor(out=lnmv[li][:, 3:4], in0=lnmv[li][:, 0:1], scalar=-1.0,
                                                         in1=lnmv[li][:, 2:3], op0=ALU.mult, op1=ALU.mult),
                 reads=[], writes=[lnB[li]])

        def ln_fin(yi, li):
            ln_rstd(li)
            P.op("act", lambda e: e.activation(out=Y[yi][:, :], in_=Y[yi][:, :], func=AF.Identity,
                                               bias=lnmv[li][:, 3:4], scale=lnmv[li][:, 2:3]),
                 reads=[lnB[li]], writes=[YB[yi]])

        out_toks = []
        xload_bf(0)
        ctoks.append(P.dma("pool", wpool[:, :].rearrange("c (g d) -> c g d", g=4),
                           w_pool_d.rearrange("g c d -> c g d"), "constp", writes=[Buf("c_wpool")]))
        constB.w = [ctoks[-2], ctoks[-1]]
        for _ in range(RING):
            issue_slot_load()
        xload_f32(0)
        csload(0)
        rr = dict(pe=0, ffn=0, y=0, ln=0, u=0)
        XT3 = XT[:, :].rearrange("p (k n) -> p k n", k=KD)
        mixT3 = xbm[:, :].rearrange("p (k n) -> p k n", k=KD)
        qT3 = qT[:, :].rearrange("p (h n) -> p h n", h=4)
        kT3 = kT[:, :].rearrange("p (h n) -> p h n", h=4)
        pT3 = pooledT[:, :].rearrange("p (g n) -> p g n", g=4)
        cw3 = cw[:, :].rearrange("p (c j) -> p c j", j=3)
        wtiles = [0, 2, 3]

        XA3 = XA[:, :].rearrange("p (k n) -> p k n", k=KD)

        def phaseX_group(t, s, banks=(2, 3), evac=None):
            bi = banks[s % 2]
            src, srcB = xb_rows(s)

            def f(e):
                ins = None
                for k in range(KD):
                    ins = e.transpose(bank_bf(bi)[:, k * 128:(k + 1) * 128], src[:, k * 128:(k + 1) * 128],
                                      ident[:, :])
                return ins
            P.op("pe", f, reads=srcB + [identB], writes=[bankB[bi]])
            eng = evac if evac else ("act" if s % 2 == 0 else "dve")
            if eng == "act":
                P.op("act", lambda e: e.activation(out=XA3[:, :, s * 128:(s + 1) * 128],
                                                   in_=bank_bf(bi).rearrange("p (k n) -> p k n", k=KD),
                                                   func=AF.Copy),
                     reads=[bankB[bi]], writes=[XAB[s]])
            else:
                P.op("dve", lambda e: e.tensor_copy(out=XA3[:, :, s * 128:(s + 1) * 128],
                                                    in_=bank_bf(bi).rearrange("p (k n) -> p k n", k=KD)),
                     reads=[bankB[bi]], writes=[XAB[s]])

        def phaseX(t):
            for s in range(S):
                phaseX_group(t, s)

        def proj_group(t, sid, s):
            j = PROJ_ORDER[sid]
            rt, rB = slot(t, sid)
            r3 = rt[:, :].rearrange("p (k n) -> p k n", k=KD)
            bi = rr["pe"] % 2
            rr["pe"] += 1
            cst = cs[t % 2]
            cstB = csB[t % 2]

            def f(e):
                ins = None
                for k in range(KD):
                    ins = e.matmul(bank(bi), XA3[:, k, s * 128:(s + 1) * 128], r3[:, k, :],
                                   start=(k == 0), stop=(k == KD - 1))
                return ins
            P.op("pe", f, reads=[XAB[s], rB], writes=[bankB[bi]])
            bk = bank(bi)
            if j < 2:
                ri = (j * S + s) % 2
                A, Bt = ropeA[ri], ropeBt[ri]
                bk4 = bk.rearrange("p (h two x) -> p h two x", h=4, two=2)
                A4 = A[:, :].rearrange("p (h two x) -> p h two x", h=4, two=2)
                B4 = Bt[:, :].rearrange("p (h two x) -> p h two x", h=4, two=2)
                cosv = cst[:, s * 128:s * 128 + 64].unsqueeze(1).unsqueeze(1).to_broadcast([128, 4, 2, 64])
                sinv = cst[:, s * 128 + 64:s * 128 + 128].unsqueeze(1).to_broadcast([128, 4, 64])
                P.op("dve", lambda e: e.tensor_tensor(out=A4, in0=bk4, in1=cosv, op=ALU.mult),
                     reads=[bankB[bi], cstB], writes=[ropeAB[ri]])

                def fb(e):
                    e.scalar_tensor_tensor(out=B4[:, :, 0, :], in0=bk4[:, :, 1, :], scalar=-1.0, in1=sinv,
                                           op0=ALU.mult, op1=ALU.mult)
                    return e.tensor_tensor(out=B4[:, :, 1, :], in0=bk4[:, :, 0, :], in1=sinv, op=ALU.mult)
                P.op("dve", fb, reads=[bankB[bi], cstB], writes=[ropeBB[ri]])
                dst = qkb[:, (j * S + s) * 512:(j * S + s + 1) * 512]
                P.op("pool", lambda e: e.tensor_tensor(out=dst, in0=A[:, :], in1=Bt[:, :], op=ALU.add),
                     reads=[ropeAB[ri], ropeBB[ri]], writes=[qkbB[j][s]])
            elif j == 2:
                def fv(e):
                    ins = None
                    for h in range(4):
                        ins = e.activation(out=vd[:, s * 512 + h * 128:s * 512 + (h + 1) * 128],
                                           in_=bk[:, h * 128:(h + 1) * 128], func=AF.Identity,
                                           scale=vdec[:, h:h + 1])
                    return ins
                P.op("act", fv, reads=[bankB[bi], constB], writes=[vdB[s]])
            elif j == 3:
                P.op("act", lambda e: e.activation(out=sg[:, s * 512:(s + 1) * 512], in_=bk, func=AF.Silu),
                     reads=[bankB[bi]], writes=[sgB[s]])
            else:
                pi = (t * S + s) % 5
                P.op("act", lambda e: e.activation(out=pb[:, pi * 512:(pi + 1) * 512], in_=bk, func=AF.Copy),
                     reads=[bankB[bi]], writes=[pbB[pi]])

        def phaseO(t):
            for s in range(S):
                gs = t * S + s
                pi, pp = gs % 5, (gs - 1) % 5
                bi = 4 + (s % 2)

                def f(e, gs=gs, pi=pi, pp=pp, bi=bi):
                    ins = None
                    for g in range(4):
                        o = bank(bi)[:, g * 128:(g + 1) * 128]
                        if gs == 0:
                            ins = e.matmul(o, pb[:, pi * 512 + g * 128:pi * 512 + (g + 1) * 128],
                                           bands[:, (8 + g) * 128:(9 + g) * 128], start=True, stop=True)
                        else:
                            e.matmul(o, pb[:, pi * 512 + g * 128:pi * 512 + (g + 1) * 128],
                                     bands[:, (4 + g) * 128:(5 + g) * 128], start=True, stop=False)
                            ins = e.matmul(o, pb[:, pp * 512 + g * 128:pp * 512 + (g + 1) * 128],
                                           bands[:, g * 128:(g + 1) * 128], start=False, stop=True)
                    return ins
                P.op("pe", f, reads=[pbB[pi], pbB[pp], constB], writes=[bankB[bi]])
                P.op("act", lambda e, s=s, bi=bi: e.activation(out=pT3[:, :, s * 128:(s + 1) * 128],
                                                               in_=bank(bi).rearrange("p (g n) -> p g n", g=4),
                                                               func=AF.Copy),
                     reads=[bankB[bi]], writes=[pooledB[s]])
            for g in range(4):
                bi = 6 + (g % 2)
                P.op("pe", lambda e, g=g, bi=bi: e.matmul(bank(bi), wpool[:, g * 128:(g + 1) * 128], pT3[:, g, :],
                                                          start=True, stop=True),
                     reads=pooledB + [constB], writes=[bankB[bi]])
                P.op("act", lambda e, g=g, bi=bi: e.activation(out=mixT3[:, 4 + g, :], in_=bank(bi),
                                                               func=AF.Identity, scale=pscale[:, g:g + 1]),
                     reads=[bankB[bi], constB], writes=[xbmB])

        def phaseT(t):
            qbk = [2, 3, 4, 5]
            kbk = [6, 7, 0, 1]

            def tr(j, s, bi):
                def f(e):
                    ins = None
                    for h in range(4):
                        ins = e.transpose(bank_bf(bi)[:, h * 128:(h + 1) * 128],
                                          qkb[:, (j * S + s) * 512 + h * 128:(j * S + s) * 512 + (h + 1) * 128],
                                          ident[:, :])
                    return ins
                P.op("pe", f, reads=[qkbB[j][s], identB], writes=[bankB[bi]])
            for s in range(S):
                bi = qbk[s]
                tr(0, s, bi)
                P.op("dve", lambda e, s=s, bi=bi: e.tensor_tensor(
                    out=qT3[:, :, s * 128:(s + 1) * 128],
                    in0=bank_bf(bi)[:, 0:512].rearrange("p (h n) -> p h n", h=4),
                    in1=qdec[:, :].rearrange("p (h n) -> p h n", h=4), op=ALU.mult),
                    reads=[bankB[bi], constB], writes=[qTB[s]])
            for s in range(S):
                bi = kbk[s]
                tr(1, s, bi)
                P.op("act", lambda e, s=s, bi=bi: e.activation(
                    out=kT3[:, :, s * 128:(s + 1) * 128],
                    in_=bank_bf(bi)[:, 0:512].rearrange("p (h n) -> p h n", h=4), func=AF.Copy),
                    reads=[bankB[bi]], writes=[kTB[s]])

        def phaseR(t):
            ub = [4, 5, 0, 1]
            sbk = [2, 3, 6, 7]
            rbk = [4, 5, 0, 1]
            for s in range(S):
                gs = t * S + s
                bi = sbk[s]

                def f(e, s=s, bi=bi):
                    ins = None
                    for h in range(4):
                        ins = e.matmul(bank(bi)[:, h * 128:(h + 1) * 128], kT3[:, h, s * 128:(s + 1) * 128],
                                       qT3[:, h, s * 128:(s + 1) * 128], start=True, stop=True)
                    return ins
                P.op("pe", f, reads=[kTB[s], qTB[s]], writes=[bankB[bi]])
                bu = ub[s]

                def fu(e, s=s, bu=bu):
                    ins = None
                    for h in range(4):
                        ins = e.matmul(bank(bu)[:, h * 128:(h + 1) * 128],
                                       qkb[:, (S + s) * 512 + h * 128:(S + s) * 512 + (h + 1) * 128],
                                       vd[:, s * 512 + h * 128:s * 512 + (h + 1) * 128], start=True, stop=True)
                    return ins
                P.op("pe", fu, reads=[qkbB[1][s], vdB[s]], writes=[bankB[bu]])
                P.op("dve", lambda e, s=s, bi=bi: e.tensor_tensor(out=ST[s][:, :], in0=bank(bi), in1=mask[:, :],
                                                                  op=ALU.mult),
                     reads=[bankB[bi], constB], writes=[STB[s]])
                snew, sold = state2[gs % 2], state2[(gs - 1) % 2]

                def fs(e, bu=bu, snew=snew, sold=sold):
                    ins = None
                    for h in range(4):
                        ins = e.scalar_tensor_tensor(out=snew[:, h * 128:(h + 1) * 128],
                                                     in0=sold[:, h * 128:(h + 1) * 128], scalar=cds[h],
                                                     in1=bank(bu)[:, h * 128:(h + 1) * 128],
                                                     op0=ALU.mult, op1=ALU.add)
                    return ins
                P.op("dve", fs, reads=[bankB[bu], state2B[(gs - 1) % 2]], writes=[state2B[gs % 2]])
                sbi = gs % 5
                P.op("act", lambda e, sbi=sbi, snew=snew: e.activation(out=stateb[sbi][:, :], in_=snew[:, :],
                                                                       func=AF.Copy),
                     reads=[state2B[gs % 2]], writes=[statebB[sbi]])
            for s in range(S):
                gs = t * S + s
                bi = rbk[s]
                prevb = stateb[(gs - 1) % 5]
                prevB = statebB[(gs - 1) % 5]

                def f(e, s=s, gs=gs, bi=bi, prevb=prevb):
                    ins = None
                    for h in range(4):
                        o = bank(bi)[:, h * 128:(h + 1) * 128]
                        ins = e.matmul(o, ST[s][:, h * 128:(h + 1) * 128],
                                       vd[:, s * 512 + h * 128:s * 512 + (h + 1) * 128],
                                       start=True, stop=(gs == 0))
                        if gs > 0:
                            ins = e.matmul(o, qT3[:, h, s * 128:(s + 1) * 128], prevb[:, h * 128:(h + 1) * 128],
                                           start=False, stop=True)
                    return ins
                P.op("pe", f, reads=[STB[s], vdB[s], qTB[s], prevB], writes=[bankB[bi]])

            def rms_sq(s):
                bi = rbk[s]

                def fq(e):
                    ins = None
                    for h in range(4):
                        ins = e.activation(out=Hh[0][:, h * 128:(h + 1) * 128], in_=bank(bi)[:, h * 128:(h + 1) * 128],
                                           func=AF.Square, accum_out=ss[s][:, h:h + 1])
                    return ins
                P.op("act", fq, reads=[bankB[bi]], writes=[ssB[s], HB[0]])

            def rms_rstd(s):
                P.op("dve", lambda e: e.tensor_scalar(
                    out=rstd[s][:, :], in0=ss[s][:, :], scalar1=1.0 / 128.0, scalar2=RMS_EPS,
                    op0=ALU.mult, op1=ALU.add), reads=[ssB[s]], writes=[rstdB[s]])
                P.op("pool", lambda e: e.tensor_tensor(
                    out=rstd[s][:, :], in0=rstd[s][:, :], in1=mhalf[:, :], op=ALU.pow),
                    reads=[], writes=[rstdB[s]])

            def rms_gate(s):
                bi = rbk[s]

                def fr(e):
                    ins = None
                    for h in range(4):
                        ins = e.scalar_tensor_tensor(out=retn[s][:, h * 128:(h + 1) * 128],
                                                     in0=bank(bi)[:, h * 128:(h + 1) * 128],
                                                     scalar=rstd[s][:, h:h + 1],
                                                     in1=sg[:, s * 512 + h * 128:s * 512 + (h + 1) * 128],
                                                     op0=ALU.mult, op1=ALU.mult)
                    return ins
                P.op("dve", fr, reads=[bankB[bi], rstdB[s], sgB[s]], writes=[retnB[s]])
                bt = 2 + (s % 2)

                def ft(e):
                    ins = None
                    for h in range(4):
                        ins = e.transpose(bank_bf(bt)[:, h * 128:(h + 1) * 128], retn[s][:, h * 128:(h + 1) * 128],
                                          ident[:, :])
                    return ins
                P.op("pe", ft, reads=[retnB[s], identB], writes=[bankB[bt]])

            def rms_evac(s):
                bt = 2 + (s % 2)
                P.op("act", lambda e: e.activation(
                    out=mixT3[:, 0:4, s * 128:(s + 1) * 128],
                    in_=bank_bf(bt)[:, 0:512].rearrange("p (h n) -> p h n", h=4), func=AF.Copy),
                    reads=[bankB[bt]], writes=[xbmB])
            for s in range(S):
                rms_sq(s)
            rms_rstd(0)
            rms_rstd(1)
            rms_gate(0)
            rms_rstd(2)
            rms_evac(0)
            rms_gate(1)
            rms_rstd(3)
            rms_evac(1)
            rms_gate(2)
            rms_evac(2)
            rms_gate(3)
            rms_evac(3)

        def phaseW(t):
            r0, r0B = slot(t, 5)
            r1, r1B = slot(t, 6)
            wo = [r0[:, :].rearrange("p (k n) -> p k n", k=KD), r1[:, :].rearrange("p (k n) -> p k n", k=KD)]
            st = {}

            def stage1(s):
                qi = wtiles[s % 3]

                def f(e):
                    ins = None
                    for hh in range(2):
                        for k in range(KD):
                            ins = e.matmul(Q[qi][:, hh * 512:(hh + 1) * 512], mixT3[:, k, s * 128:(s + 1) * 128],
                                           wo[hh][:, k, :], start=(k == 0), stop=(k == KD - 1))
                    return ins
                P.op("pe", f, reads=[xbmB, r0B, r1B], writes=[bankB[2 * qi], bankB[2 * qi + 1]])
                yi = rr["y"] % 4
                rr["y"] += 1
                li = rr["ln"] % 4
                rr["ln"] += 1
                st[s] = (yi, li)
                Xs = X[:, s * D:(s + 1) * D]
                P.op("dve", lambda e: e.scalar_tensor_tensor(
                    out=Y[yi][:, :], in0=Xs, scalar=ALPHA, in1=Q[qi][:, :], op0=ALU.mult, op1=ALU.add),
                    reads=[XB[s], bankB[2 * qi], bankB[2 * qi + 1]], writes=[YB[yi]])
                ln_stats(yi, li)

            def ynb(s):
                return aT[:, s * 1024:(s + 1) * 1024]

            def stage1b(s):
                yi, li = st[s]
                ln_rstd(li)
                P.op("act", lambda e: e.activation(out=ynb(s), in_=Y[yi][:, :], func=AF.Identity,
                                                   bias=lnmv[li][:, 3:4], scale=lnmv[li][:, 2:3]),
                     reads=[lnB[li], YB[yi]], writes=[aTB[2 * s], aTB[2 * s + 1]])

            def stage2(s):
                yi, li = st[s]
                Xs = X[:, s * D:(s + 1) * D]
                P.op("dve", lambda e: e.tensor_scalar(out=Xs, in0=Y[yi][:, :], scalar1=lnmv[li][:, 2:3],
                                                      scalar2=lnmv[li][:, 3:4], op0=ALU.mult, op1=ALU.add),
                     reads=[YB[yi], lnB[li]], writes=[XB[s]])
                P.op("dve", lambda e: e.tensor_tensor(out=Xs, in0=Xs, in1=g1t[:, :], op=ALU.mult),
                     reads=[constB], writes=[XB[s]])
                P.op("dve", lambda e: e.tensor_tensor(out=Xs, in0=Xs, in1=b1t[:, :], op=ALU.add),
                     reads=[constB], writes=[XB[s]])

            def stage3(s):
                yi, li = st[s]
                qi = 1 if s % 2 == 0 else 2
                ba, bb = 2 * qi, 2 * qi + 1

                def f(e):
                    ins = None
                    for k in range(KD):
                        ob = bank_bf(ba) if k < 4 else bank_bf(bb)
                        ins = e.transpose(ob[:, (k % 4) * 128:(k % 4 + 1) * 128], ynb(s)[:, k * 128:(k + 1) * 128],
                                          ident[:, :])
                    return ins
                P.op("pe", f, reads=[aTB[2 * s], aTB[2 * s + 1], identB], writes=[bankB[ba], bankB[bb]])

                def fa(e):
                    ins = None
                    for k in range(0, 4):
                        ins = e.activation(out=XT3[:, k, s * 128:(s + 1) * 128],
                                           in_=bank_bf(ba)[:, k * 128:(k + 1) * 128],
                                           func=AF.Identity, bias=b1p[:, k:k + 1], scale=g1p[:, k:k + 1])
                    return ins
                P.op("act", fa, reads=[bankB[ba], constB], writes=[XTB[s]])

                def fd(e):
                    ins = None
                    for k in range(4, 8):
                        ins = e.tensor_scalar(out=XT3[:, k, s * 128:(s + 1) * 128],
                                              in0=bank_bf(bb)[:, (k - 4) * 128:(k - 3) * 128],
                                              scalar1=g1p[:, k:k + 1], scalar2=b1p[:, k:k + 1],
                                              op0=ALU.mult, op1=ALU.add)
                    return ins
                P.op("dve", fd, reads=[bankB[bb], constB], writes=[XTB2[s]])

            stage1(0)
            stage1(1)
            stage1b(0)
            stage1(2)
            stage1b(1)
            stage1(3)
            stage1b(2)
            if t + 1 < NT:
                phaseX_group(t + 1, 0, banks=(6, 7), evac="act")
                phaseX_group(t + 1, 1, banks=(6, 7), evac="act")
            stage1b(3)
            stage3(0)
            stage3(1)
            if t + 1 < NT:
                phaseX_group(t + 1, 2, banks=(6, 7), evac="act")
                phaseX_group(t + 1, 3, banks=(6, 7), evac="act")
            stage3(2)
            stage3(3)
            issue_slot_load()
            issue_slot_load()
            return [(lambda s=s: stage2(s)) for s in range(S)]

        def phaseU(t, deferred):
            pend = []
            for u in range(11):
                rt, rB = slot(t, 7 + u)
                r3 = rt[:, :].rearrange("p (k n) -> p k n", k=KD)
                for cl in range(2):
                    c = 2 * u + cl
                    vb = (0, 1, 4, 5)[rr["u"] % 4]
                    gb = (6, 7)[rr["u"] % 2]
                    rr["u"] += 1

                    def f(e, vb=vb, gb=gb, cl=cl, r3=r3):
                        ins = None
                        for part in range(2):
                            col = part * 256 + cl * 128
                            ob = bank(vb) if part == 0 else bank(gb)
                            for k in range(KD):
                                ins = e.matmul(ob, r3[:, k, col:col + 128],
                                               XT3[:, k, :], start=(k == 0), stop=(k == KD - 1))
                        return ins
                    P.op("pe", f, reads=XTB + XTB2 + [rB], writes=[bankB[vb], bankB[gb]])
                    fi = rr["ffn"] % 3
                    rr["ffn"] += 1
                    Vp = bank(vb)
                    Gp = bank(gb)
                    def fg(e, fi=fi, c=c, Gp=Gp):
                        e.activation(out=G[fi][:, 0:2], in_=carry[:, 2 * c:2 * c + 2], func=AF.Copy)
                        return e.activation(out=G[fi][:, 2:514], in_=Gp, func=AF.Copy)
                    P.op("act", fg, reads=[carryB[c], bankB[gb]], writes=[GB[fi]])
                    P.op("act", lambda e, c=c, Gp=Gp: e.activation(out=carry[:, 2 * c:2 * c + 2],
                                                                  in_=Gp[:, 510:512], func=AF.Copy),
                         reads=[bankB[gb]], writes=[carryB[c]])
                    P.op("act", lambda e, fi=fi, Gp=Gp, c=c: e.activation(
                        out=Hh[fi][:, :], in_=Gp, func=AF.Identity, bias=cb[:, c:c + 1], scale=cw3[:, c, 2:3]),
                        reads=[bankB[gb], constB], writes=[HB[fi]])
                    P.op("dve", lambda e, fi=fi, c=c: e.scalar_tensor_tensor(
                        out=Hh[fi][:, :], in0=G[fi][:, 1:513], scalar=cw3[:, c, 1:2], in1=Hh[fi][:, :],
                        op0=ALU.mult, op1=ALU.add),
                        reads=[GB[fi], constB], writes=[HB[fi]])
                    P.op("dve", lambda e, fi=fi, c=c: e.scalar_tensor_tensor(
                        out=Hh[fi][:, :], in0=G[fi][:, 0:512], scalar=cw3[:, c, 0:1], in1=Hh[fi][:, :],
                        op0=ALU.mult, op1=ALU.add),
                        reads=[GB[fi], constB], writes=[HB[fi]])
                    if pend:
                        pend.pop(0)()

                    def tail(fi=fi, c=c, Vp=Vp, vb=vb):
                        P.op("act", lambda e: e.activation(out=Hh[fi][:, :], in_=Hh[fi][:, :], func=AF.Silu),
                             reads=[], writes=[HB[fi]])
                        P.op("dve", lambda e: e.tensor_tensor(
                            out=aT[:, c * TT:(c + 1) * TT], in0=Hh[fi][:, :], in1=Vp, op=ALU.mult),
                            reads=[HB[fi], bankB[vb]], writes=[aTB[c]])
                    pend.append(tail)
                    if deferred and c in (3, 7, 11, 15):
                        deferred.pop(0)()
                issue_slot_load()
            while pend:
                pend.pop(0)()

        def phaseD(t):
            for v in range(6):
                rt, rB = slot(t, 18 + v)
                nch = 4 if v < 5 else 2
                r3 = rt[:, 0:nch * 1024].rearrange("p (c n) -> p c n", c=nch)
                for s in (1, 3, 2, 0):
                    def f(e, v=v, nch=nch, r3=r3, s=s):
                        ins = None
                        for hh in range(2):
                            for cl in range(nch):
                                c = 4 * v + cl
                                ins = e.matmul(Q[s][:, hh * 512:(hh + 1) * 512],
                                               aT[:, c * TT + s * 128:c * TT + (s + 1) * 128],
                                               r3[:, cl, hh * 512:(hh + 1) * 512],
                                               start=(c == 0), stop=(c == NF - 1))
                        return ins
                    P.op("pe", f, reads=aTB[4 * v:4 * v + nch] + [rB], writes=[bankB[2 * s], bankB[2 * s + 1]])
                issue_slot_load()

        def phaseL2(t, filler):
            st = {}

            def stage1(s):
                yi = rr["y"] % 4
                rr["y"] += 1
                li = rr["ln"] % 4
                rr["ln"] += 1
                st[s] = (yi, li)
                Xs = X[:, s * D:(s + 1) * D]
                P.op("dve", lambda e: e.scalar_tensor_tensor(
                    out=Y[yi][:, :], in0=Xs, scalar=ALPHA, in1=Q[s][:, :], op0=ALU.mult, op1=ALU.add),
                    reads=[XB[s], bankB[2 * s], bankB[2 * s + 1]], writes=[YB[yi]])
                ln_stats(yi, li)

            def stage1b(s):
                yi, li = st[s]
                ln_fin(yi, li)

            def stage2(s):
                yi, li = st[s]
                P.op("dve", lambda e: e.tensor_tensor(out=Y[yi][:, :], in0=Y[yi][:, :], in1=g2t[:, :], op=ALU.mult),
                     reads=[constB], writes=[YB[yi]])
                P.op("dve" if t == NT - 1 else "pool",
                     lambda e: e.tensor_tensor(out=Y[yi][:, :], in0=Y[yi][:, :], in1=b2t[:, :], op=ALU.add),
                     reads=[constB], writes=[YB[yi]])
                r0w = t * TT + s * 128
                out_toks.append(P.dma("sp", out_d[r0w:r0w + 128, :], Y[yi][:, :], "Y%d" % yi, reads=[YB[yi]]))
            stage1(0)
            filler(2)
            stage1b(0)
            stage1(1)
            filler(2)
            stage1b(1)
            stage2(0)
            stage1(2)
            filler(2)
            stage1b(2)
            stage2(1)
            stage1(3)
            filler(2)
            stage1b(3)
            stage2(2)
            filler(2)
            stage2(3)
            filler(100)

        def proj_items(t):
            items = []
            for sid in (0, 1, 2):
                for s in range(S):
                    items.append(lambda sid=sid, s=s: proj_group(t, sid, s))
                items.append(None)
            return items

        def make_filler(items):
            def filler(n):
                k = 0
                while items and k < n:
                    it = items.pop(0)
                    if it is None:
                        issue_slot_load()
                    else:
                        it()
                        k += 1
                while items and items[0] is None:
                    items.pop(0)
                    issue_slot_load()
            return filler

        phaseX(0)
        make_filler(proj_items(0))(100)
        for t in range(NT):
            if t + 1 < NT:
                csload(t + 1)
            for sid in (3, 4):
                for s in range(S):
                    proj_group(t, sid, s)
                issue_slot_load()
            phaseT(t)
            phaseO(t)
            if t + 1 < NT:
                xload_bf(t + 1)
            phaseR(t)
            dbg("mixT", lambda: xbm[:, :], [128, 4096], [xbmB], BF16)
            deferred = phaseW(t)
            phaseU(t, deferred)
            dbg("x1", lambda: X[:, :], [128, S * D], XB)
            dbg("aT", lambda: aT[:, :], [128, NF * TT], aTB, BF16)
            phaseD(t)
            phaseL2(t, make_filler(proj_items(t + 1) if t + 1 < NT else []))
            if t + 1 < NT:
                xload_f32(t + 1)
        P.wait_all("sp", out_toks)
        P.emit()
    return nc, list(dbg_d.keys())


def make_inputs(x_b, w):
    consts, _ = host_consts()
    m = {
        "x": np.ascontiguousarray(x_b, dtype=np.float32),
        "w_in": np.ascontiguousarray(w["w_in"][0]),
        "w_pool": np.ascontiguousarray(w["w_pool"][0]),
        "w_out": np.ascontiguousarray(w["w_out"][0]),
        "w_up": np.ascontiguousarray(w["w_up"][0]),
        "w_down": np.ascontiguousarray(w["w_down"][0]),
        "pscale": np.ascontiguousarray(w["pool_scale"][0].reshape(4, 128).T),
        "g1t": np.ascontiguousarray(np.broadcast_to(w["ln1_g"][0][None, :], (128, D))),
        "b1t": np.ascontiguousarray(np.broadcast_to(w["ln1_b"][0][None, :], (128, D))),
        "g2t": np.ascontiguousarray(np.broadcast_to(w["ln2_g"][0][None, :], (128, D))),
        "b2t": np.ascontiguousarray(np.broadcast_to(w["ln2_b"][0][None, :], (128, D))),
        "cw": np.ascontiguousarray(w["conv_w"][0].reshape(3, NF, 128).transpose(2, 1, 0).reshape(128, NF * 3)),
        "cb": np.ascontiguousarray(w["conv_b"][0].reshape(NF, 128).T),
        "rope": consts["rope"],
        "qdec": consts["qdec"],
        "vdec": consts["vdec"],
        "mask": consts["mask"],
        "bands": np.ascontiguousarray(consts["bands"].reshape(128, 12 * 128)),
        "ident": consts["ident"],
        "g1p": np.ascontiguousarray(w["ln1_g"][0].reshape(KD, 128).T),
        "b1p": np.ascontiguousarray(w["ln1_b"][0].reshape(KD, 128).T),
    }
    return m


def kernel(x, w_in, w_pool, pool_scale, w_out, ln1_g, ln1_b, w_up, conv_w, conv_b, w_down, ln2_g, ln2_b):
    w = dict(w_in=np.asarray(w_in, np.float32), w_pool=np.asarray(w_pool, np.float32),
             pool_scale=np.asarray(pool_scale, np.float32), w_out=np.asarray(w_out, np.float32),
             ln1_g=np.asarray(ln1_g, np.float32), ln1_b=np.asarray(ln1_b, np.float32),
             w_up=np.asarray(w_up, np.float32), conv_w=np.asarray(conv_w, np.float32),
             conv_b=np.asarray(conv_b, np.float32), w_down=np.asarray(w_down, np.float32),
             ln2_g=np.asarray(ln2_g, np.float32), ln2_b=np.asarray(ln2_b, np.float32))
    x = np.asarray(x, np.float32)
    nb, T, _ = x.shape
    nc, _ = build(T // TT)
    in_maps = [make_inputs(x[b], w) for b in range(nb)]
    res = run_bass_kernel_spmd(nc, in_maps, core_ids=list(range(nb)))
    return np.stack([np.asarray(r["out"], dtype=np.float32) for r in res.results], axis=0)
```

```python
import contextlib
import numpy as np
import ml_dtypes
import concourse.bass as bass
import concourse.mybir as mybir
from concourse.bass_utils import run_bass_kernel_spmd

F32 = mybir.dt.float32
BF16 = mybir.dt.bfloat16
ALU = mybir.AluOpType
AF = mybir.ActivationFunctionType

D = 1024
KD = 8
TT = 512
S = 4
DFF = 2816
NF = 22
NSLOT = 24
RING = 3
PROJ_ORDER = (2, 3, 4, 0, 1)
ALPHA = 2.0 ** 0.25
LN_EPS = 1e-5
RMS_EPS = 1e-6
SELF_SYNC = True


class Buf:
    def __init__(self, name, excl=False):
        self.name = name
        self.w = []
        self.r = []
        self.excl = excl


class Prog:
    ENG = ("pe", "act", "dve", "pool", "sp")

    def __init__(self, nc, es):
        self.nc = nc
        self.es = es
        self.streams = {e: [] for e in self.ENG}
        self.cnt = {e: 0 for e in self.ENG}
        self.seen = {e: {} for e in self.ENG}
        self.sems = {}
        self.dcnt = {}
        for e in self.ENG:
            self.sems[e] = es.enter_context(nc.semaphore("s_" + e))

    def dsem(self, key):
        if key not in self.sems:
            self.sems[key] = self.es.enter_context(self.nc.semaphore("d_" + key))
            self.dcnt[key] = 0
        return self.sems[key]

    def _deps(self, eng, reads, writes):
        toks = []
        for b in reads:
            toks += b.w
            if b.excl:
                toks += b.r
        for b in writes:
            toks += b.w
            toks += b.r
        need = {}
        for (k, v) in toks:
            if k == eng and (not SELF_SYNC or eng == "pe"):
                continue
            if self.seen[eng].get(k, 0) >= v:
                continue
            if need.get(k, 0) < v:
                need[k] = v
        for k, v in need.items():
            self.seen[eng][k] = v
            self.streams[eng].append(("w", k, v))

    def _commit(self, tok, reads, writes):
        for b in reads:
            if b not in writes:
                b.r.append(tok)
                if len(b.r) > 64:
                    best = {}
                    for (k, v) in b.r:
                        if best.get(k, 0) < v:
                            best[k] = v
                    b.r = list(best.items())
        for b in writes:
            b.w = [tok]
            b.r = []

    def op(self, eng, fn, reads=(), writes=()):
        reads = list(reads)
        writes = list(writes)
        self._deps(eng, reads, writes)
        self.streams[eng].append(("op", fn))
        self.cnt[eng] += 1
        tok = (eng, self.cnt[eng])
        self._commit(tok, reads, writes)
        return tok

    def dma(self, eng, out, in_, key, reads=(), writes=()):
        reads = list(reads)
        writes = list(writes)
        self.dsem(key)
        self._deps(eng, reads, writes)
        self.streams[eng].append(("dma", out, in_, key))
        self.dcnt[key] += 16
        tok = (key, self.dcnt[key])
        self._commit(tok, reads, writes)
        return tok

    def wait_all(self, eng, toks):
        need = {}
        for (k, v) in toks:
            if need.get(k, 0) < v:
                need[k] = v
        for k, v in need.items():
            self.streams[eng].append(("w", k, v))

    def emit(self):
        nc = self.nc

        def mk(engname):
            def body(e):
                for ent in self.streams[engname]:
                    if ent[0] == "w":
                        e.wait_ge(self.sems[ent[1]], ent[2])
                    elif ent[0] == "op":
                        ins = ent[1](e)
                        ins.then_inc(self.sems[engname], 1)
                    else:
                        e.dma_start(out=ent[1], in_=ent[2]).then_inc(self.sems[ent[3]], 16)
            return body

        with nc.Block() as block:
            block.tensor(mk("pe"))
            block.scalar(mk("act"))
            block.vector(mk("dve"))
            block.gpsimd(mk("pool"))
            block.sync(mk("sp"))


def host_consts():
    dh = 128
    B = 128
    inv_freq = (10000.0 ** (-np.arange(0, dh, 2, dtype=np.float32) / np.float32(dh))).astype(np.float32)
    ang = (np.arange(4096, dtype=np.float32)[:, None] * inv_freq[None, :]).astype(np.float32)
    rope = np.concatenate([np.cos(ang), np.sin(ang)], axis=1).astype(np.float32)
    gam = 1.0 - 2.0 ** (-5.0 - np.arange(4, dtype=np.float64))
    i = np.arange(B, dtype=np.float64)
    qdec = np.zeros((128, 512), np.float32)
    vdec = np.zeros((128, 4), np.float32)
    mask = np.zeros((128, 512), np.float32)
    cds = []
    for h in range(4):
        g = gam[h]
        qd = g ** (i + 1.0) * dh ** -0.5
        qdec[:, h * 128:(h + 1) * 128] = qd[None, :]
        vd = g ** (B - 1.0 - i)
        vdec[:, h] = vd
        jj = i[:, None]
        ii = i[None, :]
        W = np.where((jj // 64) <= (ii // 64), g ** np.abs(ii - jj), 0.0)
        M = W / (g ** (ii + 1.0) * g ** (B - 1.0 - jj))
        mask[:, h * 128:(h + 1) * 128] = M
        cds.append(float(g ** B))
    bands = np.zeros((128, 12, 128), np.float32)
    m = np.arange(128)[:, None]
    t = np.arange(128)[None, :]
    for gi, w in enumerate((2, 4, 8, 16)):
        cur = np.where((m <= t) & (m > t - w), 1.0 / w, 0.0) - (m == t)
        prev = np.where((m - 128 <= t) & (m - 128 > t - w), 1.0 / w, 0.0)
        div = np.minimum(t + 1.0, float(w))
        cur0 = np.where((m <= t) & (m > t - w), 1.0 / div, 0.0) - (m == t)
        bands[:, gi, :] = prev
        bands[:, 4 + gi, :] = cur
        bands[:, 8 + gi, :] = cur0
    ident = np.eye(128, dtype=np.float32)
    return dict(rope=rope, qdec=qdec, vdec=vdec, mask=mask,
                bands=bands.astype(ml_dtypes.bfloat16), ident=ident.astype(ml_dtypes.bfloat16)), cds


def build(NT, debug=(), stop=None):
    consts, cds = host_consts()
    T = NT * TT
    nc = bass.Bass("TRN2", target_bir_lowering=False)
    dt_in = lambda name, shape, dt=F32: nc.dram_tensor(name, list(shape), dt, kind="ExternalInput").ap()
    x_d = dt_in("x", [T, D])
    w_in_d = dt_in("w_in", [D, 2560])
    w_pool_d = dt_in("w_pool", [4, 128, 128])
    w_out_d = dt_in("w_out", [D, D])
    w_up_d = dt_in("w_up", [D, 2 * DFF])
    w_down_d = dt_in("w_down", [DFF, D])
    pscale_d = dt_in("pscale", [128, 4])
    g1_d = dt_in("g1t", [128, D])
    b1_d = dt_in("b1t", [128, D])
    g2_d = dt_in("g2t", [128, D])
    b2_d = dt_in("b2t", [128, D])
    cw_d = dt_in("cw", [128, NF * 3])
    cb_d = dt_in("cb", [128, NF])
    rope_d = dt_in("rope", [4096, 128])
    qdec_d = dt_in("qdec", [128, 512])
    vdec_d = dt_in("vdec", [128, 4])
    mask_d = dt_in("mask", [128, 512])
    bands_d = dt_in("bands", [128, 12 * 128], BF16)
    g1p_d = dt_in("g1p", [128, KD])
    b1p_d = dt_in("b1p", [128, KD])
    ident_d = dt_in("ident", [128, 128], BF16)
    out_d = nc.dram_tensor("out", [T, D], F32, kind="ExternalOutput").ap()
    scr_d = nc.dram_tensor("wscr", [NSLOT, 128, 4096], BF16, kind="Internal").ap()
    dbg_d = {}

    es = contextlib.ExitStack()
    with es:
        P = Prog(nc, es)

        def sb(name, shape, dt=F32):
            return es.enter_context(nc.sbuf_tensor("sb_" + name, list(shape), dt))

        def ps(name, shape, dt=F32):
            return es.enter_context(nc.psum_tensor("ps_" + name, list(shape), dt))

        ring = [sb("ring%d" % i, [128, 4096], BF16) for i in range(RING)]
        ringB = [Buf("ring%d" % i) for i in range(RING)]
        scrB = [Buf("scr%d" % i) for i in range(NSLOT)]
        ident = sb("ident", [128, 128], BF16)
        qdec = sb("qdec", [128, 512])
        vdec = sb("vdec", [128, 4])
        mask = sb("mask", [128, 512])
        bands = sb("bands", [128, 12 * 128], BF16)
        wpool = sb("wpool", [128, 512], BF16)
        pscale = sb("pscale", [128, 4])
        g1t = sb("g1t", [128, D])
        b1t = sb("b1t", [128, D])
        g2t = sb("g2t", [128, D])
        b2t = sb("b2t", [128, D])
        cw = sb("cw", [128, NF * 3])
        cb = sb("cb", [128, NF])
        g1p = sb("g1p", [128, KD])
        b1p = sb("b1p", [128, KD])
        identB = Buf("identc")
        constB = Buf("const")
        cs = [sb("cs%d" % i, [128, S * 128]) for i in range(2)]
        csB = [Buf("cs%d" % i) for i in range(2)]
        X = sb("X", [128, S * D])
        XB = [Buf("X%d" % s) for s in range(S)]
        xbm = sb("xbm", [128, 4096], BF16)
        xbmB = Buf("xbm")
        XT = sb("XT", [128, KD * TT], BF16)
        XTB = [Buf("XT%d" % s) for s in range(S)]
        XTB2 = [Buf("XTh%d" % s) for s in range(S)]
        XA = sb("XA", [128, KD * TT], BF16)
        XAB = [Buf("XA%d" % s) for s in range(S)]
        qkb = sb("qkb", [128, 2 * S * 512], BF16)
        qkbB = [[Buf("qkb%d_%d" % (j, s)) for s in range(S)] for j in range(2)]
        qT = sb("qT", [128, 4 * TT], BF16)
        kT = sb("kT", [128, 4 * TT], BF16)
        qTB = [Buf("qT%d" % s) for s in range(S)]
        kTB = [Buf("kT%d" % s) for s in range(S)]
        vd = sb("vd", [128, S * 512], BF16)
        vdB = [Buf("vd%d" % s) for s in range(S)]
        sg = sb("sg", [128, S * 512])
        sgB = [Buf("sg%d" % s) for s in range(S)]
        pb = sb("pb", [128, 5 * 512], BF16)
        pbB = [Buf("pb%d" % i) for i in range(5)]
        ropeA = [sb("ropeA%d" % i, [128, 512]) for i in range(2)]
        ropeBt = [sb("ropeB%d" % i, [128, 512]) for i in range(2)]
        ropeAB = [Buf("ropeA%d" % i) for i in range(2)]
        ropeBB = [Buf("ropeB%d" % i) for i in range(2)]
        ST = [sb("ST%d" % i, [128, 512], BF16) for i in range(4)]
        STB = [Buf("ST%d" % i) for i in range(4)]
        state2 = [sb("state%d" % i, [128, 512]) for i in range(2)]
        state2B = [Buf("state%d" % i) for i in range(2)]
        stateb = [sb("stateb%d" % i, [128, 512], BF16) for i in range(5)]
        statebB = [Buf("stateb%d" % i) for i in range(5)]
        ss = [sb("ss%d" % i, [128, 4]) for i in range(4)]
        rstd = [sb("rstd%d" % i, [128, 4]) for i in range(4)]
        ssB = [Buf("ss%d" % i) for i in range(4)]
        rstdB = [Buf("rstd%d" % i) for i in range(4)]
        retn = [sb("retn%d" % i, [128, 512], BF16) for i in range(4)]
        retnB = [Buf("retn%d" % i) for i in range(4)]
        pooledT = sb("pooledT", [128, 4 * TT], BF16)
        pooledB = [Buf("pooled%d" % s) for s in range(S)]
        Y = [sb("Y%d" % i, [128, D]) for i in range(4)]
        YB = [Buf("Y%d" % i) for i in range(4)]
        lnst = [sb("lnst%d" % i, [128, 12]) for i in range(4)]
        lnmv = [sb("lnmv%d" % i, [128, 4]) for i in range(4)]
        lnB = [Buf("ln%d" % i) for i in range(4)]
        G = [sb("G%d" % i, [128, 514]) for i in range(3)]
        Hh = [sb("H%d" % i, [128, 512]) for i in range(3)]
        GB = [Buf("G%d" % i) for i in range(3)]
        HB = [Buf("H%d" % i) for i in range(3)]
        carry = sb("carry", [128, NF * 2])
        carryB = [Buf("carry%d" % c) for c in range(NF)]
        aT = sb("aT", [128, NF * TT], BF16)
        aTB = [Buf("aT%d" % c) for c in range(NF)]
        Q = [ps("Q%d" % i, [128, 1024]) for i in range(4)]
        bankB = [Buf("bank%d" % i, excl=True) for i in range(8)]

        def bank(i):
            return Q[i // 2][:, (i % 2) * 512:(i % 2) * 512 + 512]

        def bank_bf(i):
            return bank(i).bitcast(BF16)

        P.dma("sp", ident[:, :], ident_d, "identc", writes=[identB])
        ctoks = []
        for (dst, src) in ((qdec, qdec_d), (vdec, vdec_d), (mask, mask_d), (bands, bands_d),
                           (pscale, pscale_d), (g1t, g1_d), (b1t, b1_d), (g2t, g2_d), (b2t, b2_d),
                           (cw, cw_d), (cb, cb_d), (g1p, g1p_d), (b1p, b1p_d)):
            ctoks.append(P.dma("sp", dst[:, :], src, "const", writes=[Buf("c_" + dst.name)]))
        mhalf = sb("mhalf", [128, 4])
        P.op("pool", lambda e: e.memset(mhalf[:, :], -0.5), writes=[Buf("mhalf")])
        P.op("pool", lambda e: e.memset(carry[:, :], 0.0), writes=carryB)
        P.op("pool", lambda e: e.memset(state2[1][:, :], 0.0), writes=[state2B[1]])

        def slot_src(sid):
            if sid < 5:
                j = PROJ_ORDER[sid]
                return [(lambda r: r[:, :].rearrange("p (k n) -> p k n", k=KD),
                         w_in_d[:, j * 512:(j + 1) * 512].rearrange("(k p) n -> p k n", p=128))]
            if sid < 7:
                h = sid - 5
                return [(lambda r: r[:, :].rearrange("p (k n) -> p k n", k=KD),
                         w_out_d[:, h * 512:(h + 1) * 512].rearrange("(k p) n -> p k n", p=128))]
            if sid < 18:
                u = sid - 7
                return [
                    (lambda r: r[:, :].rearrange("p (k n) -> p k n", k=KD)[:, :, 0:256],
                     w_up_d[:, u * 256:(u + 1) * 256].rearrange("(k p) n -> p k n", p=128)),
                    (lambda r: r[:, :].rearrange("p (k n) -> p k n", k=KD)[:, :, 256:512],
                     w_up_d[:, DFF + u * 256:DFF + (u + 1) * 256].rearrange("(k p) n -> p k n", p=128)),
                ]
            v = sid - 18
            nch = 4 if v < 5 else 2
            return [(lambda r: r[:, 0:nch * 1024].rearrange("p (c n) -> p c n", c=nch),
                     w_down_d[v * 512:v * 512 + nch * 128, :].rearrange("(c p) n -> p c n", p=128))]

        nseq = NT * NSLOT
        issued = [0]

        def issue_slot_load():
            g = issued[0]
            if g >= nseq:
                return
            issued[0] += 1
            t, sid = divmod(g, NSLOT)
            ri = g % RING
            if t == 0:
                for (dv, src) in slot_src(sid):
                    P.dma("pool", dv(ring[ri]), src, "ringp%d" % ri, writes=[ringB[ri]])
                ncol = 2048 if sid == NSLOT - 1 else 4096
                if NT > 1:
                    P.dma("sp", scr_d[sid][:, 0:ncol], ring[ri][:, 0:ncol], "ringst%d" % ri, reads=[ringB[ri]],
                          writes=[scrB[sid]])
            else:
                ncol = 2048 if sid == NSLOT - 1 else 4096
                P.dma("sp", ring[ri][:, 0:ncol], scr_d[sid][:, 0:ncol], "ring%d" % ri, reads=[scrB[sid]],
                      writes=[ringB[ri]])

        def slot(t, sid):
            g = t * NSLOT + sid
            return ring[g % RING], ringB[g % RING]

        def xload_bf(t):
            P.dma("pool", qkb[:, 0:2048].rearrange("p (s d) -> p s d", s=2),
                  x_d[t * TT:t * TT + 256, :].rearrange("(s p) d -> p s d", p=128), "xbm", writes=qkbB[0])
            P.dma("pool", pooledT[:, :].rearrange("p (s d) -> p s d", s=2),
                  x_d[t * TT + 256:(t + 1) * TT, :].rearrange("(s p) d -> p s d", p=128), "xbm", writes=pooledB)

        def xb_rows(s):
            if s < 2:
                return qkb[:, s * 1024:(s + 1) * 1024], [qkbB[0][2 * s], qkbB[0][2 * s + 1]]
            return pooledT[:, (s - 2) * 1024:(s - 1) * 1024], list(pooledB)

        def xload_f32(t):
            for s in range(S):
                P.dma("sp", X[:, s * D:(s + 1) * D], x_d[t * TT + s * 128:t * TT + (s + 1) * 128, :],
                      "X%d" % s, writes=[XB[s]])

        def csload(t):
            P.dma("sp", cs[t % 2][:, :].rearrange("p (s c) -> p s c", s=S),
                  rope_d[t * TT:(t + 1) * TT, :].rearrange("(s p) c -> p s c", p=128), "cs%d" % (t % 2),
                  writes=[csB[t % 2]])

        def dbg(name, ap_fn, shape, reads, dt=F32):
            if name not in debug:
                return
            if name not in dbg_d:
                dbg_d[name] = nc.dram_tensor("dbg_" + name, list(shape), dt, kind="ExternalOutput").ap()
            tok = P.dma("sp", dbg_d[name], ap_fn(), "dbg_" + name, reads=reads)
            out_toks.append(tok)

        def ln_stats(yi, li):
            def lnstats(e):
                e.bn_stats(out=lnst[li][:, 0:6], in_=Y[yi][:, 0:512])
                return e.bn_stats(out=lnst[li][:, 6:12], in_=Y[yi][:, 512:1024])
            P.op("dve", lnstats, reads=[YB[yi]], writes=[lnB[li]])
            P.op("dve", lambda e: e.bn_aggr(out=lnmv[li][:, 0:2], in_=lnst[li][:, :]), reads=[], writes=[lnB[li]])
            P.op("dve", lambda e: e.tensor_scalar(out=lnmv[li][:, 2:3], in0=lnmv[li][:, 1:2], scalar1=LN_EPS,
                                                  scalar2=1.0, op0=ALU.add, op1=ALU.mult),
                 reads=[], writes=[lnB[li]])

        def ln_rstd(li):
            P.op("pool", lambda e: e.tensor_tensor(out=lnmv[li][:, 2:3], in0=lnmv[li][:, 2:3], in1=mhalf[:, 0:1],
                                                   op=ALU.pow), reads=[], writes=[lnB[li]])
            P.op("dve", lambda e: e.scalar_tensor_tensor(out=lnmv[li][:, 3:4], in0=lnmv[li][:, 0:1], scalar=-1.0,
                                                         in1=lnmv[li][:, 2:3], op0=ALU.mult, op1=ALU.mult),
                 reads=[], writes=[lnB[li]])

        def ln_fin(yi, li):
            ln_rstd(li)
            P.op("act", lambda e: e.activation(out=Y[yi][:, :], in_=Y[yi][:, :], func=AF.Identity,
                                               bias=lnmv[li][:, 3:4], scale=lnmv[li][:, 2:3]),
                 reads=[lnB[li]], writes=[YB[yi]])

        out_toks = []
        xload_bf(0)
        ctoks.append(P.dma("pool", wpool[:, :].rearrange("c (g d) -> c g d", g=4),
                           w_pool_d.rearrange("g c d -> c g d"), "constp", writes=[Buf("c_wpool")]))
        constB.w = [ctoks[-2], ctoks[-1]]
        for _ in range(RING):
            issue_slot_load()
        xload_f32(0)
        csload(0)
        rr = dict(pe=0, ffn=0, y=0, ln=0, u=0)
        XT3 = XT[:, :].rearrange("p (k n) -> p k n", k=KD)
        mixT3 = xbm[:, :].rearrange("p (k n) -> p k n", k=KD)
        qT3 = qT[:, :].rearrange("p (h n) -> p h n", h=4)
        kT3 = kT[:, :].rearrange("p (h n) -> p h n", h=4)
        pT3 = pooledT[:, :].rearrange("p (g n) -> p g n", g=4)
        cw3 = cw[:, :].rearrange("p (c j) -> p c j", j=3)
        wtiles = [0, 2, 3]

        XA3 = XA[:, :].rearrange("p (k n) -> p k n", k=KD)

        def phaseX_group(t, s, banks=(2, 3), evac=None):
            bi = banks[s % 2]
            src, srcB = xb_rows(s)

            def f(e):
                ins = None
                for k in range(KD):
                    ins = e.transpose(bank_bf(bi)[:, k * 128:(k + 1) * 128], src[:, k * 128:(k + 1) * 128],
                                      ident[:, :])
                return ins
            P.op("pe", f, reads=srcB + [identB], writes=[bankB[bi]])
            eng = evac if evac else ("act" if s % 2 == 0 else "dve")
            if eng == "act":
                P.op("act", lambda e: e.activation(out=XA3[:, :, s * 128:(s + 1) * 128],
                                                   in_=bank_bf(bi).rearrange("p (k n) -> p k n", k=KD),
                                                   func=AF.Copy),
                     reads=[bankB[bi]], writes=[XAB[s]])
            else:
                P.op("dve", lambda e: e.tensor_copy(out=XA3[:, :, s * 128:(s + 1) * 128],
                                                    in_=bank_bf(bi).rearrange("p (k n) -> p k n", k=KD)),
                     reads=[bankB[bi]], writes=[XAB[s]])

        def phaseX(t):
            for s in range(S):
                phaseX_group(t, s)

        def proj_group(t, sid, s):
            j = PROJ_ORDER[sid]
            rt, rB = slot(t, sid)
            r3 = rt[:, :].rearrange("p (k n) -> p k n", k=KD)
            bi = rr["pe"] % 4
            rr["pe"] += 1
            cst = cs[t % 2]
            cstB = csB[t % 2]

            def f(e):
                ins = None
                for k in range(KD):
                    ins = e.matmul(bank(bi), XA3[:, k, s * 128:(s + 1) * 128], r3[:, k, :],
                                   start=(k == 0), stop=(k == KD - 1))
                return ins
            P.op("pe", f, reads=[XAB[s], rB], writes=[bankB[bi]])
            bk = bank(bi)
            if j < 2:
                ri = (j * S + s) % 2
                A, Bt = ropeA[ri], ropeBt[ri]
                bk4 = bk.rearrange("p (h two x) -> p h two x", h=4, two=2)
                A4 = A[:, :].rearrange("p (h two x) -> p h two x", h=4, two=2)
                B4 = Bt[:, :].rearrange("p (h two x) -> p h two x", h=4, two=2)
                cosv = cst[:, s * 128:s * 128 + 64].unsqueeze(1).unsqueeze(1).to_broadcast([128, 4, 2, 64])
                sinv = cst[:, s * 128 + 64:s * 128 + 128].unsqueeze(1).to_broadcast([128, 4, 64])
                P.op("dve", lambda e: e.tensor_tensor(out=A4, in0=bk4, in1=cosv, op=ALU.mult),
                     reads=[bankB[bi], cstB], writes=[ropeAB[ri]])

                def fb(e):
                    e.scalar_tensor_tensor(out=B4[:, :, 0, :], in0=bk4[:, :, 1, :], scalar=-1.0, in1=sinv,
                                           op0=ALU.mult, op1=ALU.mult)
                    return e.tensor_tensor(out=B4[:, :, 1, :], in0=bk4[:, :, 0, :], in1=sinv, op=ALU.mult)
                P.op("dve", fb, reads=[bankB[bi], cstB], writes=[ropeBB[ri]])
                dst = qkb[:, (j * S + s) * 512:(j * S + s + 1) * 512]
                P.op("pool", lambda e: e.tensor_tensor(out=dst, in0=A[:, :], in1=Bt[:, :], op=ALU.add),
                     reads=[ropeAB[ri], ropeBB[ri]], writes=[qkbB[j][s]])
            elif j == 2:
                def fv(e):
                    ins = None
                    for h in range(4):
                        ins = e.activation(out=vd[:, s * 512 + h * 128:s * 512 + (h + 1) * 128],
                                           in_=bk[:, h * 128:(h + 1) * 128], func=AF.Identity,
                                           scale=vdec[:, h:h + 1])
                    return ins
                P.op("act", fv, reads=[bankB[bi], constB], writes=[vdB[s]])
            elif j == 3:
                P.op("act", lambda e: e.activation(out=sg[:, s * 512:(s + 1) * 512], in_=bk, func=AF.Silu),
                     reads=[bankB[bi]], writes=[sgB[s]])
            else:
                pi = (t * S + s) % 5
                P.op("act", lambda e: e.activation(out=pb[:, pi * 512:(pi + 1) * 512], in_=bk, func=AF.Copy),
                     reads=[bankB[bi]], writes=[pbB[pi]])

        def phaseO(t):
            for s in range(S):
                gs = t * S + s
                pi, pp = gs % 5, (gs - 1) % 5
                bi = 4 + (s % 2)

                def f(e, gs=gs, pi=pi, pp=pp, bi=bi):
                    ins = None
                    for g in range(4):
                        o = bank(bi)[:, g * 128:(g + 1) * 128]
                        if gs == 0:
                            ins = e.matmul(o, pb[:, pi * 512 + g * 128:pi * 512 + (g + 1) * 128],
                                           bands[:, (8 + g) * 128:(9 + g) * 128], start=True, stop=True)
                        else:
                            e.matmul(o, pb[:, pi * 512 + g * 128:pi * 512 + (g + 1) * 128],
                                     bands[:, (4 + g) * 128:(5 + g) * 128], start=True, stop=False)
                            ins = e.matmul(o, pb[:, pp * 512 + g * 128:pp * 512 + (g + 1) * 128],
                                           bands[:, g * 128:(g + 1) * 128], start=False, stop=True)
                    return ins
                P.op("pe", f, reads=[pbB[pi], pbB[pp], constB], writes=[bankB[bi]])
                P.op("act", lambda e, s=s, bi=bi: e.activation(out=pT3[:, :, s * 128:(s + 1) * 128],
                                                               in_=bank(bi).rearrange("p (g n) -> p g n", g=4),
                                                               func=AF.Copy),
                     reads=[bankB[bi]], writes=[pooledB[s]])
            for g in range(4):
                bi = 6 + (g % 2)
                P.op("pe", lambda e, g=g, bi=bi: e.matmul(bank(bi), wpool[:, g * 128:(g + 1) * 128], pT3[:, g, :],
                                                          start=True, stop=True),
                     reads=pooledB + [constB], writes=[bankB[bi]])
                P.op("act", lambda e, g=g, bi=bi: e.activation(out=mixT3[:, 4 + g, :], in_=bank(bi),
                                                               func=AF.Identity, scale=pscale[:, g:g + 1]),
                     reads=[bankB[bi], constB], writes=[xbmB])

        def phaseT(t):
            qbk = [4, 5, 6, 7]
            kbk = [0, 1, 2, 3]

            def tr(j, s, bi):
                def f(e):
                    ins = None
                    for h in range(4):
                        ins = e.transpose(bank_bf(bi)[:, h * 128:(h + 1) * 128],
                                          qkb[:, (j * S + s) * 512 + h * 128:(j * S + s) * 512 + (h + 1) * 128],
                                          ident[:, :])
                    return ins
                P.op("pe", f, reads=[qkbB[j][s], identB], writes=[bankB[bi]])
            for s in range(S):
                bi = qbk[s]
                tr(0, s, bi)
                P.op("dve", lambda e, s=s, bi=bi: e.tensor_tensor(
                    out=qT3[:, :, s * 128:(s + 1) * 128],
                    in0=bank_bf(bi)[:, 0:512].rearrange("p (h n) -> p h n", h=4),
                    in1=qdec[:, :].rearrange("p (h n) -> p h n", h=4), op=ALU.mult),
                    reads=[bankB[bi], constB], writes=[qTB[s]])
            for s in range(S):
                bi = kbk[s]
                tr(1, s, bi)
                P.op("act", lambda e, s=s, bi=bi: e.activation(
                    out=kT3[:, :, s * 128:(s + 1) * 128],
                    in_=bank_bf(bi)[:, 0:512].rearrange("p (h n) -> p h n", h=4), func=AF.Copy),
                    reads=[bankB[bi]], writes=[kTB[s]])

        def phaseR(t):
            ub = [4, 5, 0, 1]
            sbk = [2, 3, 6, 7]
            rbk = [4, 5, 0, 1]
            for s in range(S):
                gs = t * S + s
                bi = sbk[s]

                def f(e, s=s, bi=bi):
                    ins = None
                    for h in range(4):
                        ins = e.matmul(bank(bi)[:, h * 128:(h + 1) * 128], kT3[:, h, s * 128:(s + 1) * 128],
                                       qT3[:, h, s * 128:(s + 1) * 128], start=True, stop=True)
                    return ins
                P.op("pe", f, reads=[kTB[s], qTB[s]], writes=[bankB[bi]])
                bu = ub[s]

                def fu(e, s=s, bu=bu):
                    ins = None
                    for h in range(4):
                        ins = e.matmul(bank(bu)[:, h * 128:(h + 1) * 128],
                                       qkb[:, (S + s) * 512 + h * 128:(S + s) * 512 + (h + 1) * 128],
                                       vd[:, s * 512 + h * 128:s * 512 + (h + 1) * 128], start=True, stop=True)
                    return ins
                P.op("pe", fu, reads=[qkbB[1][s], vdB[s]], writes=[bankB[bu]])
                P.op("dve", lambda e, s=s, bi=bi: e.tensor_tensor(out=ST[s][:, :], in0=bank(bi), in1=mask[:, :],
                                                                  op=ALU.mult),
                     reads=[bankB[bi], constB], writes=[STB[s]])
                snew, sold = state2[gs % 2], state2[(gs - 1) % 2]

                def fs(e, bu=bu, snew=snew, sold=sold):
                    ins = None
                    for h in range(4):
                        ins = e.scalar_tensor_tensor(out=snew[:, h * 128:(h + 1) * 128],
                                                     in0=sold[:, h * 128:(h + 1) * 128], scalar=cds[h],
                                                     in1=bank(bu)[:, h * 128:(h + 1) * 128],
                                                     op0=ALU.mult, op1=ALU.add)
                    return ins
                P.op("dve", fs, reads=[bankB[bu], state2B[(gs - 1) % 2]], writes=[state2B[gs % 2]])
                sbi = gs % 5
                P.op("act", lambda e, sbi=sbi, snew=snew: e.activation(out=stateb[sbi][:, :], in_=snew[:, :],
                                                                       func=AF.Copy),
                     reads=[state2B[gs % 2]], writes=[statebB[sbi]])
            for s in range(S):
                gs = t * S + s
                bi = rbk[s]
                prevb = stateb[(gs - 1) % 5]
                prevB = statebB[(gs - 1) % 5]

                def f(e, s=s, gs=gs, bi=bi, prevb=prevb):
                    ins = None
                    for h in range(4):
                        o = bank(bi)[:, h * 128:(h + 1) * 128]
                        ins = e.matmul(o, ST[s][:, h * 128:(h + 1) * 128],
                                       vd[:, s * 512 + h * 128:s * 512 + (h + 1) * 128],
                                       start=True, stop=(gs == 0))
                        if gs > 0:
                            ins = e.matmul(o, qT3[:, h, s * 128:(s + 1) * 128], prevb[:, h * 128:(h + 1) * 128],
                                           start=False, stop=True)
                    return ins
                P.op("pe", f, reads=[STB[s], vdB[s], qTB[s], prevB], writes=[bankB[bi]])

            def rms_sq(s):
                bi = rbk[s]

                def fq(e):
                    ins = None
                    for h in range(4):
                        ins = e.activation(out=Hh[0][:, h * 128:(h + 1) * 128], in_=bank(bi)[:, h * 128:(h + 1) * 128],
                                           func=AF.Square, accum_out=ss[s][:, h:h + 1])
                    return ins
                P.op("act", fq, reads=[bankB[bi]], writes=[ssB[s], HB[0]])

            def rms_rstd(s):
                P.op("dve", lambda e: e.tensor_scalar(
                    out=rstd[s][:, :], in0=ss[s][:, :], scalar1=1.0 / 128.0, scalar2=RMS_EPS,
                    op0=ALU.mult, op1=ALU.add), reads=[ssB[s]], writes=[rstdB[s]])
                P.op("pool", lambda e: e.tensor_tensor(
                    out=rstd[s][:, :], in0=rstd[s][:, :], in1=mhalf[:, :], op=ALU.pow),
                    reads=[], writes=[rstdB[s]])

            def rms_gate(s):
                bi = rbk[s]

                def fr(e):
                    ins = None
                    for h in range(4):
                        ins = e.scalar_tensor_tensor(out=retn[s][:, h * 128:(h + 1) * 128],
                                                     in0=bank(bi)[:, h * 128:(h + 1) * 128],
                                                     scalar=rstd[s][:, h:h + 1],
                                                     in1=sg[:, s * 512 + h * 128:s * 512 + (h + 1) * 128],
                                                     op0=ALU.mult, op1=ALU.mult)
                    return ins
                P.op("dve", fr, reads=[bankB[bi], rstdB[s], sgB[s]], writes=[retnB[s]])
                bt = 2 + (s % 2)

                def ft(e):
                    ins = None
                    for h in range(4):
                        ins = e.transpose(bank_bf(bt)[:, h * 128:(h + 1) * 128], retn[s][:, h * 128:(h + 1) * 128],
                                          ident[:, :])
                    return ins
                P.op("pe", ft, reads=[retnB[s], identB], writes=[bankB[bt]])

            def rms_evac(s):
                bt = 2 + (s % 2)
                P.op("act", lambda e: e.activation(
                    out=mixT3[:, 0:4, s * 128:(s + 1) * 128],
                    in_=bank_bf(bt)[:, 0:512].rearrange("p (h n) -> p h n", h=4), func=AF.Copy),
                    reads=[bankB[bt]], writes=[xbmB])
            for s in range(S):
                rms_sq(s)
            rms_rstd(0)
            rms_rstd(1)
            rms_gate(0)
            rms_rstd(2)
            rms_evac(0)
            rms_gate(1)
            rms_rstd(3)
            rms_evac(1)
            rms_gate(2)
            rms_evac(2)
            rms_gate(3)
            rms_evac(3)

        def phaseW(t):
            r0, r0B = slot(t, 5)
            r1, r1B = slot(t, 6)
            wo = [r0[:, :].rearrange("p (k n) -> p k n", k=KD), r1[:, :].rearrange("p (k n) -> p k n", k=KD)]
            st = {}

            def stage1(s):
                qi = wtiles[s % 3]

                def f(e):
                    ins = None
                    for hh in range(2):
                        for k in range(KD):
                            ins = e.matmul(Q[qi][:, hh * 512:(hh + 1) * 512], mixT3[:, k, s * 128:(s + 1) * 128],
                                           wo[hh][:, k, :], start=(k == 0), stop=(k == KD - 1))
                    return ins
                P.op("pe", f, reads=[xbmB, r0B, r1B], writes=[bankB[2 * qi], bankB[2 * qi + 1]])
                yi = rr["y"] % 4
                rr["y"] += 1
                li = rr["ln"] % 4
                rr["ln"] += 1
                st[s] = (yi, li)
                Xs = X[:, s * D:(s + 1) * D]
                P.op("dve", lambda e: e.scalar_tensor_tensor(
                    out=Y[yi][:, :], in0=Xs, scalar=ALPHA, in1=Q[qi][:, :], op0=ALU.mult, op1=ALU.add),
                    reads=[XB[s], bankB[2 * qi], bankB[2 * qi + 1]], writes=[YB[yi]])
                ln_stats(yi, li)

            def ynb(s):
                return aT[:, s * 1024:(s + 1) * 1024]

            def stage1b(s):
                yi, li = st[s]
                ln_rstd(li)
                P.op("act", lambda e: e.activation(out=ynb(s), in_=Y[yi][:, :], func=AF.Identity,
                                                   bias=lnmv[li][:, 3:4], scale=lnmv[li][:, 2:3]),
                     reads=[lnB[li], YB[yi]], writes=[aTB[2 * s], aTB[2 * s + 1]])

            def stage2(s):
                yi, li = st[s]
                Xs = X[:, s * D:(s + 1) * D]
                P.op("dve", lambda e: e.tensor_scalar(out=Xs, in0=Y[yi][:, :], scalar1=lnmv[li][:, 2:3],
                                                      scalar2=lnmv[li][:, 3:4], op0=ALU.mult, op1=ALU.add),
                     reads=[YB[yi], lnB[li]], writes=[XB[s]])
                P.op("dve", lambda e: e.tensor_tensor(out=Xs, in0=Xs, in1=g1t[:, :], op=ALU.mult),
                     reads=[constB], writes=[XB[s]])
                P.op("dve", lambda e: e.tensor_tensor(out=Xs, in0=Xs, in1=b1t[:, :], op=ALU.add),
                     reads=[constB], writes=[XB[s]])

            def stage3(s):
                yi, li = st[s]
                qi = 1 if s % 2 == 0 else 2
                ba, bb = 2 * qi, 2 * qi + 1

                def f(e):
                    ins = None
                    for k in range(KD):
                        ob = bank_bf(ba) if k < 4 else bank_bf(bb)
                        ins = e.transpose(ob[:, (k % 4) * 128:(k % 4 + 1) * 128], ynb(s)[:, k * 128:(k + 1) * 128],
                                          ident[:, :])
                    return ins
                P.op("pe", f, reads=[aTB[2 * s], aTB[2 * s + 1], identB], writes=[bankB[ba], bankB[bb]])

                def fa(e):
                    ins = None
                    for k in range(0, 4):
                        ins = e.activation(out=XT3[:, k, s * 128:(s + 1) * 128],
                                           in_=bank_bf(ba)[:, k * 128:(k + 1) * 128],
                                           func=AF.Identity, bias=b1p[:, k:k + 1], scale=g1p[:, k:k + 1])
                    return ins
                P.op("act", fa, reads=[bankB[ba], constB], writes=[XTB[s]])

                def fd(e):
                    ins = None
                    for k in range(4, 8):
                        ins = e.tensor_scalar(out=XT3[:, k, s * 128:(s + 1) * 128],
                                              in0=bank_bf(bb)[:, (k - 4) * 128:(k - 3) * 128],
                                              scalar1=g1p[:, k:k + 1], scalar2=b1p[:, k:k + 1],
                                              op0=ALU.mult, op1=ALU.add)
                    return ins
                P.op("dve", fd, reads=[bankB[bb], constB], writes=[XTB2[s]])

            stage1(0)
            stage1(1)
            stage1b(0)
            stage1(2)
            stage1b(1)
            stage1(3)
            stage1b(2)
            if t + 1 < NT:
                phaseX_group(t + 1, 0, banks=(6, 7), evac="act")
                phaseX_group(t + 1, 1, banks=(6, 7), evac="act")
            stage1b(3)
            stage3(0)
            stage3(1)
            if t + 1 < NT:
                phaseX_group(t + 1, 2, banks=(6, 7), evac="act")
                phaseX_group(t + 1, 3, banks=(6, 7), evac="act")
            stage3(2)
            stage3(3)
            issue_slot_load()
            issue_slot_load()
            return [(lambda s=s: stage2(s)) for s in range(S)]

        def phaseU(t, deferred):
            pend = []
            for u in range(11):
                rt, rB = slot(t, 7 + u)
                r3 = rt[:, :].rearrange("p (k n) -> p k n", k=KD)
                for cl in range(2):
                    c = 2 * u + cl
                    vb = (0, 1, 4, 5)[rr["u"] % 4]
                    gb = (6, 7)[rr["u"] % 2]
                    rr["u"] += 1

                    def f(e, vb=vb, gb=gb, cl=cl, r3=r3):
                        ins = None
                        for part in range(2):
                            col = part * 256 + cl * 128
                            ob = bank(vb) if part == 0 else bank(gb)
                            for k in range(KD):
                                ins = e.matmul(ob, r3[:, k, col:col + 128],
                                               XT3[:, k, :], start=(k == 0), stop=(k == KD - 1))
                        return ins
                    P.op("pe", f, reads=XTB + XTB2 + [rB], writes=[bankB[vb], bankB[gb]])
                    fi = rr["ffn"] % 3
                    rr["ffn"] += 1
                    Vp = bank(vb)
                    Gp = bank(gb)
                    def fg(e, fi=fi, c=c, Gp=Gp):
                        e.activation(out=G[fi][:, 0:2], in_=carry[:, 2 * c:2 * c + 2], func=AF.Copy)
                        return e.activation(out=G[fi][:, 2:514], in_=Gp, func=AF.Copy)
                    P.op("act", fg, reads=[carryB[c], bankB[gb]], writes=[GB[fi]])
                    P.op("act", lambda e, c=c, Gp=Gp: e.activation(out=carry[:, 2 * c:2 * c + 2],
                                                                  in_=Gp[:, 510:512], func=AF.Copy),
                         reads=[bankB[gb]], writes=[carryB[c]])
                    P.op("act", lambda e, fi=fi, Gp=Gp, c=c: e.activation(
                        out=Hh[fi][:, :], in_=Gp, func=AF.Identity, bias=cb[:, c:c + 1], scale=cw3[:, c, 2:3]),
                        reads=[bankB[gb], constB], writes=[HB[fi]])
                    P.op("dve", lambda e, fi=fi, c=c: e.scalar_tensor_tensor(
                        out=Hh[fi][:, :], in0=G[fi][:, 1:513], scalar=cw3[:, c, 1:2], in1=Hh[fi][:, :],
                        op0=ALU.mult, op1=ALU.add),
                        reads=[GB[fi], constB], writes=[HB[fi]])
                    P.op("dve", lambda e, fi=fi, c=c: e.scalar_tensor_tensor(
                        out=Hh[fi][:, :], in0=G[fi][:, 0:512], scalar=cw3[:, c, 0:1], in1=Hh[fi][:, :],
                        op0=ALU.mult, op1=ALU.add),
                        reads=[GB[fi], constB], writes=[HB[fi]])
                    if pend:
                        pend.pop(0)()

                    def tail(fi=fi, c=c, Vp=Vp, vb=vb):
                        P.op("act", lambda e: e.activation(out=Hh[fi][:, :], in_=Hh[fi][:, :], func=AF.Silu),
                             reads=[], writes=[HB[fi]])
                        P.op("dve", lambda e: e.tensor_tensor(
                            out=aT[:, c * TT:(c + 1) * TT], in0=Hh[fi][:, :], in1=Vp, op=ALU.mult),
                            reads=[HB[fi], bankB[vb]], writes=[aTB[c]])
                    pend.append(tail)
                    if deferred and c in (3, 7, 11, 15):
                        deferred.pop(0)()
                issue_slot_load()
            while pend:
                pend.pop(0)()

        def phaseD(t):
            for v in range(6):
                rt, rB = slot(t, 18 + v)
                nch = 4 if v < 5 else 2
                r3 = rt[:, 0:nch * 1024].rearrange("p (c n) -> p c n", c=nch)
                for s in (1, 3, 2, 0):
                    def f(e, v=v, nch=nch, r3=r3, s=s):
                        ins = None
                        for hh in range(2):
                            for cl in range(nch):
                                c = 4 * v + cl
                                ins = e.matmul(Q[s][:, hh * 512:(hh + 1) * 512],
                                               aT[:, c * TT + s * 128:c * TT + (s + 1) * 128],
                                               r3[:, cl, hh * 512:(hh + 1) * 512],
                                               start=(c == 0), stop=(c == NF - 1))
                        return ins
                    P.op("pe", f, reads=aTB[4 * v:4 * v + nch] + [rB], writes=[bankB[2 * s], bankB[2 * s + 1]])
                issue_slot_load()

        def phaseL2(t, filler):
            st = {}

            def stage1(s):
                yi = rr["y"] % 4
                rr["y"] += 1
                li = rr["ln"] % 4
                rr["ln"] += 1
                st[s] = (yi, li)
                Xs = X[:, s * D:(s + 1) * D]
                P.op("dve", lambda e: e.scalar_tensor_tensor(
                    out=Y[yi][:, :], in0=Xs, scalar=ALPHA, in1=Q[s][:, :], op0=ALU.mult, op1=ALU.add),
                    reads=[XB[s], bankB[2 * s], bankB[2 * s + 1]], writes=[YB[yi]])
                ln_stats(yi, li)

            def stage1b(s):
                yi, li = st[s]
                ln_fin(yi, li)

            def stage2(s):
                yi, li = st[s]
                P.op("dve", lambda e: e.tensor_tensor(out=Y[yi][:, :], in0=Y[yi][:, :], in1=g2t[:, :], op=ALU.mult),
                     reads=[constB], writes=[YB[yi]])
                P.op("dve" if t == NT - 1 else "pool",
                     lambda e: e.tensor_tensor(out=Y[yi][:, :], in0=Y[yi][:, :], in1=b2t[:, :], op=ALU.add),
                     reads=[constB], writes=[YB[yi]])
                r0w = t * TT + s * 128
                out_toks.append(P.dma("sp", out_d[r0w:r0w + 128, :], Y[yi][:, :], "Y%d" % yi, reads=[YB[yi]]))
            stage1(0)
            filler(2)
            stage1b(0)
            stage1(1)
            filler(2)
            stage1b(1)
            stage2(0)
            stage1(2)
            filler(2)
            stage1b(2)
            stage2(1)
            stage1(3)
            filler(2)
            stage1b(3)
            stage2(2)
            filler(2)
            stage2(3)
            filler(100)

        def proj_items(t):
            items = []
            for sid in (0, 1, 2):
                for s in range(S):
                    items.append(lambda sid=sid, s=s: proj_group(t, sid, s))
                items.append(None)
            return items

        def make_filler(items):
            def filler(n):
                k = 0
                while items and k < n:
                    it = items.pop(0)
                    if it is None:
                        issue_slot_load()
                    else:
                        it()
                        k += 1
                while items and items[0] is None:
                    items.pop(0)
                    issue_slot_load()
            return filler

        phaseX(0)
        make_filler(proj_items(0))(100)
        for t in range(NT):
            if t + 1 < NT:
                csload(t + 1)
            for sid in (3, 4):
                for s in range(S):
                    proj_group(t, sid, s)
                issue_slot_load()
            phaseT(t)
            phaseO(t)
            if t + 1 < NT:
                xload_bf(t + 1)
            phaseR(t)
            dbg("mixT", lambda: xbm[:, :], [128, 4096], [xbmB], BF16)
            deferred = phaseW(t)
            phaseU(t, deferred)
            dbg("x1", lambda: X[:, :], [128, S * D], XB)
            dbg("aT", lambda: aT[:, :], [128, NF * TT], aTB, BF16)
            phaseD(t)
            phaseL2(t, make_filler(proj_items(t + 1) if t + 1 < NT else []))
            if t + 1 < NT:
                xload_f32(t + 1)
        P.wait_all("sp", out_toks)
        P.emit()
    return nc, list(dbg_d.keys())


def make_inputs(x_b, w):
    consts, _ = host_consts()
    m = {
        "x": np.ascontiguousarray(x_b, dtype=np.float32),
        "w_in": np.ascontiguousarray(w["w_in"][0]),
        "w_pool": np.ascontiguousarray(w["w_pool"][0]),
        "w_out": np.ascontiguousarray(w["w_out"][0]),
        "w_up": np.ascontiguousarray(w["w_up"][0]),
        "w_down": np.ascontiguousarray(w["w_down"][0]),
        "pscale": np.ascontiguousarray(w["pool_scale"][0].reshape(4, 128).T),
        "g1t": np.ascontiguousarray(np.broadcast_to(w["ln1_g"][0][None, :], (128, D))),
        "b1t": np.ascontiguousarray(np.broadcast_to(w["ln1_b"][0][None, :], (128, D))),
        "g2t": np.ascontiguousarray(np.broadcast_to(w["ln2_g"][0][None, :], (128, D))),
        "b2t": np.ascontiguousarray(np.broadcast_to(w["ln2_b"][0][None, :], (128, D))),
        "cw": np.ascontiguousarray(w["conv_w"][0].reshape(3, NF, 128).transpose(2, 1, 0).reshape(128, NF * 3)),
        "cb": np.ascontiguousarray(w["conv_b"][0].reshape(NF, 128).T),
        "rope": consts["rope"],
        "qdec": consts["qdec"],
        "vdec": consts["vdec"],
        "mask": consts["mask"],
        "bands": np.ascontiguousarray(consts["bands"].reshape(128, 12 * 128)),
        "ident": consts["ident"],
        "g1p": np.ascontiguousarray(w["ln1_g"][0].reshape(KD, 128).T),
        "b1p": np.ascontiguousarray(w["ln1_b"][0].reshape(KD, 128).T),
    }
    return m


def kernel(x, w_in, w_pool, pool_scale, w_out, ln1_g, ln1_b, w_up, conv_w, conv_b, w_down, ln2_g, ln2_b):
    w = dict(w_in=np.asarray(w_in, np.float32), w_pool=np.asarray(w_pool, np.float32),
             pool_scale=np.asarray(pool_scale, np.float32), w_out=np.asarray(w_out, np.float32),
             ln1_g=np.asarray(ln1_g, np.float32), ln1_b=np.asarray(ln1_b, np.float32),
             w_up=np.asarray(w_up, np.float32), conv_w=np.asarray(conv_w, np.float32),
             conv_b=np.asarray(conv_b, np.float32), w_down=np.asarray(w_down, np.float32),
             ln2_g=np.asarray(ln2_g, np.float32), ln2_b=np.asarray(ln2_b, np.float32))
    x = np.asarray(x, np.float32)
    nb, T, _ = x.shape
    nc, _ = build(T // TT)
    in_maps = [make_inputs(x[b], w) for b in range(nb)]
    res = run_bass_kernel_spmd(nc, in_maps, core_ids=list(range(nb)))
    return np.stack([np.asarray(r["out"], dtype=np.float32) for r in res.results], axis=0)
```
